# Optimizing a Trainium2 kernel written in Bass

```python
import math
import jax
import jax.numpy as jnp
from jax import lax
import numpy as np

D_MODEL = 4096
BATCH = 2
SEQ = 8192
DEPTH = 2

GRID_W = 64
CTX_LEN = 256
HEAD_DIM = 128
ROPE_THETA = 10000.0
Q_BLOCK = 128
EPS = 1e-6
A_HEADS = D_MODEL // (2 * HEAD_DIM)
A_KV_HEADS = A_HEADS // 4
B_HEADS = D_MODEL // (4 * HEAD_DIM)
B_VDIM = 2 * HEAD_DIM
C_HEADS = D_MODEL // HEAD_DIM
NA_ROWS = 8
NA_COLS = 16
MLP_HIDDEN = 4 * D_MODEL
N_EVEN = (DEPTH + 1) // 2
N_ODD = DEPTH // 2
A_Q = A_HEADS * HEAD_DIM
A_KV = A_KV_HEADS * HEAD_DIM
B_QK = B_HEADS * 2 * HEAD_DIM
B_V = B_HEADS * B_VDIM
EVEN_IN = A_Q + 2 * A_KV + 2 * B_QK + B_V
EVEN_MIX = A_Q + B_V
ODD_MIX = C_HEADS * HEAD_DIM
ODD_IN = 3 * ODD_MIX

kernel_name = 'hybrid_dit_gqa_diffattn_natten_block'


def rms_norm(x, g):
    xf = x.astype(jnp.float32)
    y = xf * lax.rsqrt(jnp.mean(xf * xf, axis=-1, keepdims=True) + EPS)
    return (y * g.astype(jnp.float32)).astype(x.dtype)


def modulate(x, g, shift, scale):
    return rms_norm(x, g) * (1 + scale) + shift


def axial_rope_tables(n_tokens, dtype):
    t = jnp.arange(n_tokens, dtype=jnp.int32)
    row = (t // GRID_W).astype(jnp.float32)
    col = (t % GRID_W).astype(jnp.float32)
    n_freq = HEAD_DIM // 4
    freqs = ROPE_THETA ** (-jnp.arange(n_freq, dtype=jnp.float32) / n_freq)
    ang = jnp.stack([row[:, None] * freqs, col[:, None] * freqs], axis=1)
    return jnp.cos(ang).astype(dtype), jnp.sin(ang).astype(dtype)


def apply_rope(x, cos, sin):
    b, l, h, d = x.shape
    xs = x.reshape(b, l, h, 2, 2, d // 4)
    x0, x1 = xs[..., 0, :], xs[..., 1, :]
    cs, sn = cos[None, :, None], sin[None, :, None]
    out = jnp.stack([x0 * cs - x1 * sn, x1 * cs + x0 * sn], axis=-2)
    return out.reshape(b, l, h, d)


def gqa_blocks(q, k, v):
    b, l, hq, d = q.shape
    hkv = k.shape[2]
    grp = hq // hkv
    nb = l // Q_BLOCK
    qb = q.reshape(b, nb, Q_BLOCK, hkv, grp, d).swapaxes(0, 1)
    scale = d ** -0.5

    def one(qblk):
        s = jnp.einsum('bqhgd,bnhd->bhgqn', qblk, k).astype(jnp.float32) * scale
        p = jax.nn.softmax(s, axis=-1).astype(v.dtype)
        return jnp.einsum('bhgqn,bnhd->bqhgd', p, v)

    o = lax.map(one, qb)
    return o.swapaxes(0, 1).reshape(b, l, hq * d)


def diff_blocks(q, k, v, lam, subln_g, lambda_init):
    b, l, h, _, d = q.shape
    nb = l // Q_BLOCK
    qb = q.reshape(b, nb, Q_BLOCK, h, 2, d).swapaxes(0, 1)
    scale = d ** -0.5

    def one(qblk):
        s = jnp.einsum('bqhmd,bnhmd->bhmqn', qblk, k).astype(jnp.float32) * scale
        p = jax.nn.softmax(s, axis=-1)
        a = (p[:, :, 0] - lam * p[:, :, 1]).astype(v.dtype)
        return jnp.einsum('bhqn,bnhe->bqhe', a, v)

    o = lax.map(one, qb).swapaxes(0, 1).reshape(b, l, h, v.shape[-1])
    o = rms_norm(o, subln_g) * (1.0 - lambda_init)
    return o.reshape(b, l, h * v.shape[-1])


def na_latent(q, k, v, kc, vc, rel_bias, rows):
    b, s, h, d = q.shape
    kr = min(NA_ROWS, rows)
    scale = d ** -0.5
    qg = q.reshape(b, rows, GRID_W, h, d)
    kg = k.reshape(b, rows, GRID_W, h, d)
    vg = v.reshape(b, rows, GRID_W, h, d)
    col = jnp.arange(GRID_W, dtype=jnp.int32)
    c0 = jnp.clip(col - NA_COLS // 2, 0, GRID_W - NA_COLS)
    col_idx = c0[:, None] + jnp.arange(NA_COLS, dtype=jnp.int32)[None]
    dc = col_idx - col[:, None] + (NA_COLS - 1)
    r_idx = jnp.arange(rows, dtype=jnp.int32)
    r0 = jnp.clip(r_idx - kr // 2, 0, rows - kr)

    def one(args):
        qrow, r, r0r = args
        kband = lax.dynamic_slice_in_dim(kg, r0r, kr, axis=1)
        vband = lax.dynamic_slice_in_dim(vg, r0r, kr, axis=1)
        kwin = kband[:, :, col_idx]
        vwin = vband[:, :, col_idx]
        dr = r0r + jnp.arange(kr, dtype=jnp.int32) - r + (NA_ROWS - 1)
        bias = rel_bias[:, dr[:, None, None], dc[None]]
        s_loc = jnp.einsum('bqhd,brqjhd->bhqrj', qrow, kwin).astype(jnp.float32) * scale
        s_loc = s_loc + bias.transpose(0, 2, 1, 3)[None].astype(jnp.float32)
        s_ctx = jnp.einsum('bqhd,bnhd->bhqn', qrow, kc).astype(jnp.float32) * scale
        n_loc = kr * NA_COLS
        sc = jnp.concatenate([s_loc.reshape(b, h, GRID_W, n_loc), s_ctx], axis=-1)
        p = jax.nn.softmax(sc, axis=-1).astype(v.dtype)
        p_loc = p[..., :n_loc].reshape(b, h, GRID_W, kr, NA_COLS)
        p_ctx = p[..., n_loc:]
        return (jnp.einsum('bhqrj,brqjhd->bqhd', p_loc, vwin)
                + jnp.einsum('bhqn,bnhd->bqhd', p_ctx, vc))

    o = lax.map(one, (qg.swapaxes(0, 1), r_idx, r0))
    return o.swapaxes(0, 1).reshape(b, s, h * d)


def even_mixer(u, uc, w_in, w_out, q_g, k_g, lq1, lk1, lq2, lk2, subln_g, cos, sin, lambda_init, need_ctx):
    idx = [A_Q, A_Q + A_KV, A_Q + 2 * A_KV, A_Q + 2 * A_KV + B_QK, A_Q + 2 * A_KV + 2 * B_QK]

    def project(t):
        bt, l, _ = t.shape
        qa, ka, va, qb, kb, vb = jnp.split(t @ w_in, idx, axis=-1)
        qa = rms_norm(qa.reshape(bt, l, A_HEADS, HEAD_DIM), q_g)
        ka = rms_norm(ka.reshape(bt, l, A_KV_HEADS, HEAD_DIM), k_g)
        va = va.reshape(bt, l, A_KV_HEADS, HEAD_DIM)
        qb = qb.reshape(bt, l, 2 * B_HEADS, HEAD_DIM)
        kb = kb.reshape(bt, l, 2 * B_HEADS, HEAD_DIM)
        vb = vb.reshape(bt, l, B_HEADS, B_VDIM)
        return qa, ka, va, qb, kb, vb

    def to_maps(t):
        return t.reshape(t.shape[0], t.shape[1], B_HEADS, 2, HEAD_DIM)

    qa, ka, va, qb, kb, vb = project(u)
    qa_c, ka_c, va_c, qb_c, kb_c, vb_c = project(uc)
    qa, ka, qb, kb = [apply_rope(t, cos, sin) for t in (qa, ka, qb, kb)]
    f32 = jnp.float32
    lam = (jnp.exp(jnp.sum(lq1.astype(f32) * lk1.astype(f32)))
           - jnp.exp(jnp.sum(lq2.astype(f32) * lk2.astype(f32))) + lambda_init)
    a_lat = gqa_blocks(qa, jnp.concatenate([ka, ka_c], axis=1), jnp.concatenate([va, va_c], axis=1))
    b_lat = diff_blocks(to_maps(qb), to_maps(jnp.concatenate([kb, kb_c], axis=1)),
                        jnp.concatenate([vb, vb_c], axis=1), lam, subln_g, lambda_init)
    y = jnp.concatenate([a_lat, b_lat], axis=-1) @ w_out
    yc = None
    if need_ctx:
        a_ctx = gqa_blocks(qa_c, ka_c, va_c)
        b_ctx = diff_blocks(to_maps(qb_c), to_maps(kb_c), vb_c, lam, subln_g, lambda_init)
        yc = jnp.concatenate([a_ctx, b_ctx], axis=-1) @ w_out
    return y, yc


def odd_mixer(u, uc, w_in, w_out, rel_bias, rows, need_ctx):
    def project(t):
        bt, l, _ = t.shape
        q, k, v = jnp.split(t @ w_in, 3, axis=-1)
        return (q.reshape(bt, l, C_HEADS, HEAD_DIM), k.reshape(bt, l, C_HEADS, HEAD_DIM),
                v.reshape(bt, l, C_HEADS, HEAD_DIM))

    q, k, v = project(u)
    qc, kc, vc = project(uc)
    y = na_latent(q, k, v, kc, vc, rel_bias, rows) @ w_out
    yc = None
    if need_ctx:
        yc = gqa_blocks(qc, kc, vc) @ w_out
    return y, yc


def sq_relu_mlp(h, w1, w2):
    return jnp.square(jax.nn.relu(h @ w1)) @ w2


def setup_inputs(seed: int = 0) -> dict:
    key = jax.random.key(seed)
    ks = jax.random.split(key, 24)
    f32 = jnp.float32
    nrm = lambda k, shape, s: jax.random.normal(k, shape, f32) * s
    return {
        'x': nrm(ks[0], (BATCH, SEQ, D_MODEL), 1.0),
        'c': nrm(ks[1], (BATCH, D_MODEL), 1.0),
        'ctx': nrm(ks[2], (BATCH, CTX_LEN, D_MODEL), 1.0),
        'c_ctx': nrm(ks[3], (D_MODEL,), 1.0),
        'ada_w': nrm(ks[4], (DEPTH, D_MODEL, 6 * D_MODEL), 0.5 * D_MODEL ** -0.5),
        'ada_b': nrm(ks[5], (DEPTH, 6 * D_MODEL), 0.02),
        'norm1_g': 1.0 + nrm(ks[6], (DEPTH, D_MODEL), 0.02),
        'norm2_g': 1.0 + nrm(ks[7], (DEPTH, D_MODEL), 0.02),
        'w_in_even': nrm(ks[8], (N_EVEN, D_MODEL, EVEN_IN), D_MODEL ** -0.5),
        'w_out_even': nrm(ks[9], (N_EVEN, EVEN_MIX, D_MODEL), EVEN_MIX ** -0.5),
        'a_q_norm': 1.0 + nrm(ks[10], (N_EVEN, HEAD_DIM), 0.02),
        'a_k_norm': 1.0 + nrm(ks[11], (N_EVEN, HEAD_DIM), 0.02),
        'b_lambda_q1': nrm(ks[12], (N_EVEN, HEAD_DIM), 0.1),
        'b_lambda_k1': nrm(ks[13], (N_EVEN, HEAD_DIM), 0.1),
        'b_lambda_q2': nrm(ks[14], (N_EVEN, HEAD_DIM), 0.1),
        'b_lambda_k2': nrm(ks[15], (N_EVEN, HEAD_DIM), 0.1),
        'b_subln_g': 1.0 + nrm(ks[16], (N_EVEN, B_VDIM), 0.02),
        'w_in_odd': nrm(ks[17], (N_ODD, D_MODEL, ODD_IN), D_MODEL ** -0.5),
        'w_out_odd': nrm(ks[18], (N_ODD, ODD_MIX, D_MODEL), ODD_MIX ** -0.5),
        'na_rel_bias': nrm(ks[19], (N_ODD, C_HEADS, 2 * NA_ROWS - 1, 2 * NA_COLS - 1), 0.2),
        'mlp_w1': nrm(ks[20], (DEPTH, D_MODEL, MLP_HIDDEN), D_MODEL ** -0.5),
        'mlp_w2': nrm(ks[21], (DEPTH, MLP_HIDDEN, D_MODEL), MLP_HIDDEN ** -0.5),
        'final_g': 1.0 + nrm(ks[22], (D_MODEL,), 0.02),
    }


def reference(x, c, ctx, c_ctx, ada_w, ada_b, norm1_g, norm2_g, w_in_even, w_out_even,
              a_q_norm, a_k_norm, b_lambda_q1, b_lambda_k1, b_lambda_q2, b_lambda_k2, b_subln_g,
              w_in_odd, w_out_odd, na_rel_bias, mlp_w1, mlp_w2, final_g):
    s = x.shape[1]
    rows = s // GRID_W
    cos, sin = axial_rope_tables(s, x.dtype)
    cond_lat = jax.nn.silu(c)
    cond_ctx = jax.nn.silu(c_ctx)
    h, hc = x, ctx
    for i in range(DEPTH):
        need_ctx = i < DEPTH - 1
        mod = cond_lat @ ada_w[i] + ada_b[i]
        mod_c = cond_ctx @ ada_w[i] + ada_b[i]
        sh1, sc1, g1, sh2, sc2, g2 = [m[:, None] for m in jnp.split(mod, 6, axis=-1)]
        csh1, csc1, cg1, csh2, csc2, cg2 = jnp.split(mod_c, 6, axis=-1)
        u = modulate(h, norm1_g[i], sh1, sc1)
        uc = modulate(hc, norm1_g[i], csh1, csc1)
        j = i // 2
        if i % 2 == 0:
            lambda_init = 0.8 - 0.6 * math.exp(-0.3 * i)
            y, yc = even_mixer(u, uc, w_in_even[j], w_out_even[j], a_q_norm[j], a_k_norm[j],
                               b_lambda_q1[j], b_lambda_k1[j], b_lambda_q2[j], b_lambda_k2[j],
                               b_subln_g[j], cos, sin, lambda_init, need_ctx)
        else:
            y, yc = odd_mixer(u, uc, w_in_odd[j], w_out_odd[j], na_rel_bias[j], rows, need_ctx)
        h = h + g1 * y
        h = h + g2 * sq_relu_mlp(modulate(h, norm2_g[i], sh2, sc2), mlp_w1[i], mlp_w2[i])
        if need_ctx:
            hc = hc + cg1 * yc
            hc = hc + cg2 * sq_relu_mlp(modulate(hc, norm2_g[i], csh2, csc2), mlp_w1[i], mlp_w2[i])
    return rms_norm(h, final_g)
```

```python
import math
from contextlib import ExitStack
import numpy as np
import concourse.bass as bass
import concourse.mybir as mybir
from concourse.bass_utils import run_bass_kernel_spmd

F32 = mybir.dt.float32
BF16 = mybir.dt.bfloat16
AF = mybir.ActivationFunctionType
ALU = mybir.AluOpType
EPS = 1e-6
GRID_W = 64
CTX = 256
HD = 128
NEG = -30000.0
STOP_AFTER = 99


class Cfg:
    def __init__(s, D=4096, S=8192):
        s.D, s.S = D, S
        s.KC = D // 128
        s.T = S // 4
        s.TT = s.T + CTX
        s.R = s.T // GRID_W
        s.G = s.R // 8
        s.ROWS = S // GRID_W
        s.A_H = D // 256
        s.A_KV = s.A_H // 4
        s.B_H = D // 512
        s.H = 4 * D
        s.NM_K = s.A_KV + 2 * s.B_H
        s.NM_Q = s.A_H + 2 * s.B_H
        s.VW = s.A_KV * 128 + s.B_H * 256
        s.NKC = (4 * s.T + CTX) // 128
        s.EXT = (8 + s.R) * GRID_W + CTX
        s.MJ = 6 * s.KC // 4
        if s.T == 2048:
            s.SB = [[(0, 384, 0), (384, 384, 0), (768, 384, 0)], [(1152, 512, 0), (1664, 384, 0), (2048, 256, 1)]]
        elif s.T == 1024:
            s.SB = [[(0, 512, 0), (512, 512, 0)], [(1024, 256, 1)]]
        else:
            raise ValueError
        s.RW = max(sum(n for _, n, _ in sb) for sb in s.SB)


class DS:
    def __init__(s, sem):
        s.sem, s.n, s.waited = sem, 0, 0


class Eng:
    def __init__(s, P, name, eng):
        s.P, s.name, s.eng = P, name, eng
        s.sem = P.new_sem("e_" + name)
        s.n = 0
        s.seen = {}

    def wait(s, *toks):
        for t in toks:
            if t is None:
                continue
            if isinstance(t, list):
                s.wait(*t)
                continue
            src, v = t
            if v <= 0 or s.seen.get(id(src), 0) >= v:
                continue
            s.eng.wait_ge(src.sem, v)
            s.seen[id(src)] = v
            if isinstance(src, DS):
                src.waited = max(src.waited, v)

    def sig(s, ins):
        s.n += 1
        ins.then_inc(s.sem, 1)
        return (s, s.n)


class Prog:
    def __init__(s, cfg):
        s.cfg = cfg
        s.nc = bass.Bass("TRN2", target_bir_lowering=False)
        s.es = ExitStack()
        s.nsem = 0
        s.dsems = []
        s.pool_sync = True

    def new_sem(s, name):
        s.nsem += 1
        return s.es.enter_context(s.nc.semaphore(name + "_%d" % s.nsem))

    def ds(s, name="d", st=None):
        if not hasattr(s, "dfree"):
            s.dfree = []
        if s.dfree:
            d = s.dfree.pop()
        else:
            d = DS(s.new_sem(name))
            s.dsems.append(d)
        if st is not None:
            st.callback(lambda d=d: s.dfree.append(d))
        return d

    def start(s):
        nc = s.nc
        s.block = s.es.enter_context(nc.Block())
        s.PE = Eng(s, "pe", nc.tensor)
        s.ACT = Eng(s, "act", nc.scalar)
        s.DVE = Eng(s, "dve", nc.vector)
        s.POOL = Eng(s, "pool", nc.gpsimd)
        s.SP = Eng(s, "sp", nc.sync)
        s.engs = [s.PE, s.ACT, s.DVE, s.POOL, s.SP]
        s.banks = [s.es.enter_context(nc.psum_tensor("bank%d" % i, [128, 512], F32)) for i in range(8)]
        s.bank_free = [None] * 8

    def dma(s, q, out, in_, ds, waits=()):
        q.wait(*waits)
        if ds.waited > 0:
            q.wait((ds, ds.waited))
        ins = q.eng.dma_start(out=out, in_=in_)
        ds.n += 16
        ins.then_inc(ds.sem, 16)
        return (ds, ds.n)

    def barrier(s, full=False):
        toks = [(e, e.n) for e in s.engs] + [(d, d.n) for d in s.dsems]
        for e in s.engs:
            if e is s.POOL and not (full or s.pool_sync):
                continue
            e.wait(*toks)
        s.bank_free = [None] * 8

    def sb(s, st, name, shape, dt):
        s.nsb = getattr(s, "nsb", 0) + 1
        return st.enter_context(s.nc.sbuf_tensor("%s_u%d" % (name, s.nsb), shape, dt))


def build(cfg, debug=False):
    P = Prog(cfg)
    nc = P.nc
    c = cfg
    D, KC, T, TT, H = c.D, c.KC, c.T, c.TT, c.H

    def din(name, shape, dt=F32):
        return nc.dram_tensor(name, list(shape), dt, kind="ExternalInput")

    def dscr(name, shape, dt, out=False):
        return nc.dram_tensor(name, list(shape), dt, kind=("ExternalOutput" if (out and debug) else "Internal"))

    xT = din("xT", [KC, 128, TT])
    cvT = din("cvT", [128, KC, 2])
    adaw = din("adaw", [2, D, c.MJ * 128])
    adab = din("adab", [128, 2, c.MJ])
    gn1 = din("gn1", [128, 2, KC])
    gn2 = din("gn2", [128, 2, KC])
    gfin = din("gfin", [128, KC])
    w_in0 = din("w_in0", [D, 9 * D // 4])
    w_out0 = din("w_out0", [D, D])
    w_in1 = din("w_in1", [D, 3 * D])
    w_out1 = din("w_out1", [D, D])
    w1 = din("w1", [2, D, H])
    w2 = din("w2", [2, H, D])
    smalls = din("smalls", [128, 8])
    cosT = din("cosT", [128, TT])
    sinT = din("sinT", [128, TT])
    permI = din("perm", [128, 128])
    tabs = din("tabs", [3, KC, 8, 128, 512])
    sel = din("sel", [128, 8])
    outT = nc.dram_tensor("outT", [KC, 128, T], F32, kind="ExternalOutput")

    hT = dscr("hT", [KC, 128, TT], F32, out=True)
    mod_src = dscr("mod_src", [128, 2 * c.MJ * 2], F32)
    mod_g = dscr("mod_g", [4 * 128, 2 * c.MJ * 2], F32)
    Q0 = dscr("Q0", [c.NM_Q, 128, TT], BF16)
    NKG = c.A_KV + c.B_H
    KGR = [128] * c.A_KV + [256] * c.B_H
    K0m = [dscr("K0m%d" % i, [KGR[i], T], BF16) for i in range(NKG)]
    K0mg = [dscr("K0mg%d" % i, [4 * KGR[i], T], BF16) for i in range(NKG)]
    K0c = dscr("K0c", [c.NM_K, 128, CTX], BF16)
    NVH = c.A_KV + c.B_H
    DV = [128] * c.A_KV + [256] * c.B_H
    V0h = [dscr("V0h%d" % i, [T, DV[i]], BF16) for i in range(NVH)]
    V0hg = [dscr("V0hg%d" % i, [4 * T, DV[i]], BF16) for i in range(NVH)]
    V0c = dscr("V0c", [CTX, c.VW], BF16)
    attT = dscr("attT", [KC, 128, TT], BF16)
    actT = dscr("actT", [4 * KC, 128, TT], BF16)
    Q1 = dscr("Q1", [KC, 128, T], BF16)
    K1 = dscr("K1", [KC, 128, c.EXT], BF16)
    V1 = dscr("V1", [c.EXT, D], BF16)
    HP = min(8, KC)
    NKP = KC // HP
    Kb = [dscr("Kb%d" % i, [HP * 128, 512], BF16) for i in range(NKP)]
    Kbg = [dscr("Kbg%d" % i, [4 * HP * 128, 512], BF16) for i in range(NKP)]
    VPW = min(D, 4096)
    NVP = D // VPW
    Vb = [[dscr("Vb%d_%d" % (i, j), [128, VPW], BF16) for j in range(NVP)] for i in range(4)]
    Vbg = [[dscr("Vbg%d_%d" % (i, j), [4 * 128, VPW], BF16) for j in range(NVP)] for i in range(4)]

    P.start()
    PE, ACT, DVE, POOL, SP = P.PE, P.ACT, P.DVE, P.POOL, P.SP
    banks = P.banks
    G4 = [[0, 1, 2, 3], [4, 5, 6, 7]]

    gs = P.es
    ones = P.sb(gs, "ones", [128, 128], BF16)
    perm = P.sb(gs, "permb", [128, 128], BF16)
    epsb = P.sb(gs, "epsb", [128, 1], F32)
    smb = P.sb(gs, "smb", [128, 8], F32)
    selb = P.sb(gs, "selb", [128, 8], F32)
    modf = P.sb(gs, "modf", [128, 2, 6 * KC, 2], F32)
    Am = P.sb(gs, "Am", [128, 2, 2, 2, KC], F32)
    g1b = P.sb(gs, "g1b", [128, 2, KC], F32)
    g2b = P.sb(gs, "g2b", [128, 2, KC], F32)
    gfb = P.sb(gs, "gfb", [128, KC], F32)
    Gq = P.sb(gs, "Gq", [128, 4], F32)
    lamb = P.sb(gs, "lamb", [128, 2], F32)
    cosb = P.sb(gs, "cosb", [128, TT], BF16)
    sinb = P.sb(gs, "sinb", [128, TT], BF16)
    NW = 4
    wring = [None] * NW
    wds = [P.ds("w") for _ in range(NW)]
    wfree = [None] * NW
    wcnt = [0]

    def alloc_w(st):
        P.barrier(full=True)
        wring[:] = [P.sb(st, "wr%d_%d" % (i, P.nsem), [128, KC, 256], BF16) for i in range(NW)]
        for i in range(NW):
            wfree[i] = None
        P.pool_sync = False

        def _restore():
            P.pool_sync = True
            P.barrier(full=True)
        st.callback(_restore)
    d_misc = P.ds("misc")
    d_pm = P.ds("pmisc")
    d_cc = P.ds("cc")

    def modv(i, part, v):
        return modf[:, i, part * KC:(part + 1) * KC, v]

    with ExitStack() as st:
        cvf = P.sb(st, "cvf", [128, KC, 2], F32)
        cvb = P.sb(st, "cvb", [128, KC, 2], BF16)
        adabb = P.sb(st, "adabb", [128, 2, c.MJ], F32)
        modl = P.sb(st, "modl", [128, 2, c.MJ, 2], F32)
        lt = P.sb(st, "lt", [128, 4], F32)
        alloc_w(st)
        t_in = [P.dma(SP, cvf[:, :, :], cvT[:, :, :], d_misc),
                P.dma(SP, adabb[:, :, :], adab[:, :, :], d_misc),
                P.dma(SP, smb[:, :], smalls[:, :], d_misc),
                P.dma(SP, selb[:, :], sel[:, :], d_misc),
                P.dma(SP, g1b[:, :, :], gn1[:, :, :], d_misc),
                P.dma(SP, g2b[:, :, :], gn2[:, :, :], d_misc),
                P.dma(SP, gfb[:, :], gfin[:, :], d_misc),
                P.dma(POOL, perm[:, :], permI[:, :], d_pm),
                P.dma(POOL, cosb[:, :], cosT[:, :], d_pm),
                P.dma(POOL, sinb[:, :], sinT[:, :], d_pm)]
        t_ones = DVE.sig(nc.vector.memset(ones[:, :], 1.0))
        t_eps = DVE.sig(nc.vector.memset(epsb[:, :], EPS))
        ACT.wait(t_in[6], t_in[-1])
        t_cv = ACT.sig(nc.scalar.activation(out=cvb[:, :, :], in_=cvf[:, :, :], func=AF.Silu))
        psm = banks[0]
        for i in range(2):
            for ct in range(c.MJ // 2):
                slot = wcnt[0] % NW
                wcnt[0] += 1
                src = adaw[i].rearrange("(kc p) n -> p kc n", p=128)[:, :, ct * 256:(ct + 1) * 256]
                tw = P.dma(POOL, wring[slot][:, :, :], src, wds[slot], waits=[wfree[slot]])
                PE.wait(tw, t_cv)
                for mc in range(2):
                    j = ct * 2 + mc
                    col = (i * c.MJ + j) * 2
                    for kc in range(KC):
                        mm = nc.tensor.matmul(psm[:, col:col + 2], wring[slot][:, kc, mc * 128:(mc + 1) * 128],
                                              cvb[:, kc, :], start=(kc == 0), stop=(kc == KC - 1))
                wfree[slot] = PE.sig(mm)
        DVE.wait((PE, PE.n), t_in[6], t_in[-1])
        for i in range(2):
            for v in range(2):
                tm = DVE.sig(nc.vector.tensor_tensor(
                    out=modl[:, i, :, v], in0=psm[:, i * c.MJ * 2:(i + 1) * c.MJ * 2].rearrange("p (j v) -> p j v", v=2)[:, :, v],
                    in1=adabb[:, i, :], op=ALU.add))
        t1 = P.dma(SP, mod_src[:, :], modl[:, :, :, :].rearrange("p i j v -> p (i j v)"), d_misc, waits=[tm])
        POOL.wait(t1)
        cc = nc.gpsimd.collective_compute("AllGather", ALU.bypass, replica_groups=G4,
                                          ins=[mod_src.ap().opt()], outs=[mod_g.ap().opt()])
        d_cc.n += 1
        cc.then_inc(d_cc.sem, 1)
        tcc = (d_cc, d_cc.n)
        tl = None
        for r in range(4):
            for i in range(2):
                tl = P.dma(SP, modf[:, i, r * c.MJ:(r + 1) * c.MJ, :],
                           mod_g[r * 128:(r + 1) * 128, :].rearrange("p (i j v) -> p i j v", i=2, v=2)[:, i, :, :],
                           d_misc, waits=[tcc])
        DVE.wait(tl)
        for i in range(2):
            for sub, (part, gb) in enumerate(((1, g1b), (4, g2b))):
                for v in range(2):
                    ta = DVE.sig(nc.vector.scalar_tensor_tensor(out=Am[:, i, sub, v, :], in0=modv(i, part, v), scalar=1.0,
                                                                in1=gb[:, i, :], op0=ALU.add, op1=ALU.mult))
        DVE.sig(nc.vector.tensor_copy(out=Gq[:, 0:2], in_=smb[:, 0:2]))
        DVE.sig(nc.vector.tensor_scalar(out=Gq[:, 2:4], in0=smb[:, 6:8], scalar1=0.8, scalar2=None, op0=ALU.mult))
        onesf = P.sb(st, "onesf", [128, 128], F32)
        tof = DVE.sig(nc.vector.memset(onesf[:, :], 1.0))
        DVE.sig(nc.vector.tensor_tensor(out=lt[:, 0:1], in0=smb[:, 2:3], in1=smb[:, 3:4], op=ALU.mult))
        tl2 = DVE.sig(nc.vector.tensor_tensor(out=lt[:, 1:2], in0=smb[:, 4:5], in1=smb[:, 5:6], op=ALU.mult))
        PE.wait(tl2)
        tp = PE.sig(nc.tensor.matmul(banks[1][:, 0:2], onesf[:, :], lt[:, 0:2], start=True, stop=True))
        ACT.wait(tp)
        te = ACT.sig(nc.scalar.activation(out=lt[:, 2:4], in_=banks[1][:, 0:2], func=AF.Exp))
        DVE.wait(te)
        DVE.sig(nc.vector.tensor_tensor(out=lamb[:, 0:1], in0=lt[:, 3:4], in1=lt[:, 2:3], op=ALU.subtract))
        DVE.wait((DVE, DVE.n))
        DVE.sig(nc.vector.tensor_scalar(out=lamb[:, 0:1], in0=lamb[:, 0:1], scalar1=-0.2, scalar2=None, op0=ALU.add))
        for kc in range(KC):
            P.dma(SP, hT[kc], xT[kc], d_misc)
        P.barrier()

    def w_tile(Wap, c0):
        slot = wcnt[0] % NW
        wcnt[0] += 1
        src = Wap.rearrange("(kc p) n -> p kc n", p=128)[:, :, c0:c0 + 256]
        tok = P.dma(POOL, wring[slot][:, :, :], src, wds[slot], waits=[wfree[slot]])
        return slot, tok

    bank_rr = [0]

    def next_bank(lo, n):
        b = lo + (bank_rr[0] % n)
        bank_rr[0] += 1
        return b

    def run_pending(pend, final=False):
        keep = []
        for g in pend:
            try:
                next(g)
                keep.append(g)
            except StopIteration:
                pass
        pend[:] = keep
        if final:
            while pend:
                run_pending(pend)

    def norm_phase(layer, sub, sbi, res, final=False, groups=None):
        groups = c.SB[sbi] if groups is None else groups
        if not groups:
            return
        t0 = c.SB[sbi][0][0]
        with ExitStack() as st:
            stg = [P.sb(st, "nst%d" % i, [128, 512], F32) for i in range(3)]
            sgd = [P.ds("nst", st) for _ in range(3)]
            sgf = [None] * 3
            sq = [P.sb(st, "nsq%d" % i, [128, 512], BF16) for i in range(2)]
            sqf = [None] * 2
            rstd = P.sb(st, "rstd", [128, c.RW], F32)
            tmp = [P.sb(st, "ntmp%d" % i, [128, 512], F32) for i in range(2)]
            tmpf = [None] * 2
            od = [P.ds("no", st) for _ in range(2)]
            u = 0
            for kc in range(KC):
                for gi, (s0, n, kind) in enumerate(groups):
                    sl = u % 3
                    td = P.dma(SP, stg[sl][:, :n], hT[kc, :, s0:s0 + n], sgd[sl], waits=[sgf[sl]])
                    ACT.wait(td, sqf[u % 2])
                    ta = ACT.sig(nc.scalar.activation(out=sq[u % 2][:, :n], in_=stg[sl][:, :n], func=AF.Square))
                    sgf[sl] = ta
                    PE.wait(ta)
                    mm = nc.tensor.matmul(banks[gi][:, :n], ones[:, :], sq[u % 2][:, :n], start=(kc == 0), stop=(kc == KC - 1))
                    sqf[u % 2] = PE.sig(mm)
                    u += 1
            for gi, (s0, n, kind) in enumerate(groups):
                ACT.wait((PE, PE.n))
                ta = ACT.sig(nc.scalar.activation(out=rstd[:, s0 - t0:s0 - t0 + n], in_=banks[gi][:, :n], func=AF.Sqrt,
                                                  bias=epsb[:, 0:1], scale=1.0 / D))
                DVE.wait(ta)
                tr = DVE.sig(nc.vector.reciprocal(out=rstd[:, s0 - t0:s0 - t0 + n], in_=rstd[:, s0 - t0:s0 - t0 + n]))
            DVE.wait(tr)
            for kc in range(KC):
                for gi, (s0, n, kind) in enumerate(groups):
                    sl = u % 3
                    td = P.dma(SP, stg[sl][:, :n], hT[kc, :, s0:s0 + n], sgd[sl], waits=[sgf[sl]])
                    DVE.wait(td, tmpf[u % 2])
                    tv = DVE.sig(nc.vector.tensor_tensor(out=tmp[u % 2][:, :n], in0=stg[sl][:, :n],
                                                         in1=rstd[:, s0 - t0:s0 - t0 + n], op=ALU.mult))
                    sgf[sl] = tv
                    ACT.wait(tv)
                    if final:
                        ta = ACT.sig(nc.scalar.activation(out=tmp[u % 2][:, :n], in_=tmp[u % 2][:, :n], func=AF.Identity,
                                                          scale=gfb[:, kc:kc + 1]))
                        P.dma(SP, outT[kc, :, s0:s0 + n], tmp[u % 2][:, :n], od[u % 2], waits=[ta])
                        tmpf[u % 2] = (od[u % 2], od[u % 2].n)
                    else:
                        ta = ACT.sig(nc.scalar.activation(out=res[:, kc, s0 - t0:s0 - t0 + n], in_=tmp[u % 2][:, :n],
                                                          func=AF.Identity, scale=Am[:, layer, sub, kind, kc:kc + 1],
                                                          bias=modf[:, layer, (0 if sub == 0 else 3) * KC + kc, kind:kind + 1]))
                        tmpf[u % 2] = ta
                    u += 1
            P.barrier()

    def load_res(res, src, sbi, kq=0, groups=None):
        groups = c.SB[sbi] if groups is None else groups
        t0, t1 = groups[0][0], groups[-1][0] + groups[-1][1]
        tk = None
        for kc in range(KC):
            tk = P.dma(SP, res[:, kc, 0:t1 - t0], src[kq * KC + kc, :, t0:t1], d_misc)
        PE.wait((d_misc, d_misc.n))

    def gemm(res, Wap, ncols, sbi, chunk_fn, groups=None, nb=4):
        groups = c.SB[sbi] if groups is None else groups
        t0 = c.SB[sbi][0][0]
        pend = []
        nt = ncols // 256
        tiles = []
        for ct in range(nt):
            while len(tiles) < min(nt, ct + NW):
                tiles.append(w_tile(Wap, len(tiles) * 256))
            slot, tw = tiles[ct]
            PE.wait(tw)
            kinds = [chunk_fn(ct * 2 + mc) for mc in range(2)]
            last = None
            if kinds[0] is not None and kinds[0][0] == 'v' and kinds[1] is not None and kinds[1][0] == 'v':
                vlist = [(0, 256, kinds[0])]
            else:
                vlist = [(mc * 128, 128, kinds[mc]) for mc in range(2) if kinds[mc] is not None and kinds[mc][0] == 'v']
            for (wc0, wn, kd) in vlist:
                tb0, tb1 = groups[0][0], groups[-1][0] + groups[-1][1]
                for tt in range(tb0, tb1, 128):
                    b = next_bank(0, nb)
                    PE.wait(P.bank_free[b])
                    for kc in range(KC):
                        mm = nc.tensor.matmul(banks[b][:, :wn], res[:, kc, tt - t0:tt - t0 + 128],
                                              wring[slot][:, kc, wc0:wc0 + wn], start=(kc == 0), stop=(kc == KC - 1))
                    last = PE.sig(mm)
                    pend.append(kd[2](kd[1], wn, tt, b, last))
                    run_pending(pend)
            for mc in range(2):
                kd = kinds[mc]
                if kd is None or kd[0] != 'r':
                    continue
                for gi, grp in enumerate(groups):
                    s0, n, kind = grp
                    if len(kd) > 2 and kd[2] and kind == 1:
                        continue
                    b = next_bank(0, nb)
                    PE.wait(P.bank_free[b])
                    for kc in range(KC):
                        mm = nc.tensor.matmul(banks[b][:, :n], wring[slot][:, kc, mc * 128:(mc + 1) * 128],
                                              res[:, kc, s0 - t0:s0 - t0 + n], start=(kc == 0), stop=(kc == KC - 1))
                    last = PE.sig(mm)
                    pend.append(kd[1](ct * 2 + mc, gi, grp, b, last))
                    run_pending(pend)
            wfree[slot] = last if last is not None else (PE, PE.n)
        run_pending(pend, final=True)
        P.barrier()

    class Ring:
        def __init__(s, st, name, shape, dt, n, dsem=False):
            s.t = [P.sb(st, "%s%d" % (name, i), shape, dt) for i in range(n)]
            s.free = [None] * n
            s.ds = [P.ds(name, st) for _ in range(n)] if dsem else None
            s.i = 0
            s.n = n

        def nxt(s):
            k = s.i % s.n
            s.i += 1
            return k

    def make_qk_epi(st):
        sq = Ring(st, "esq", [128, 512], BF16, 2)
        rs = Ring(st, "ers", [128, 512], F32, 2)
        qn = Ring(st, "eqn", [128, 512], BF16, 3)
        t1 = Ring(st, "et1", [128, 512], F32, 2)
        t2 = Ring(st, "et2", [128, 512], F32, 2)
        ob = Ring(st, "eob", [128, 512], BF16, 2, dsem=True)

        def epi(m, gi, grp, b, tok, normed, gcol, dstap):
            s0, n, kind = grp
            kq = qn.nxt()
            if normed:
                k1 = sq.nxt()
                ACT.wait(tok, sq.free[k1])
                ta = ACT.sig(nc.scalar.activation(out=sq.t[k1][:, :n], in_=banks[b][:, :n], func=AF.Square))
                yield
                bs = next_bank(4, 2)
                PE.wait(ta, P.bank_free[bs])
                tp = PE.sig(nc.tensor.matmul(banks[bs][:, :n], ones[:, :], sq.t[k1][:, :n], start=True, stop=True))
                sq.free[k1] = tp
                k2 = rs.nxt()
                ACT.wait(tp, rs.free[k2])
                ta2 = ACT.sig(nc.scalar.activation(out=rs.t[k2][:, :n], in_=banks[bs][:, :n], func=AF.Sqrt,
                                                   bias=epsb[:, 0:1], scale=1.0 / 128))
                P.bank_free[bs] = ta2
                DVE.wait(ta2)
                tr = DVE.sig(nc.vector.reciprocal(out=rs.t[k2][:, :n], in_=rs.t[k2][:, :n]))
                DVE.wait(tr, qn.free[kq])
                tq = DVE.sig(nc.vector.scalar_tensor_tensor(out=qn.t[kq][:, :n], in0=banks[b][:, :n], scalar=Gq[:, gcol:gcol + 1],
                                                            in1=rs.t[k2][:, :n], op0=ALU.mult, op1=ALU.mult))
                rs.free[k2] = tq
            else:
                ACT.wait(tok, qn.free[kq])
                tq = ACT.sig(nc.scalar.activation(out=qn.t[kq][:, :n], in_=banks[b][:, :n], func=AF.Copy))
            P.bank_free[b] = tq
            yield
            bw = next_bank(6, 2)
            PE.wait(tq, P.bank_free[bw])
            tp2 = PE.sig(nc.tensor.matmul(banks[bw][:, :n], perm[:, :], qn.t[kq][:, :n], start=True, stop=True))
            ka, kb, ko = t1.nxt(), t2.nxt(), ob.nxt()
            DVE.wait(tq, t1.free[ka])
            ta_ = DVE.sig(nc.vector.tensor_tensor(out=t1.t[ka][:, :n], in0=qn.t[kq][:, :n], in1=cosb[:, s0:s0 + n], op=ALU.mult))
            DVE.wait(tp2, t2.free[kb])
            tb_ = DVE.sig(nc.vector.tensor_tensor(out=t2.t[kb][:, :n], in0=banks[bw][:, :n], in1=sinb[:, s0:s0 + n], op=ALU.mult))
            P.bank_free[bw] = tb_
            qn.free[kq] = [tp2, ta_]
            DVE.wait(tb_, (ob.ds[ko], ob.ds[ko].n))
            to = DVE.sig(nc.vector.tensor_tensor(out=ob.t[ko][:, :n], in0=t1.t[ka][:, :n], in1=t2.t[kb][:, :n], op=ALU.add))
            t1.free[ka] = to
            t2.free[kb] = to
            P.dma(SP, dstap, ob.t[ko][:, :n], ob.ds[ko], waits=[to])
        return epi

    def make_copy_epi(st, dst_fn):
        ob = Ring(st, "cob", [128, 512], BF16, 3, dsem=True)

        def epi(m, gi, grp, b, tok):
            s0, n, kind = grp
            ko = ob.nxt()
            eng = ACT if (ob.i % 2 == 0) else DVE
            eng.wait(tok, (ob.ds[ko], ob.ds[ko].n))
            if eng is ACT:
                tq = ACT.sig(nc.scalar.activation(out=ob.t[ko][:, :n], in_=banks[b][:, :n], func=AF.Copy))
            else:
                tq = DVE.sig(nc.vector.tensor_copy(out=ob.t[ko][:, :n], in_=banks[b][:, :n]))
            P.bank_free[b] = tq
            P.dma(SP, dst_fn(m, s0, n, kind), ob.t[ko][:, :n], ob.ds[ko], waits=[tq])
            return
            yield
        return epi

    def make_v_epi(st, dst_fn):
        ob = Ring(st, "vob", [128, 256], BF16, 3, dsem=True)

        def epi(vcol0, wn, tt, b, tok):
            ko = ob.nxt()
            eng = ACT if (ob.i % 2 == 0) else DVE
            eng.wait(tok, (ob.ds[ko], ob.ds[ko].n))
            if eng is ACT:
                tq = ACT.sig(nc.scalar.activation(out=ob.t[ko][:, :wn], in_=banks[b][:, :wn], func=AF.Copy))
            else:
                tq = DVE.sig(nc.vector.tensor_copy(out=ob.t[ko][:, :wn], in_=banks[b][:, :wn]))
            P.bank_free[b] = tq
            for (dst, a_, b_) in dst_fn(tt, vcol0, wn):
                P.dma(SP, dst, ob.t[ko][:, a_:b_], ob.ds[ko], waits=[tq])
            return
            yield
        return epi

    def make_resid_epi(st, layer, gate_part, kq_off=0):
        hin = Ring(st, "hin", [128, 512], F32, 3, dsem=True)
        hout = Ring(st, "hout", [128, 512], F32, 3, dsem=True)

        def epi(m, gi, grp, b, tok):
            s0, n, kind = grp
            ki, ko = hin.nxt(), hout.nxt()
            td = P.dma(SP, hin.t[ki][:, :n], hT[m, :, s0:s0 + n], hin.ds[ki], waits=[hin.free[ki]])
            DVE.wait(tok, td, (hout.ds[ko], hout.ds[ko].n))
            to = DVE.sig(nc.vector.scalar_tensor_tensor(out=hout.t[ko][:, :n], in0=banks[b][:, :n],
                                                        scalar=modf[:, layer, gate_part * KC + m, kind:kind + 1],
                                                        in1=hin.t[ki][:, :n], op0=ALU.mult, op1=ALU.add))
            hin.free[ki] = to
            P.bank_free[b] = to
            P.dma(SP, hT[m, :, s0:s0 + n], hout.t[ko][:, :n], hout.ds[ko], waits=[to])
            return
            yield
        return epi

    def make_relu2_epi(st):
        rb = Ring(st, "rb", [128, 512], F32, 3)
        ob = Ring(st, "rob", [128, 512], BF16, 3, dsem=True)

        def epi(m, gi, grp, b, tok):
            s0, n, kind = grp
            kr, ko = rb.nxt(), ob.nxt()
            ACT.wait(tok, rb.free[kr])
            ta = ACT.sig(nc.scalar.activation(out=rb.t[kr][:, :n], in_=banks[b][:, :n], func=AF.Relu))
            P.bank_free[b] = ta
            DVE.wait(ta, (ob.ds[ko], ob.ds[ko].n))
            to = DVE.sig(nc.vector.tensor_tensor(out=ob.t[ko][:, :n], in0=rb.t[kr][:, :n], in1=rb.t[kr][:, :n], op=ALU.mult))
            rb.free[kr] = to
            P.dma(SP, actT[m, :, s0:s0 + n], ob.t[ko][:, :n], ob.ds[ko], waits=[to])
            return
            yield
        return epi

    SCALE = 1.0 / math.sqrt(HD)

    class AttnCtx:
        def __init__(s, st, with_tab=False):
            s.pb = Ring(st, "apb", [128, 512], BF16, 3)
            s.sfree = [None, None]
            s.si = 0
            s.ui = 0
            s.tabtok = None
            if with_tab:
                s.sbuf = Ring(st, "asb", [128, 512], F32, 2)

    def attn_unit(ac, qT, nq, chunks, nv):
        base = 2 + 3 * (ac.ui % 2)
        ac.ui += 1
        ob = [base + i for i in range(nv)]
        sb_ = base + 2
        nch = len(chunks)

        def qk(i):
            b = ac.si % 2
            ac.si += 1
            PE.wait(ac.sfree[b])
            mm = nc.tensor.matmul(banks[b][:, :nq], chunks[i][0], qT, start=True, stop=True)
            return b, PE.sig(mm)
        for bb in ob + [sb_]:
            PE.wait(P.bank_free[bb])
        cur = qk(0)
        last = None
        for i in range(nch):
            nx = qk(i + 1) if i + 1 < nch else None
            b, tqk = cur
            kp = ac.pb.nxt()
            tabsrc = chunks[i][2]
            if tabsrc is not None:
                ks = ac.sbuf.nxt()
                DVE.wait(tqk, ac.tabtok, ac.sbuf.free[ks])
                tv = DVE.sig(nc.vector.scalar_tensor_tensor(out=ac.sbuf.t[ks][:, :nq], in0=banks[b][:, :nq], scalar=SCALE,
                                                            in1=tabsrc, op0=ALU.mult, op1=ALU.add))
                ac.sfree[b] = tv
                ACT.wait(tv, ac.pb.free[kp])
                te = ACT.sig(nc.scalar.activation(out=ac.pb.t[kp][:, :nq], in_=ac.sbuf.t[ks][:, :nq], func=AF.Exp))
                ac.sbuf.free[ks] = te
            else:
                ACT.wait(tqk, ac.pb.free[kp])
                te = ACT.sig(nc.scalar.activation(out=ac.pb.t[kp][:, :nq], in_=banks[b][:, :nq], func=AF.Exp, scale=SCALE))
                ac.sfree[b] = te
            PE.wait(te)
            for vi in range(nv):
                nc.tensor.matmul(banks[ob[vi]][:, :nq], chunks[i][1][vi], ac.pb.t[kp][:, :nq], start=(i == 0), stop=(i == nch - 1))
            mm = nc.tensor.matmul(banks[sb_][:, :nq], ones[:, :], ac.pb.t[kp][:, :nq], start=(i == 0), stop=(i == nch - 1))
            last = PE.sig(mm)
            ac.pb.free[kp] = last
            cur = nx
        return ob, sb_, last

    def make_attn_epi_A(st):
        rec = Ring(st, "arec", [128, 512], F32, 2)
        ob = Ring(st, "aob", [128, 512], BF16, 3, dsem=True)

        def epi(obanks, sbank, tok, nq, dst):
            kr, ko = rec.nxt(), ob.nxt()
            DVE.wait(tok, rec.free[kr])
            tr = DVE.sig(nc.vector.reciprocal(out=rec.t[kr][:, :nq], in_=banks[sbank][:, :nq]))
            P.bank_free[sbank] = tr
            DVE.wait(tr, (ob.ds[ko], ob.ds[ko].n))
            to = DVE.sig(nc.vector.tensor_tensor(out=ob.t[ko][:, :nq], in0=banks[obanks[0]][:, :nq], in1=rec.t[kr][:, :nq], op=ALU.mult))
            rec.free[kr] = to
            P.bank_free[obanks[0]] = to
            P.dma(SP, dst, ob.t[ko][:, :nq], ob.ds[ko], waits=[to])
        return epi

    def finish():
        P.barrier()
        P.es.close()
        return P
    if STOP_AFTER == 0:
        return finish()
    n_aq, n_ak, n_av, n_bq, n_bk = c.A_H, c.A_KV, c.A_KV, 2 * c.B_H, 2 * c.B_H
    o_ak = n_aq
    o_av = o_ak + n_ak
    o_bq = o_av + n_av
    o_bk = o_bq + n_bq
    o_bv = o_bk + n_bk
    def k0_dst(mk, grp):
        s0, n, kind = grp
        if kind == 0:
            if mk < c.A_KV:
                return K0m[mk][:, s0:s0 + n]
            hb_, m_ = (mk - c.A_KV) // 2, (mk - c.A_KV) % 2
            return K0m[c.A_KV + hb_][m_ * 128:(m_ + 1) * 128, s0:s0 + n]
        return K0c[mk, :, s0 - T:s0 - T + n]

    def v0_dst(tt, vcol0, wn):
        if tt >= T:
            return [(V0c[tt - T:tt - T + 128, vcol0:vcol0 + wn], 0, wn)]
        outl = []
        for a_ in range(0, wn, 128):
            vc = vcol0 + a_
            if vc < c.A_KV * 128:
                j, off = vc // 128, 0
            else:
                j, off = c.A_KV + (vc - c.A_KV * 128) // 256, (vc - c.A_KV * 128) % 256
            outl.append((V0h[j][tt:tt + 128, off:off + 128], a_, a_ + 128))
        return outl

    def lat_groups(sbi):
        return [g for g in c.SB[sbi] if g[2] == 0]

    def mlp(layer, sbi, groups):
        with ExitStack() as st:
            res = P.sb(st, "res", [128, KC, c.RW], BF16)
            alloc_w(st)
            norm_phase(layer, 1, sbi, res, groups=groups)
            with ExitStack() as st2:
                e = make_relu2_epi(st2)
                gemm(res, w1[layer], H, sbi, lambda m: ('r', e), groups=groups)
            for kq in range(4):
                load_res(res, actT, sbi, kq, groups=groups)
                with ExitStack() as st2:
                    e = make_resid_epi(st2, layer, 5)
                    gemm(res, w2[layer, kq * D:(kq + 1) * D, :], D, sbi, lambda m: ('r', e), groups=groups)

    def wout(layer, W, sbi, groups):
        with ExitStack() as st:
            res = P.sb(st, "res", [128, KC, c.RW], BF16)
            alloc_w(st)
            load_res(res, attT, sbi, 0, groups=groups)
            e = make_resid_epi(st, layer, 2)
            gemm(res, W.ap(), D, sbi, lambda m: ('r', e), groups=groups)

    for sbi in range(2):
        with ExitStack() as st:
            res = P.sb(st, "res", [128, KC, c.RW], BF16)
            alloc_w(st)
            norm_phase(0, 0, sbi, res)
            qk = make_qk_epi(st)
            vepi = make_v_epi(st, v0_dst)

            def chunk_fn(m):
                if m < o_ak:
                    return ('r', lambda m_, gi, grp, b, tok, mq=m: qk(m_, gi, grp, b, tok, True, 0, Q0[mq, :, grp[0]:grp[0] + grp[1]]))
                if m < o_av:
                    return ('r', lambda m_, gi, grp, b, tok, mk=m - o_ak: qk(m_, gi, grp, b, tok, True, 1, k0_dst(mk, grp)))
                if m < o_bq:
                    return ('v', (m - o_av) * 128, vepi)
                if m < o_bk:
                    return ('r', lambda m_, gi, grp, b, tok, mq=c.A_H + m - o_bq: qk(m_, gi, grp, b, tok, False, 0, Q0[mq, :, grp[0]:grp[0] + grp[1]]))
                if m < o_bv:
                    return ('r', lambda m_, gi, grp, b, tok, mk=c.A_KV + m - o_bk: qk(m_, gi, grp, b, tok, False, 0, k0_dst(mk, grp)))
                return ('v', c.A_KV * 128 + (m - o_bv) * 128, vepi)
            gemm(res, w_in0.ap(), 9 * D // 4, sbi, chunk_fn)

    def allgather(src, dst, ds=None):
        ds = d_cc if ds is None else ds
        ins = nc.gpsimd.collective_compute("AllGather", ALU.bypass, replica_groups=G4, ins=[src.ap().opt()], outs=[dst.ap().opt()])
        ds.n += 1
        ins.then_inc(ds.sem, 1)
        return (ds, ds.n)

    if STOP_AFTER == 1:
        return finish()
    P.barrier(full=True)
    jobcc = []
    for i in range(NKG):
        allgather(K0m[i], K0mg[i])
        allgather(V0h[i], V0hg[i])
        jobcc.append(None)
    P.barrier(full=True)
    if STOP_AFTER == 2:
        return finish()

    NKC = c.NKC
    with ExitStack() as st:
        ac = AttnCtx(st)
        epiA = make_attn_epi_A(st)
        KT = [P.sb(st, "KT%d" % i, [128, 2, NKC * 128], BF16) for i in range(2)]
        VT = [P.sb(st, "VT%d" % i, [128, NKC, 256], BF16) for i in range(2)]
        kvds = [P.ds("kv", st) for _ in range(2)]
        kvfree = [None, None]
        QT = Ring(st, "QT", [128, TT], BF16, 4, dsem=True)
        omap = P.sb(st, "omap", [128, 2, 2, 512], F32)
        dbuf = P.sb(st, "dbuf", [128, 2, 512], F32)
        sqb = P.sb(st, "sqb", [128, 2, 512], BF16)
        recb = P.sb(st, "recb", [128, 512], F32)
        rsB = P.sb(st, "rsB", [128, 512], F32)
        obB = Ring(st, "obB", [128, 512], BF16, 3, dsem=True)
        bst = {"dfree": None, "sqfree": None, "recfree": None, "omapfree": None, "rsfree": None}
        qblocks = [(s, 512, list(range(NKC))) for s in range(0, T, 512)] + [(T, CTX, [NKC - 2, NKC - 1])]
        TC = T // 128

        def load_kv(slot, kmaps, vcol0, dv, vh):
            w = [kvfree[slot], jobcc[vh]]
            nm_ = len(kmaps)
            for j, mk in enumerate(kmaps):
                P.dma(SP, KT[slot][:, j, 0:4 * T].rearrange("p (r t) -> p r t", r=4),
                      K0mg[vh].ap().rearrange("(r m p) t -> p m r t", r=4, m=nm_)[:, j], kvds[slot], waits=w)
                P.dma(SP, KT[slot][:, j, 4 * T:4 * T + CTX], K0c[mk], kvds[slot])
            P.dma(SP, VT[slot][:, 0:4 * TC, 0:dv], V0hg[vh].ap().rearrange("(c p) w -> p c w", p=128), kvds[slot])
            P.dma(SP, VT[slot][:, 4 * TC:4 * TC + 2, 0:dv], V0c[:, vcol0:vcol0 + dv].rearrange("(c p) w -> p c w", p=128), kvds[slot])
            return (kvds[slot], kvds[slot].n)

        def load_q(mq):
            k = QT.nxt()
            tq = P.dma(SP, QT.t[k][:, :], Q0[mq], QT.ds[k], waits=[QT.free[k]])
            return k, tq

        jobs = [('A', kv) for kv in range(c.A_KV)] + [('B', hb) for hb in range(c.B_H)]

        def job_kv(ji):
            kind, idx = jobs[ji]
            if kind == 'A':
                return load_kv(ji % 2, [idx], idx * 128, 128, idx)
            return load_kv(ji % 2, [c.A_KV + 2 * idx, c.A_KV + 2 * idx + 1], c.A_KV * 128 + idx * 256, 256, c.A_KV + idx)

        pendB = []
        tkv_next = job_kv(0)
        for ji, (kind, idx) in enumerate(jobs):
            slot = ji % 2
            tkv = tkv_next
            if ji + 1 < len(jobs):
                tkv_next = job_kv(ji + 1)
            PE.wait(tkv)
            if kind == 'A':
                for hq in range(4 * idx, 4 * idx + 4):
                    kq_, tq = load_q(hq)
                    PE.wait(tq)
                    for (s0, nq, cl) in qblocks:
                        chunks = [(KT[slot][:, 0, ci * 128:(ci + 1) * 128], [VT[slot][:, ci, 0:128]], None) for ci in cl]
                        obk, sbk, tok = attn_unit(ac, QT.t[kq_][:, s0:s0 + nq], nq, chunks, 1)
                        epiA(obk, sbk, tok, nq, attT[hq, :, s0:s0 + nq])
                        run_pending(pendB)
                    QT.free[kq_] = (PE, PE.n)
            else:
                hb = idx
                qs = [load_q(c.A_H + 2 * hb + m) for m in range(2)]
                for (s0, nq, cl) in qblocks:
                    for m in range(2):
                        PE.wait(qs[m][1])
                        chunks = [(KT[slot][:, m, ci * 128:(ci + 1) * 128], [VT[slot][:, ci, 0:128], VT[slot][:, ci, 128:256]], None) for ci in cl]
                        obk, sbk, tok = attn_unit(ac, QT.t[qs[m][0]][:, s0:s0 + nq], nq, chunks, 2)
                        DVE.wait(tok, bst["recfree"])
                        tr = DVE.sig(nc.vector.reciprocal(out=recb[:, :nq], in_=banks[sbk][:, :nq]))
                        DVE.wait(tr, bst["omapfree"])
                        for cc_ in range(2):
                            to = DVE.sig(nc.vector.tensor_tensor(out=omap[:, m, cc_, :nq], in0=banks[obk[cc_]][:, :nq], in1=recb[:, :nq], op=ALU.mult))
                            P.bank_free[obk[cc_]] = to
                        bst["recfree"] = to
                        run_pending(pendB)
                        if m == 0:
                            P.bank_free[sbk] = tr
                            continue
                        DVE.wait(to, bst["dfree"])
                        for cc_ in range(2):
                            td = DVE.sig(nc.vector.scalar_tensor_tensor(out=dbuf[:, cc_, :nq], in0=omap[:, 1, cc_, :nq], scalar=lamb[:, 0:1],
                                                                        in1=omap[:, 0, cc_, :nq], op0=ALU.mult, op1=ALU.add))
                        bst["omapfree"] = td
                        ACT.wait(td, bst["sqfree"])
                        ts = ACT.sig(nc.scalar.activation(out=sqb[:, :, :nq], in_=dbuf[:, :, :nq], func=AF.Square))
                        bst["dfree"] = ts

                        def stage1(sbk=sbk, nq=nq, s0=s0, hb=hb, ts=ts, tr=tr):
                            yield
                            PE.wait(ts, tr)
                            nc.tensor.matmul(banks[sbk][:, :nq], ones[:, :], sqb[:, 0, :nq], start=True, stop=False)
                            tp = PE.sig(nc.tensor.matmul(banks[sbk][:, :nq], ones[:, :], sqb[:, 1, :nq], start=False, stop=True))
                            bst["sqfree"] = tp
                            ACT.wait(tp, bst["rsfree"])
                            ta = ACT.sig(nc.scalar.activation(out=rsB[:, :nq], in_=banks[sbk][:, :nq], func=AF.Sqrt, bias=epsb[:, 0:1], scale=1.0 / 256))
                            P.bank_free[sbk] = ta
                            DVE.wait(ta)
                            tr2 = DVE.sig(nc.vector.reciprocal(out=rsB[:, :nq], in_=rsB[:, :nq]))
                            for cc_ in range(2):
                                ko = obB.nxt()
                                DVE.wait(tr2, (obB.ds[ko], obB.ds[ko].n))
                                to2 = DVE.sig(nc.vector.scalar_tensor_tensor(out=obB.t[ko][:, :nq], in0=dbuf[:, cc_, :nq], scalar=Gq[:, 2 + cc_:3 + cc_],
                                                                             in1=rsB[:, :nq], op0=ALU.mult, op1=ALU.mult))
                                P.dma(SP, attT[c.A_H + 2 * hb + cc_, :, s0:s0 + nq], obB.t[ko][:, :nq], obB.ds[ko], waits=[to2])
                            bst["rsfree"] = to2
                            bst["dfree"] = [ts, to2]
                        pendB.append(stage1())
                        run_pending(pendB)
                for m in range(2):
                    QT.free[qs[m][0]] = (PE, PE.n)
            kvfree[slot] = (PE, PE.n)
        run_pending(pendB, final=True)
        P.barrier()

    if STOP_AFTER == 3:
        return finish()
    for sbi in range(2):
        wout(0, w_out0, sbi, c.SB[sbi])
    for sbi in range(2):
        mlp(0, sbi, c.SB[sbi])

    if STOP_AFTER == 4:
        return finish()
    OWN0 = 256
    BEL0 = 256 + T
    CTX0 = 512 + T
    for sbi in range(2):
        with ExitStack() as st:
            res = P.sb(st, "res", [128, KC, c.RW], BF16)
            alloc_w(st)
            norm_phase(1, 0, sbi, res)

            def q1_dst(m, s0, n, kind):
                return Q1[m, :, s0:s0 + n]

            def k1_dst(m, s0, n, kind):
                e0 = OWN0 + s0 if kind == 0 else CTX0 + s0 - T
                return K1[m - KC, :, e0:e0 + n]

            def v1_dst(tt, vcol0, wn):
                e0 = OWN0 + tt if tt < T else CTX0 + tt - T
                return [(V1[e0:e0 + 128, vcol0:vcol0 + wn], 0, wn)]
            eq = make_copy_epi(st, q1_dst)
            ek = make_copy_epi(st, k1_dst)
            ev = make_v_epi(st, v1_dst)

            def chunk_fn1(m):
                if m < KC:
                    return ('r', eq, True)
                if m < 2 * KC:
                    return ('r', ek)
                return ('v', (m - 2 * KC) * 128, ev)
            gemm(res, w_in1.ap(), 3 * D, sbi, chunk_fn1)

    if STOP_AFTER == 5:
        return finish()
    K1f = K1.ap().rearrange("h p n -> (h p) n")
    for p_ in range(NKP):
        rs_ = slice(p_ * HP * 128, (p_ + 1) * HP * 128)
        P.dma(SP, Kb[p_][:, 0:256], K1f[rs_, OWN0:OWN0 + 256], d_misc)
        P.dma(SP, Kb[p_][:, 256:512], K1f[rs_, T:T + 256], d_misc)
    for i_ in range(4):
        r0_ = (OWN0 if i_ < 2 else T) + (i_ % 2) * 128
        for j_ in range(NVP):
            P.dma(SP, Vb[i_][j_][:, :], V1[r0_:r0_ + 128, j_ * VPW:(j_ + 1) * VPW], d_misc)
    POOL.wait((d_misc, d_misc.n))
    for p_ in range(NKP):
        allgather(Kb[p_], Kbg[p_])
    for i_ in range(4):
        for j_ in range(NVP):
            allgather(Vb[i_][j_], Vbg[i_][j_])
    P.barrier()
    with ExitStack() as st:
        X = P.sb(st, "hx", [128, 4, KC * 256], BF16)
        Y = [P.sb(st, "hy%d" % i, [128, KC * 256], BF16) for i in range(2)]
        yds = [P.ds("hy", st) for _ in range(2)]
        xfree = None
        it = 0
        for (isv, side) in ((0, 0), (0, 1), (1, 0), (1, 1)):
            c0_ = 256 if side == 0 else 0
            tl_ = None
            for r in range(4):
                if isv == 0:
                    for p_ in range(NKP):
                        src = Kbg[p_][r * HP * 128:(r + 1) * HP * 128, c0_:c0_ + 256].rearrange("(h p) n -> p h n", p=128)
                        dstx = X[:, r, p_ * HP * 256:(p_ + 1) * HP * 256].rearrange("p (h n) -> p h n", n=256)
                        tl_ = P.dma(SP, dstx, src, d_misc, waits=[xfree])
                else:
                    for cl_ in range(2):
                        for j_ in range(NVP):
                            src = Vbg[(c0_ // 128) + cl_][j_][r * 128:(r + 1) * 128, :]
                            tl_ = P.dma(SP, X[:, r, cl_ * D + j_ * VPW:cl_ * D + (j_ + 1) * VPW], src, d_misc, waits=[xfree])
            W_ = KC * 256 if isv == 0 else 2 * D
            y = Y[it % 2]
            DVE.wait(tl_, (d_misc, d_misc.n), (yds[it % 2], yds[it % 2].n))
            ty = DVE.sig(nc.vector.tensor_scalar(out=y[:, :W_], in0=X[:, 0, :W_], scalar1=selb[:, 4 * side:4 * side + 1], scalar2=None, op0=ALU.mult))
            for r in range(1, 4):
                DVE.wait(ty)
                ty = DVE.sig(nc.vector.scalar_tensor_tensor(out=y[:, :W_], in0=X[:, r, :W_], scalar=selb[:, 4 * side + r:4 * side + r + 1],
                                                            in1=y[:, :W_], op0=ALU.mult, op1=ALU.add))
            xfree = ty
            e0 = 0 if side == 0 else BEL0
            if isv == 0:
                P.dma(SP, K1[:, :, e0:e0 + 256].rearrange("h p n -> p h n"), y[:, :W_].rearrange("p (h n) -> p h n", n=256), yds[it % 2], waits=[ty])
            else:
                P.dma(SP, V1[e0:e0 + 256, :].rearrange("(c p) w -> p c w", p=128), y[:, :W_].rearrange("p (c w) -> p c w", w=D), yds[it % 2], waits=[ty])
            it += 1
        P.barrier()

    if STOP_AFTER == 6:
        return finish()
    HG = 2
    NT1 = c.EXT // 128
    with ExitStack() as st:
        ac = AttnCtx(st, with_tab=True)
        epiA = make_attn_epi_A(st)
        KT1 = [P.sb(st, "K1T%d" % i, [128, HG, c.EXT], BF16) for i in range(2)]
        VT1 = [P.sb(st, "V1T%d" % i, [128, NT1, HG * 128], BF16) for i in range(2)]
        kvds = [P.ds("kv1", st) for _ in range(2)]
        kvfree = [None, None]
        QT = Ring(st, "Q1T", [128, T], BF16, 2, dsem=True)
        TB = Ring(st, "TB", [128, 3, 8, 512], F32, 2, dsem=True)

        def load_tb(h):
            k = TB.nxt()
            for ts__ in range(3):
                P.dma(SP, TB.t[k][:, ts__, :, :], tabs[ts__, h].rearrange("k p n -> p k n"), TB.ds[k], waits=[TB.free[k]])
            return k, (TB.ds[k], TB.ds[k].n)
        tb_next = load_tb(0)

        def load_kv1(hg):
            slot = hg % 2
            P.dma(SP, KT1[slot][:, :, :], K1[hg * HG:(hg + 1) * HG].rearrange("h p n -> p h n"), kvds[slot], waits=[kvfree[slot]])
            P.dma(SP, VT1[slot][:, :, :], V1[:, hg * HG * 128:(hg + 1) * HG * 128].rearrange("(c p) w -> p c w", p=128), kvds[slot])
            return (kvds[slot], kvds[slot].n)
        nhg = KC // HG
        tnext = load_kv1(0)
        for hg in range(nhg):
            slot = hg % 2
            tkv = tnext
            if hg + 1 < nhg:
                tnext = load_kv1(hg + 1)
            PE.wait(tkv)
            for hh in range(HG):
                h = hg * HG + hh
                ktb, ac.tabtok = tb_next
                if h + 1 < KC:
                    tb_next = load_tb(h + 1)
                kq_ = QT.nxt()
                tq = P.dma(SP, QT.t[kq_][:, :], Q1[h], QT.ds[kq_], waits=[QT.free[kq_]])
                PE.wait(tq)
                for g in range(c.G):
                    ts_ = 0 if g == 0 else (2 if g == c.G - 1 else 1)
                    chunks = []
                    for k in range(8):
                        e0 = g * 512 + k * 128
                        chunks.append((KT1[slot][:, hh, e0:e0 + 128], [VT1[slot][:, e0 // 128, hh * 128:(hh + 1) * 128]], TB.t[ktb][:, ts_, k, :]))
                    for j in range(2):
                        e0 = CTX0 + j * 128
                        chunks.append((KT1[slot][:, hh, e0:e0 + 128], [VT1[slot][:, e0 // 128, hh * 128:(hh + 1) * 128]], None))
                    obk, sbk, tok = attn_unit(ac, QT.t[kq_][:, g * 512:(g + 1) * 512], 512, chunks, 1)
                    epiA(obk, sbk, tok, 512, attT[h, :, g * 512:(g + 1) * 512])
                QT.free[kq_] = (PE, PE.n)
                TB.free[ktb] = (DVE, DVE.n)
            kvfree[slot] = (PE, PE.n)
        P.barrier()

    for sbi in range(2):
        lg = lat_groups(sbi)
        if lg:
            wout(1, w_out1, sbi, lg)
    for sbi in range(2):
        lg = lat_groups(sbi)
        if lg:
            mlp(1, sbi, lg)
    for sbi in range(2):
        lg = lat_groups(sbi)
        if lg:
            norm_phase(1, 0, sbi, None, final=True, groups=lg)
    P.barrier()
    P.es.close()
    return P


_CACHE = {}


def _rope_tables(cfg, q):
    T, TT = cfg.T, cfg.TT
    t = np.arange(T, dtype=np.int32) + q * T
    row = (t // GRID_W).astype(np.float32)
    col = (t % GRID_W).astype(np.float32)
    nf = HD // 4
    freqs = (np.float32(10000.0) ** (-np.arange(nf, dtype=np.float32) / np.float32(nf))).astype(np.float32)
    cosT = np.ones((128, TT), np.float32)
    sinT = np.zeros((128, TT), np.float32)
    for axis, pos in enumerate((row, col)):
        ang = (pos[None, :] * freqs[:, None]).astype(np.float32)
        cs, sn = np.cos(ang).astype(np.float32), np.sin(ang).astype(np.float32)
        for pair in range(2):
            d0 = axis * 64 + pair * 32
            cosT[d0:d0 + 32, :T] = cs
            sinT[d0:d0 + 32, :T] = -sn if pair == 0 else sn
    return cosT, sinT


def _na_tables(cfg, rel_bias):
    ROWS = cfg.ROWS
    out = []
    for qr0 in (0, 8, ROWS - 8):
        w = np.arange(16)
        krow = qr0 - 4 + w
        j = np.arange(8)
        qrow = qr0 + j
        r0q = np.clip(qrow - 4, 0, ROWS - 8)
        vrow = (krow[:, None] >= 0) & (krow[:, None] < ROWS) & (krow[:, None] >= r0q[None, :]) & (krow[:, None] < r0q[None, :] + 8)
        kcol = np.arange(GRID_W)
        qcol = np.arange(GRID_W)
        c0 = np.clip(qcol - 8, 0, GRID_W - 16)
        vcol = (kcol[:, None] >= c0[None, :]) & (kcol[:, None] < c0[None, :] + 16)
        dr = np.clip(krow[:, None] - qrow[None, :] + 7, 0, 14)
        dc = np.clip(kcol[:, None] - qcol[None, :] + 15, 0, 30)
        tab = rel_bias[:, dr[:, None, :, None], dc[None, :, None, :]]
        valid = vrow[:, None, :, None] & vcol[None, :, None, :]
        tab = np.where(valid[None], tab, np.float32(NEG)).astype(np.float32)
        out.append(tab.reshape(rel_bias.shape[0], 8, 128, 512))
    return out


def kernel(x, c, ctx, c_ctx, ada_w, ada_b, norm1_g, norm2_g, w_in_even, w_out_even,
           a_q_norm, a_k_norm, b_lambda_q1, b_lambda_k1, b_lambda_q2, b_lambda_k2, b_subln_g,
           w_in_odd, w_out_odd, na_rel_bias, mlp_w1, mlp_w2, final_g, _cfg=None):
    f = lambda a: np.ascontiguousarray(np.asarray(a, dtype=np.float32))
    x, c, ctx, c_ctx = f(x), f(c), f(ctx), f(c_ctx)
    B, S, D = x.shape
    cfg = _cfg or Cfg(D, S)
    KC, T, TT, MJ = cfg.KC, cfg.T, cfg.TT, cfg.MJ
    key = (D, S)
    if key not in _CACHE:
        _CACHE[key] = build(cfg)
    P = _CACHE[key]
    ada_w, ada_b = f(ada_w), f(ada_b)
    tabs3 = _na_tables(cfg, f(na_rel_bias)[0])
    perm = np.zeros((128, 128), np.float32)
    perm[np.arange(128) ^ 32, np.arange(128)] = 1.0
    smalls = np.stack([f(a_q_norm)[0], f(a_k_norm)[0], f(b_lambda_q1)[0], f(b_lambda_k1)[0], f(b_lambda_q2)[0], f(b_lambda_k2)[0],
                       f(b_subln_g)[0][:128], f(b_subln_g)[0][128:]], axis=1)
    shared = {
        "gn1": f(f(norm1_g).reshape(2, KC, 128).transpose(2, 0, 1)),
        "gn2": f(f(norm2_g).reshape(2, KC, 128).transpose(2, 0, 1)),
        "gfin": f(f(final_g).reshape(KC, 128).T),
        "w_in0": f(w_in_even)[0], "w_out0": f(w_out_even)[0], "w_in1": f(w_in_odd)[0], "w_out1": f(w_out_odd)[0],
        "w1": f(mlp_w1), "w2": f(mlp_w2), "smalls": f(smalls), "perm": perm,
    }
    in_maps = []
    for r in range(8):
        b, q = r // 4, r % 4
        xt = np.concatenate([x[b, q * T:(q + 1) * T], ctx[b]], axis=0)
        cosT, sinT = _rope_tables(cfg, q)
        sel = np.zeros((128, 8), np.float32)
        sel[:, (q - 1) % 4] = 1.0
        sel[:, 4 + (q + 1) % 4] = 1.0
        m = dict(shared)
        m.update({
            "xT": f(xt.T.reshape(KC, 128, TT)),
            "cvT": f(np.stack([c[b], c_ctx]).reshape(2, KC, 128).transpose(2, 1, 0)),
            "adaw": f(ada_w[:, :, q * MJ * 128:(q + 1) * MJ * 128]),
            "adab": f(ada_b[:, q * MJ * 128:(q + 1) * MJ * 128].reshape(2, MJ, 128).transpose(2, 0, 1)),
            "cosT": cosT, "sinT": sinT,
            "tabs": f(np.stack([tabs3[0] if q == 0 else tabs3[1], tabs3[1], tabs3[2] if q == 3 else tabs3[1]])),
            "sel": sel,
        })
        in_maps.append(m)
    res = run_bass_kernel_spmd(P.nc, in_maps, core_ids=list(range(8)))
    out = np.empty((B, S, D), np.float32)
    for r in range(8):
        b, q = r // 4, r % 4
        out[b, q * T:(q + 1) * T] = res.results[r]["outT"].reshape(D, T).T
    kernel.last = res
    return out
```

```python
import math
from contextlib import ExitStack
import numpy as np
import concourse.bass as bass
import concourse.mybir as mybir
from concourse.bass_utils import run_bass_kernel_spmd

F32 = mybir.dt.float32
BF16 = mybir.dt.bfloat16
AF = mybir.ActivationFunctionType
ALU = mybir.AluOpType
EPS = 1e-6
GRID_W = 64
CTX = 256
HD = 128
NEG = -30000.0
STOP_AFTER = 99


class Cfg:
    def __init__(s, D=4096, S=8192):
        s.D, s.S = D, S
        s.KC = D // 128
        s.T = S // 4
        s.TT = s.T + CTX
        s.R = s.T // GRID_W
        s.G = s.R // 8
        s.ROWS = S // GRID_W
        s.A_H = D // 256
        s.A_KV = s.A_H // 4
        s.B_H = D // 512
        s.H = 4 * D
        s.NM_K = s.A_KV + 2 * s.B_H
        s.NM_Q = s.A_H + 2 * s.B_H
        s.VW = s.A_KV * 128 + s.B_H * 256
        s.NKC = (4 * s.T + CTX) // 128
        s.EXT = (8 + s.R) * GRID_W + CTX
        s.MJ = 6 * s.KC // 4
        if s.T == 2048:
            s.SB = [[(0, 384, 0), (384, 384, 0), (768, 384, 0)], [(1152, 512, 0), (1664, 384, 0), (2048, 256, 1)]]
        elif s.T == 1024:
            s.SB = [[(0, 512, 0), (512, 512, 0)], [(1024, 256, 1)]]
        else:
            raise ValueError
        s.RW = max(sum(n for _, n, _ in sb) for sb in s.SB)


class DS:
    def __init__(s, sem):
        s.sem, s.n, s.waited = sem, 0, 0


class Eng:
    def __init__(s, P, name, eng):
        s.P, s.name, s.eng = P, name, eng
        s.sem = P.new_sem("e_" + name)
        s.n = 0
        s.seen = {}

    def wait(s, *toks):
        for t in toks:
            if t is None:
                continue
            if isinstance(t, list):
                s.wait(*t)
                continue
            src, v = t
            if v <= 0 or s.seen.get(id(src), 0) >= v:
                continue
            s.eng.wait_ge(src.sem, v)
            s.seen[id(src)] = v
            if isinstance(src, DS):
                src.waited = max(src.waited, v)

    def sig(s, ins):
        s.n += 1
        ins.then_inc(s.sem, 1)
        return (s, s.n)


class Prog:
    def __init__(s, cfg):
        s.cfg = cfg
        s.nc = bass.Bass("TRN2", target_bir_lowering=False)
        s.es = ExitStack()
        s.nsem = 0
        s.dsems = []
        s.pool_sync = True

    def new_sem(s, name):
        s.nsem += 1
        return s.es.enter_context(s.nc.semaphore(name + "_%d" % s.nsem))

    def ds(s, name="d", st=None):
        if not hasattr(s, "dfree"):
            s.dfree = []
        if s.dfree:
            d = s.dfree.pop()
        else:
            d = DS(s.new_sem(name))
            s.dsems.append(d)
        if st is not None:
            st.callback(lambda d=d: s.dfree.append(d))
        return d

    def start(s):
        nc = s.nc
        s.block = s.es.enter_context(nc.Block())
        s.PE = Eng(s, "pe", nc.tensor)
        s.ACT = Eng(s, "act", nc.scalar)
        s.DVE = Eng(s, "dve", nc.vector)
        s.POOL = Eng(s, "pool", nc.gpsimd)
        s.SP = Eng(s, "sp", nc.sync)
        s.engs = [s.PE, s.ACT, s.DVE, s.POOL, s.SP]
        s.banks = [s.es.enter_context(nc.psum_tensor("bank%d" % i, [128, 512], F32)) for i in range(8)]
        s.bank_free = [None] * 8

    def dma(s, q, out, in_, ds, waits=()):
        q.wait(*waits)
        if ds.waited > 0:
            q.wait((ds, ds.waited))
        ins = q.eng.dma_start(out=out, in_=in_)
        ds.n += 16
        ins.then_inc(ds.sem, 16)
        return (ds, ds.n)

    def barrier(s, full=False):
        toks = [(e, e.n) for e in s.engs] + [(d, d.n) for d in s.dsems]
        for e in s.engs:
            if e is s.POOL and not (full or s.pool_sync):
                continue
            e.wait(*toks)
        s.bank_free = [None] * 8

    def sb(s, st, name, shape, dt):
        s.nsb = getattr(s, "nsb", 0) + 1
        return st.enter_context(s.nc.sbuf_tensor("%s_u%d" % (name, s.nsb), shape, dt))


def build(cfg, debug=False):
    P = Prog(cfg)
    nc = P.nc
    c = cfg
    D, KC, T, TT, H = c.D, c.KC, c.T, c.TT, c.H

    def din(name, shape, dt=F32):
        return nc.dram_tensor(name, list(shape), dt, kind="ExternalInput")

    def dscr(name, shape, dt, out=False):
        return nc.dram_tensor(name, list(shape), dt, kind=("ExternalOutput" if (out and debug) else "Internal"))

    xT = din("xT", [KC, 128, TT])
    cvT = din("cvT", [128, KC, 2])
    adaw = din("adaw", [2, D, c.MJ * 128])
    adab = din("adab", [128, 2, c.MJ])
    gn1 = din("gn1", [128, 2, KC])
    gn2 = din("gn2", [128, 2, KC])
    gfin = din("gfin", [128, KC])
    w_in0 = din("w_in0", [D, 9 * D // 4])
    w_out0 = din("w_out0", [D, D])
    w_in1 = din("w_in1", [D, 3 * D])
    w_out1 = din("w_out1", [D, D])
    w1 = din("w1", [2, D, H])
    w2 = din("w2", [2, H, D])
    smalls = din("smalls", [128, 8])
    cosT = din("cosT", [128, TT])
    sinT = din("sinT", [128, TT])
    permI = din("perm", [128, 128])
    tabs = din("tabs", [3, KC, 8, 128, 512])
    sel = din("sel", [128, 8])
    outT = nc.dram_tensor("outT", [KC, 128, T], F32, kind="ExternalOutput")

    hT = dscr("hT", [KC, 128, TT], F32, out=True)
    mod_src = dscr("mod_src", [128, 2 * c.MJ * 2], F32)
    mod_g = dscr("mod_g", [4 * 128, 2 * c.MJ * 2], F32)
    Q0 = dscr("Q0", [c.NM_Q, 128, TT], BF16)
    NKG = c.A_KV + c.B_H
    KGR = [128] * c.A_KV + [256] * c.B_H
    K0m = [dscr("K0m%d" % i, [KGR[i], T], BF16) for i in range(NKG)]
    K0mg = [dscr("K0mg%d" % i, [4 * KGR[i], T], BF16) for i in range(NKG)]
    K0c = dscr("K0c", [c.NM_K, 128, CTX], BF16)
    NVH = c.A_KV + c.B_H
    DV = [128] * c.A_KV + [256] * c.B_H
    V0h = [dscr("V0h%d" % i, [T, DV[i]], BF16) for i in range(NVH)]
    V0hg = [dscr("V0hg%d" % i, [4 * T, DV[i]], BF16) for i in range(NVH)]
    V0c = dscr("V0c", [CTX, c.VW], BF16)
    attT = dscr("attT", [KC, 128, TT], BF16)
    actT = dscr("actT", [4 * KC, 128, TT], BF16)
    Q1 = dscr("Q1", [KC, 128, T], BF16)
    K1 = dscr("K1", [KC, 128, c.EXT], BF16)
    V1 = dscr("V1", [c.EXT, D], BF16)
    HP = min(8, KC)
    NKP = KC // HP
    Kb = [dscr("Kb%d" % i, [HP * 128, 512], BF16) for i in range(NKP)]
    Kbg = [dscr("Kbg%d" % i, [4 * HP * 128, 512], BF16) for i in range(NKP)]
    VPW = min(D, 4096)
    NVP = D // VPW
    Vb = [[dscr("Vb%d_%d" % (i, j), [128, VPW], BF16) for j in range(NVP)] for i in range(4)]
    Vbg = [[dscr("Vbg%d_%d" % (i, j), [4 * 128, VPW], BF16) for j in range(NVP)] for i in range(4)]

    P.start()
    PE, ACT, DVE, POOL, SP = P.PE, P.ACT, P.DVE, P.POOL, P.SP
    banks = P.banks
    G4 = [[0, 1, 2, 3], [4, 5, 6, 7]]

    gs = P.es
    ones = P.sb(gs, "ones", [128, 128], BF16)
    perm = P.sb(gs, "permb", [128, 128], BF16)
    epsb = P.sb(gs, "epsb", [128, 1], F32)
    smb = P.sb(gs, "smb", [128, 8], F32)
    selb = P.sb(gs, "selb", [128, 8], F32)
    modf = P.sb(gs, "modf", [128, 2, 6 * KC, 2], F32)
    Am = P.sb(gs, "Am", [128, 2, 2, 2, KC], F32)
    g1b = P.sb(gs, "g1b", [128, 2, KC], F32)
    g2b = P.sb(gs, "g2b", [128, 2, KC], F32)
    gfb = P.sb(gs, "gfb", [128, KC], F32)
    Gq = P.sb(gs, "Gq", [128, 4], F32)
    lamb = P.sb(gs, "lamb", [128, 2], F32)
    cosb = P.sb(gs, "cosb", [128, TT], BF16)
    sinb = P.sb(gs, "sinb", [128, TT], BF16)
    NW = 4
    wring = [None] * NW
    wds = [P.ds("w") for _ in range(NW)]
    wfree = [None] * NW
    wcnt = [0]

    def alloc_w(st):
        P.barrier(full=True)
        wring[:] = [P.sb(st, "wr%d_%d" % (i, P.nsem), [128, KC, 256], BF16) for i in range(NW)]
        for i in range(NW):
            wfree[i] = None
        P.pool_sync = False

        def _restore():
            P.pool_sync = True
            P.barrier(full=True)
        st.callback(_restore)
    d_misc = P.ds("misc")
    d_pm = P.ds("pmisc")
    d_cc = P.ds("cc")

    def modv(i, part, v):
        return modf[:, i, part * KC:(part + 1) * KC, v]

    with ExitStack() as st:
        cvf = P.sb(st, "cvf", [128, KC, 2], F32)
        cvb = P.sb(st, "cvb", [128, KC, 2], BF16)
        adabb = P.sb(st, "adabb", [128, 2, c.MJ], F32)
        modl = P.sb(st, "modl", [128, 2, c.MJ, 2], F32)
        lt = P.sb(st, "lt", [128, 4], F32)
        alloc_w(st)
        t_in = [P.dma(SP, cvf[:, :, :], cvT[:, :, :], d_misc),
                P.dma(SP, adabb[:, :, :], adab[:, :, :], d_misc),
                P.dma(SP, smb[:, :], smalls[:, :], d_misc),
                P.dma(SP, selb[:, :], sel[:, :], d_misc),
                P.dma(SP, g1b[:, :, :], gn1[:, :, :], d_misc),
                P.dma(SP, g2b[:, :, :], gn2[:, :, :], d_misc),
                P.dma(SP, gfb[:, :], gfin[:, :], d_misc),
                P.dma(POOL, perm[:, :], permI[:, :], d_pm),
                P.dma(POOL, cosb[:, :], cosT[:, :], d_pm),
                P.dma(POOL, sinb[:, :], sinT[:, :], d_pm)]
        t_ones = DVE.sig(nc.vector.memset(ones[:, :], 1.0))
        t_eps = DVE.sig(nc.vector.memset(epsb[:, :], EPS))
        ACT.wait(t_in[6], t_in[-1])
        t_cv = ACT.sig(nc.scalar.activation(out=cvb[:, :, :], in_=cvf[:, :, :], func=AF.Silu))
        psm = banks[0]
        for i in range(2):
            for ct in range(c.MJ // 2):
                slot = wcnt[0] % NW
                wcnt[0] += 1
                src = adaw[i].rearrange("(kc p) n -> p kc n", p=128)[:, :, ct * 256:(ct + 1) * 256]
                tw = P.dma(POOL, wring[slot][:, :, :], src, wds[slot], waits=[wfree[slot]])
                PE.wait(tw, t_cv)
                for mc in range(2):
                    j = ct * 2 + mc
                    col = (i * c.MJ + j) * 2
                    for kc in range(KC):
                        mm = nc.tensor.matmul(psm[:, col:col + 2], wring[slot][:, kc, mc * 128:(mc + 1) * 128],
                                              cvb[:, kc, :], start=(kc == 0), stop=(kc == KC - 1))
                wfree[slot] = PE.sig(mm)
        DVE.wait((PE, PE.n), t_in[6], t_in[-1])
        for i in range(2):
            for v in range(2):
                tm = DVE.sig(nc.vector.tensor_tensor(
                    out=modl[:, i, :, v], in0=psm[:, i * c.MJ * 2:(i + 1) * c.MJ * 2].rearrange("p (j v) -> p j v", v=2)[:, :, v],
                    in1=adabb[:, i, :], op=ALU.add))
        t1 = P.dma(SP, mod_src[:, :], modl[:, :, :, :].rearrange("p i j v -> p (i j v)"), d_misc, waits=[tm])
        POOL.wait(t1)
        cc = nc.gpsimd.collective_compute("AllGather", ALU.bypass, replica_groups=G4,
                                          ins=[mod_src.ap().opt()], outs=[mod_g.ap().opt()])
        d_cc.n += 1
        cc.then_inc(d_cc.sem, 1)
        tcc = (d_cc, d_cc.n)
        tl = None
        for r in range(4):
            for i in range(2):
                tl = P.dma(SP, modf[:, i, r * c.MJ:(r + 1) * c.MJ, :],
                           mod_g[r * 128:(r + 1) * 128, :].rearrange("p (i j v) -> p i j v", i=2, v=2)[:, i, :, :],
                           d_misc, waits=[tcc])
        DVE.wait(tl)
        for i in range(2):
            for sub, (part, gb) in enumerate(((1, g1b), (4, g2b))):
                for v in range(2):
                    ta = DVE.sig(nc.vector.scalar_tensor_tensor(out=Am[:, i, sub, v, :], in0=modv(i, part, v), scalar=1.0,
                                                                in1=gb[:, i, :], op0=ALU.add, op1=ALU.mult))
        DVE.sig(nc.vector.tensor_copy(out=Gq[:, 0:2], in_=smb[:, 0:2]))
        DVE.sig(nc.vector.tensor_scalar(out=Gq[:, 2:4], in0=smb[:, 6:8], scalar1=0.8, scalar2=None, op0=ALU.mult))
        onesf = P.sb(st, "onesf", [128, 128], F32)
        tof = DVE.sig(nc.vector.memset(onesf[:, :], 1.0))
        DVE.sig(nc.vector.tensor_tensor(out=lt[:, 0:1], in0=smb[:, 2:3], in1=smb[:, 3:4], op=ALU.mult))
        tl2 = DVE.sig(nc.vector.tensor_tensor(out=lt[:, 1:2], in0=smb[:, 4:5], in1=smb[:, 5:6], op=ALU.mult))
        PE.wait(tl2)
        tp = PE.sig(nc.tensor.matmul(banks[1][:, 0:2], onesf[:, :], lt[:, 0:2], start=True, stop=True))
        ACT.wait(tp)
        te = ACT.sig(nc.scalar.activation(out=lt[:, 2:4], in_=banks[1][:, 0:2], func=AF.Exp))
        DVE.wait(te)
        DVE.sig(nc.vector.tensor_tensor(out=lamb[:, 0:1], in0=lt[:, 3:4], in1=lt[:, 2:3], op=ALU.subtract))
        DVE.wait((DVE, DVE.n))
        DVE.sig(nc.vector.tensor_scalar(out=lamb[:, 0:1], in0=lamb[:, 0:1], scalar1=-0.2, scalar2=None, op0=ALU.add))
        for kc in range(KC):
            P.dma(SP, hT[kc], xT[kc], d_misc)
        P.barrier()

    def w_tile(Wap, c0):
        slot = wcnt[0] % NW
        wcnt[0] += 1
        src = Wap.rearrange("(kc p) n -> p kc n", p=128)[:, :, c0:c0 + 256]
        tok = P.dma(POOL, wring[slot][:, :, :], src, wds[slot], waits=[wfree[slot]])
        return slot, tok

    bank_rr = [0]

    def next_bank(lo, n):
        b = lo + (bank_rr[0] % n)
        bank_rr[0] += 1
        return b

    def run_pending(pend, final=False):
        keep = []
        for g in pend:
            try:
                next(g)
                keep.append(g)
            except StopIteration:
                pass
        pend[:] = keep
        if final:
            while pend:
                run_pending(pend)

    def norm_phase(layer, sub, sbi, res, final=False, groups=None):
        groups = c.SB[sbi] if groups is None else groups
        if not groups:
            return
        t0 = c.SB[sbi][0][0]
        with ExitStack() as st:
            stg = [P.sb(st, "nst%d" % i, [128, 512], F32) for i in range(3)]
            sgd = [P.ds("nst", st) for _ in range(3)]
            sgf = [None] * 3
            sq = [P.sb(st, "nsq%d" % i, [128, 512], BF16) for i in range(2)]
            sqf = [None] * 2
            rstd = P.sb(st, "rstd", [128, c.RW], F32)
            tmp = [P.sb(st, "ntmp%d" % i, [128, 512], F32) for i in range(2)]
            tmpf = [None] * 2
            od = [P.ds("no", st) for _ in range(2)]
            u = 0
            for kc in range(KC):
                for gi, (s0, n, kind) in enumerate(groups):
                    sl = u % 3
                    td = P.dma(SP, stg[sl][:, :n], hT[kc, :, s0:s0 + n], sgd[sl], waits=[sgf[sl]])
                    ACT.wait(td, sqf[u % 2])
                    ta = ACT.sig(nc.scalar.activation(out=sq[u % 2][:, :n], in_=stg[sl][:, :n], func=AF.Square))
                    sgf[sl] = ta
                    PE.wait(ta)
                    mm = nc.tensor.matmul(banks[gi][:, :n], ones[:, :], sq[u % 2][:, :n], start=(kc == 0), stop=(kc == KC - 1))
                    sqf[u % 2] = PE.sig(mm)
                    u += 1
            for gi, (s0, n, kind) in enumerate(groups):
                ACT.wait((PE, PE.n))
                ta = ACT.sig(nc.scalar.activation(out=rstd[:, s0 - t0:s0 - t0 + n], in_=banks[gi][:, :n], func=AF.Sqrt,
                                                  bias=epsb[:, 0:1], scale=1.0 / D))
                DVE.wait(ta)
                tr = DVE.sig(nc.vector.reciprocal(out=rstd[:, s0 - t0:s0 - t0 + n], in_=rstd[:, s0 - t0:s0 - t0 + n]))
            DVE.wait(tr)
            for kc in range(KC):
                for gi, (s0, n, kind) in enumerate(groups):
                    sl = u % 3
                    td = P.dma(SP, stg[sl][:, :n], hT[kc, :, s0:s0 + n], sgd[sl], waits=[sgf[sl]])
                    DVE.wait(td, tmpf[u % 2])
                    tv = DVE.sig(nc.vector.tensor_tensor(out=tmp[u % 2][:, :n], in0=stg[sl][:, :n],
                                                         in1=rstd[:, s0 - t0:s0 - t0 + n], op=ALU.mult))
                    sgf[sl] = tv
                    ACT.wait(tv)
                    if final:
                        ta = ACT.sig(nc.scalar.activation(out=tmp[u % 2][:, :n], in_=tmp[u % 2][:, :n], func=AF.Identity,
                                                          scale=gfb[:, kc:kc + 1]))
                        P.dma(SP, outT[kc, :, s0:s0 + n], tmp[u % 2][:, :n], od[u % 2], waits=[ta])
                        tmpf[u % 2] = (od[u % 2], od[u % 2].n)
                    else:
                        ta = ACT.sig(nc.scalar.activation(out=res[:, kc, s0 - t0:s0 - t0 + n], in_=tmp[u % 2][:, :n],
                                                          func=AF.Identity, scale=Am[:, layer, sub, kind, kc:kc + 1],
                                                          bias=modf[:, layer, (0 if sub == 0 else 3) * KC + kc, kind:kind + 1]))
                        tmpf[u % 2] = ta
                    u += 1
            P.barrier()

    def load_res(res, src, sbi, kq=0, groups=None):
        groups = c.SB[sbi] if groups is None else groups
        t0, t1 = groups[0][0], groups[-1][0] + groups[-1][1]
        tk = None
        for kc in range(KC):
            tk = P.dma(SP, res[:, kc, 0:t1 - t0], src[kq * KC + kc, :, t0:t1], d_misc)
        PE.wait((d_misc, d_misc.n))

    def gemm(res, Wap, ncols, sbi, chunk_fn, groups=None, nb=4):
        groups = c.SB[sbi] if groups is None else groups
        t0 = c.SB[sbi][0][0]
        pend = []
        nt = ncols // 256
        tiles = []
        for ct in range(nt):
            while len(tiles) < min(nt, ct + NW):
                tiles.append(w_tile(Wap, len(tiles) * 256))
            slot, tw = tiles[ct]
            PE.wait(tw)
            kinds = [chunk_fn(ct * 2 + mc) for mc in range(2)]
            last = None
            if kinds[0] is not None and kinds[0][0] == 'v' and kinds[1] is not None and kinds[1][0] == 'v':
                vlist = [(0, 256, kinds[0])]
            else:
                vlist = [(mc * 128, 128, kinds[mc]) for mc in range(2) if kinds[mc] is not None and kinds[mc][0] == 'v']
            for (wc0, wn, kd) in vlist:
                tb0, tb1 = groups[0][0], groups[-1][0] + groups[-1][1]
                for tt in range(tb0, tb1, 128):
                    b = next_bank(0, nb)
                    PE.wait(P.bank_free[b])
                    for kc in range(KC):
                        mm = nc.tensor.matmul(banks[b][:, :wn], res[:, kc, tt - t0:tt - t0 + 128],
                                              wring[slot][:, kc, wc0:wc0 + wn], start=(kc == 0), stop=(kc == KC - 1))
                    last = PE.sig(mm)
                    pend.append(kd[2](kd[1], wn, tt, b, last))
                    run_pending(pend)
            for mc in range(2):
                kd = kinds[mc]
                if kd is None or kd[0] != 'r':
                    continue
                for gi, grp in enumerate(groups):
                    s0, n, kind = grp
                    if len(kd) > 2 and kd[2] and kind == 1:
                        continue
                    b = next_bank(0, nb)
                    PE.wait(P.bank_free[b])
                    for kc in range(KC):
                        mm = nc.tensor.matmul(banks[b][:, :n], wring[slot][:, kc, mc * 128:(mc + 1) * 128],
                                              res[:, kc, s0 - t0:s0 - t0 + n], start=(kc == 0), stop=(kc == KC - 1))
                    last = PE.sig(mm)
                    pend.append(kd[1](ct * 2 + mc, gi, grp, b, last))
                    run_pending(pend)
            wfree[slot] = last if last is not None else (PE, PE.n)
        run_pending(pend, final=True)
        P.barrier()

    class Ring:
        def __init__(s, st, name, shape, dt, n, dsem=False):
            s.t = [P.sb(st, "%s%d" % (name, i), shape, dt) for i in range(n)]
            s.free = [None] * n
            s.ds = [P.ds(name, st) for _ in range(n)] if dsem else None
            s.i = 0
            s.n = n

        def nxt(s):
            k = s.i % s.n
            s.i += 1
            return k

    def make_qk_epi(st):
        sq = Ring(st, "esq", [128, 512], BF16, 2)
        rs = Ring(st, "ers", [128, 512], F32, 2)
        qn = Ring(st, "eqn", [128, 512], BF16, 3)
        t1 = Ring(st, "et1", [128, 512], F32, 2)
        t2 = Ring(st, "et2", [128, 512], F32, 2)
        ob = Ring(st, "eob", [128, 512], BF16, 2, dsem=True)

        def epi(m, gi, grp, b, tok, normed, gcol, dstap):
            s0, n, kind = grp
            kq = qn.nxt()
            if normed:
                k1 = sq.nxt()
                ACT.wait(tok, sq.free[k1])
                ta = ACT.sig(nc.scalar.activation(out=sq.t[k1][:, :n], in_=banks[b][:, :n], func=AF.Square))
                yield
                bs = next_bank(4, 2)
                PE.wait(ta, P.bank_free[bs])
                tp = PE.sig(nc.tensor.matmul(banks[bs][:, :n], ones[:, :], sq.t[k1][:, :n], start=True, stop=True))
                sq.free[k1] = tp
                k2 = rs.nxt()
                ACT.wait(tp, rs.free[k2])
                ta2 = ACT.sig(nc.scalar.activation(out=rs.t[k2][:, :n], in_=banks[bs][:, :n], func=AF.Sqrt,
                                                   bias=epsb[:, 0:1], scale=1.0 / 128))
                P.bank_free[bs] = ta2
                DVE.wait(ta2)
                tr = DVE.sig(nc.vector.reciprocal(out=rs.t[k2][:, :n], in_=rs.t[k2][:, :n]))
                DVE.wait(tr, qn.free[kq])
                tq = DVE.sig(nc.vector.scalar_tensor_tensor(out=qn.t[kq][:, :n], in0=banks[b][:, :n], scalar=Gq[:, gcol:gcol + 1],
                                                            in1=rs.t[k2][:, :n], op0=ALU.mult, op1=ALU.mult))
                rs.free[k2] = tq
            else:
                ACT.wait(tok, qn.free[kq])
                tq = ACT.sig(nc.scalar.activation(out=qn.t[kq][:, :n], in_=banks[b][:, :n], func=AF.Copy))
            P.bank_free[b] = tq
            yield
            bw = next_bank(6, 2)
            PE.wait(tq, P.bank_free[bw])
            tp2 = PE.sig(nc.tensor.matmul(banks[bw][:, :n], perm[:, :], qn.t[kq][:, :n], start=True, stop=True))
            ka, kb, ko = t1.nxt(), t2.nxt(), ob.nxt()
            DVE.wait(tq, t1.free[ka])
            ta_ = DVE.sig(nc.vector.tensor_tensor(out=t1.t[ka][:, :n], in0=qn.t[kq][:, :n], in1=cosb[:, s0:s0 + n], op=ALU.mult))
            DVE.wait(tp2, t2.free[kb])
            tb_ = DVE.sig(nc.vector.tensor_tensor(out=t2.t[kb][:, :n], in0=banks[bw][:, :n], in1=sinb[:, s0:s0 + n], op=ALU.mult))
            P.bank_free[bw] = tb_
            qn.free[kq] = [tp2, ta_]
            DVE.wait(tb_, (ob.ds[ko], ob.ds[ko].n))
            to = DVE.sig(nc.vector.tensor_tensor(out=ob.t[ko][:, :n], in0=t1.t[ka][:, :n], in1=t2.t[kb][:, :n], op=ALU.add))
            t1.free[ka] = to
            t2.free[kb] = to
            P.dma(SP, dstap, ob.t[ko][:, :n], ob.ds[ko], waits=[to])
        return epi

    def make_copy_epi(st, dst_fn):
        ob = Ring(st, "cob", [128, 512], BF16, 3, dsem=True)

        def epi(m, gi, grp, b, tok):
            s0, n, kind = grp
            ko = ob.nxt()
            eng = ACT if (ob.i % 2 == 0) else DVE
            eng.wait(tok, (ob.ds[ko], ob.ds[ko].n))
            if eng is ACT:
                tq = ACT.sig(nc.scalar.activation(out=ob.t[ko][:, :n], in_=banks[b][:, :n], func=AF.Copy))
            else:
                tq = DVE.sig(nc.vector.tensor_copy(out=ob.t[ko][:, :n], in_=banks[b][:, :n]))
            P.bank_free[b] = tq
            P.dma(SP, dst_fn(m, s0, n, kind), ob.t[ko][:, :n], ob.ds[ko], waits=[tq])
            return
            yield
        return epi

    def make_v_epi(st, dst_fn):
        ob = Ring(st, "vob", [128, 256], BF16, 3, dsem=True)

        def epi(vcol0, wn, tt, b, tok):
            ko = ob.nxt()
            eng = ACT if (ob.i % 2 == 0) else DVE
            eng.wait(tok, (ob.ds[ko], ob.ds[ko].n))
            if eng is ACT:
                tq = ACT.sig(nc.scalar.activation(out=ob.t[ko][:, :wn], in_=banks[b][:, :wn], func=AF.Copy))
            else:
                tq = DVE.sig(nc.vector.tensor_copy(out=ob.t[ko][:, :wn], in_=banks[b][:, :wn]))
            P.bank_free[b] = tq
            for (dst, a_, b_) in dst_fn(tt, vcol0, wn):
                P.dma(SP, dst, ob.t[ko][:, a_:b_], ob.ds[ko], waits=[tq])
            return
            yield
        return epi

    def make_resid_epi(st, layer, gate_part, kq_off=0):
        hin = Ring(st, "hin", [128, 512], F32, 3, dsem=True)
        hout = Ring(st, "hout", [128, 512], F32, 3, dsem=True)

        def epi(m, gi, grp, b, tok):
            s0, n, kind = grp
            ki, ko = hin.nxt(), hout.nxt()
            td = P.dma(SP, hin.t[ki][:, :n], hT[m, :, s0:s0 + n], hin.ds[ki], waits=[hin.free[ki]])
            DVE.wait(tok, td, (hout.ds[ko], hout.ds[ko].n))
            to = DVE.sig(nc.vector.scalar_tensor_tensor(out=hout.t[ko][:, :n], in0=banks[b][:, :n],
                                                        scalar=modf[:, layer, gate_part * KC + m, kind:kind + 1],
                                                        in1=hin.t[ki][:, :n], op0=ALU.mult, op1=ALU.add))
            hin.free[ki] = to
            P.bank_free[b] = to
            P.dma(SP, hT[m, :, s0:s0 + n], hout.t[ko][:, :n], hout.ds[ko], waits=[to])
            return
            yield
        return epi

    def make_relu2_epi(st):
        rb = Ring(st, "rb", [128, 512], F32, 3)
        ob = Ring(st, "rob", [128, 512], BF16, 3, dsem=True)

        def epi(m, gi, grp, b, tok):
            s0, n, kind = grp
            kr, ko = rb.nxt(), ob.nxt()
            ACT.wait(tok, rb.free[kr])
            ta = ACT.sig(nc.scalar.activation(out=rb.t[kr][:, :n], in_=banks[b][:, :n], func=AF.Relu))
            P.bank_free[b] = ta
            DVE.wait(ta, (ob.ds[ko], ob.ds[ko].n))
            to = DVE.sig(nc.vector.tensor_tensor(out=ob.t[ko][:, :n], in0=rb.t[kr][:, :n], in1=rb.t[kr][:, :n], op=ALU.mult))
            rb.free[kr] = to
            P.dma(SP, actT[m, :, s0:s0 + n], ob.t[ko][:, :n], ob.ds[ko], waits=[to])
            return
            yield
        return epi

    SCALE = 1.0 / math.sqrt(HD)

    class AttnCtx:
        def __init__(s, st, with_tab=False):
            s.pb = Ring(st, "apb", [128, 512], BF16, 5)
            s.si = 0
            s.ui = 0
            s.tabtok = None
            if with_tab:
                s.sbuf = Ring(st, "asb", [128, 512], F32, 3)

    def attn_unit(ac, qT, nq, chunks, nv):
        if nv == 1:
            Sb, LA = [0, 1, 2, 3], 3
            base = 4 + 2 * (ac.ui % 2)
            ob, sb_ = [base], base + 1
        else:
            Sb, LA = [0, 1], 1
            base = 2 + 3 * (ac.ui % 2)
            ob, sb_ = [base, base + 1], base + 2
        ac.ui += 1
        nch = len(chunks)

        def qk(i):
            b = Sb[ac.si % len(Sb)]
            ac.si += 1
            PE.wait(P.bank_free[b])
            mm = nc.tensor.matmul(banks[b][:, :nq], chunks[i][0], qT, start=True, stop=True)
            return b, PE.sig(mm)
        for bb in ob + [sb_]:
            PE.wait(P.bank_free[bb])
        qks = {}
        for j in range(min(LA, nch)):
            qks[j] = qk(j)
        last = None
        for i in range(nch):
            if i + LA < nch:
                qks[i + LA] = qk(i + LA)
            b, tqk = qks.pop(i)
            kp = ac.pb.nxt()
            tabsrc = chunks[i][2]
            if tabsrc is not None:
                ks = ac.sbuf.nxt()
                DVE.wait(tqk, ac.tabtok, ac.sbuf.free[ks])
                tv = DVE.sig(nc.vector.scalar_tensor_tensor(out=ac.sbuf.t[ks][:, :nq], in0=banks[b][:, :nq], scalar=SCALE,
                                                            in1=tabsrc, op0=ALU.mult, op1=ALU.add))
                P.bank_free[b] = tv
                ACT.wait(tv, ac.pb.free[kp])
                te = ACT.sig(nc.scalar.activation(out=ac.pb.t[kp][:, :nq], in_=ac.sbuf.t[ks][:, :nq], func=AF.Exp))
                ac.sbuf.free[ks] = te
            else:
                ACT.wait(tqk, ac.pb.free[kp])
                te = ACT.sig(nc.scalar.activation(out=ac.pb.t[kp][:, :nq], in_=banks[b][:, :nq], func=AF.Exp, scale=SCALE))
                P.bank_free[b] = te
            PE.wait(te)
            for vi in range(nv):
                nc.tensor.matmul(banks[ob[vi]][:, :nq], chunks[i][1][vi], ac.pb.t[kp][:, :nq], start=(i == 0), stop=(i == nch - 1))
            mm = nc.tensor.matmul(banks[sb_][:, :nq], ones[:, :], ac.pb.t[kp][:, :nq], start=(i == 0), stop=(i == nch - 1))
            last = PE.sig(mm)
            ac.pb.free[kp] = last
        return ob, sb_, last

    def make_attn_epi_A(st):
        rec = Ring(st, "arec", [128, 512], F32, 2)
        ob = Ring(st, "aob", [128, 512], BF16, 3, dsem=True)

        def epi(obanks, sbank, tok, nq, dst):
            kr, ko = rec.nxt(), ob.nxt()
            DVE.wait(tok, rec.free[kr])
            tr = DVE.sig(nc.vector.reciprocal(out=rec.t[kr][:, :nq], in_=banks[sbank][:, :nq]))
            P.bank_free[sbank] = tr
            DVE.wait(tr, (ob.ds[ko], ob.ds[ko].n))
            to = DVE.sig(nc.vector.tensor_tensor(out=ob.t[ko][:, :nq], in0=banks[obanks[0]][:, :nq], in1=rec.t[kr][:, :nq], op=ALU.mult))
            rec.free[kr] = to
            P.bank_free[obanks[0]] = to
            P.dma(SP, dst, ob.t[ko][:, :nq], ob.ds[ko], waits=[to])
        return epi

    def finish():
        P.barrier()
        P.es.close()
        return P
    if STOP_AFTER == 0:
        return finish()
    n_aq, n_ak, n_av, n_bq, n_bk = c.A_H, c.A_KV, c.A_KV, 2 * c.B_H, 2 * c.B_H
    o_ak = n_aq
    o_av = o_ak + n_ak
    o_bq = o_av + n_av
    o_bk = o_bq + n_bq
    o_bv = o_bk + n_bk
    def k0_dst(mk, grp):
        s0, n, kind = grp
        if kind == 0:
            if mk < c.A_KV:
                return K0m[mk][:, s0:s0 + n]
            hb_, m_ = (mk - c.A_KV) // 2, (mk - c.A_KV) % 2
            return K0m[c.A_KV + hb_][m_ * 128:(m_ + 1) * 128, s0:s0 + n]
        return K0c[mk, :, s0 - T:s0 - T + n]

    def v0_dst(tt, vcol0, wn):
        if tt >= T:
            return [(V0c[tt - T:tt - T + 128, vcol0:vcol0 + wn], 0, wn)]
        outl = []
        for a_ in range(0, wn, 128):
            vc = vcol0 + a_
            if vc < c.A_KV * 128:
                j, off = vc // 128, 0
            else:
                j, off = c.A_KV + (vc - c.A_KV * 128) // 256, (vc - c.A_KV * 128) % 256
            outl.append((V0h[j][tt:tt + 128, off:off + 128], a_, a_ + 128))
        return outl

    def lat_groups(sbi):
        return [g for g in c.SB[sbi] if g[2] == 0]

    def mlp(layer, sbi, groups):
        with ExitStack() as st:
            res = P.sb(st, "res", [128, KC, c.RW], BF16)
            alloc_w(st)
            norm_phase(layer, 1, sbi, res, groups=groups)
            with ExitStack() as st2:
                e = make_relu2_epi(st2)
                gemm(res, w1[layer], H, sbi, lambda m: ('r', e), groups=groups)
            for kq in range(4):
                load_res(res, actT, sbi, kq, groups=groups)
                with ExitStack() as st2:
                    e = make_resid_epi(st2, layer, 5)
                    gemm(res, w2[layer, kq * D:(kq + 1) * D, :], D, sbi, lambda m: ('r', e), groups=groups)

    def wout(layer, W, sbi, groups):
        with ExitStack() as st:
            res = P.sb(st, "res", [128, KC, c.RW], BF16)
            alloc_w(st)
            load_res(res, attT, sbi, 0, groups=groups)
            e = make_resid_epi(st, layer, 2)
            gemm(res, W.ap(), D, sbi, lambda m: ('r', e), groups=groups)

    for sbi in range(2):
        with ExitStack() as st:
            res = P.sb(st, "res", [128, KC, c.RW], BF16)
            alloc_w(st)
            norm_phase(0, 0, sbi, res)
            qk = make_qk_epi(st)
            vepi = make_v_epi(st, v0_dst)

            def chunk_fn(m):
                if m < o_ak:
                    return ('r', lambda m_, gi, grp, b, tok, mq=m: qk(m_, gi, grp, b, tok, True, 0, Q0[mq, :, grp[0]:grp[0] + grp[1]]))
                if m < o_av:
                    return ('r', lambda m_, gi, grp, b, tok, mk=m - o_ak: qk(m_, gi, grp, b, tok, True, 1, k0_dst(mk, grp)))
                if m < o_bq:
                    return ('v', (m - o_av) * 128, vepi)
                if m < o_bk:
                    return ('r', lambda m_, gi, grp, b, tok, mq=c.A_H + m - o_bq: qk(m_, gi, grp, b, tok, False, 0, Q0[mq, :, grp[0]:grp[0] + grp[1]]))
                if m < o_bv:
                    return ('r', lambda m_, gi, grp, b, tok, mk=c.A_KV + m - o_bk: qk(m_, gi, grp, b, tok, False, 0, k0_dst(mk, grp)))
                return ('v', c.A_KV * 128 + (m - o_bv) * 128, vepi)
            gemm(res, w_in0.ap(), 9 * D // 4, sbi, chunk_fn)

    def allgather(src, dst, ds=None):
        ds = d_cc if ds is None else ds
        ins = nc.gpsimd.collective_compute("AllGather", ALU.bypass, replica_groups=G4, ins=[src.ap().opt()], outs=[dst.ap().opt()])
        ds.n += 1
        ins.then_inc(ds.sem, 1)
        return (ds, ds.n)

    if STOP_AFTER == 1:
        return finish()
    P.barrier(full=True)
    jobcc = []
    for i in range(NKG):
        allgather(K0m[i], K0mg[i])
        allgather(V0h[i], V0hg[i])
        jobcc.append(None)
    P.barrier(full=True)
    if STOP_AFTER == 2:
        return finish()

    NKC = c.NKC
    with ExitStack() as st:
        ac = AttnCtx(st)
        epiA = make_attn_epi_A(st)
        KT = [P.sb(st, "KT%d" % i, [128, 2, NKC * 128], BF16) for i in range(2)]
        VT = [P.sb(st, "VT%d" % i, [128, NKC, 256], BF16) for i in range(2)]
        kvds = [P.ds("kv", st) for _ in range(2)]
        kvfree = [None, None]
        QT = Ring(st, "QT", [128, TT], BF16, 4, dsem=True)
        omap = P.sb(st, "omap", [128, 2, 2, 512], F32)
        dbuf = P.sb(st, "dbuf", [128, 2, 512], F32)
        sqb = P.sb(st, "sqb", [128, 2, 512], BF16)
        recb = P.sb(st, "recb", [128, 512], F32)
        rsB = P.sb(st, "rsB", [128, 512], F32)
        obB = Ring(st, "obB", [128, 512], BF16, 3, dsem=True)
        bst = {"dfree": None, "sqfree": None, "recfree": None, "omapfree": None, "rsfree": None}
        qblocks = [(s, 512, list(range(NKC))) for s in range(0, T, 512)] + [(T, CTX, [NKC - 2, NKC - 1])]
        TC = T // 128

        def load_kv(slot, kmaps, vcol0, dv, vh):
            w = [kvfree[slot], jobcc[vh]]
            nm_ = len(kmaps)
            for j, mk in enumerate(kmaps):
                P.dma(SP, KT[slot][:, j, 0:4 * T].rearrange("p (r t) -> p r t", r=4),
                      K0mg[vh].ap().rearrange("(r m p) t -> p m r t", r=4, m=nm_)[:, j], kvds[slot], waits=w)
                P.dma(SP, KT[slot][:, j, 4 * T:4 * T + CTX], K0c[mk], kvds[slot])
            P.dma(SP, VT[slot][:, 0:4 * TC, 0:dv], V0hg[vh].ap().rearrange("(c p) w -> p c w", p=128), kvds[slot])
            P.dma(SP, VT[slot][:, 4 * TC:4 * TC + 2, 0:dv], V0c[:, vcol0:vcol0 + dv].rearrange("(c p) w -> p c w", p=128), kvds[slot])
            return (kvds[slot], kvds[slot].n)

        def load_q(mq):
            k = QT.nxt()
            tq = P.dma(SP, QT.t[k][:, :], Q0[mq], QT.ds[k], waits=[QT.free[k]])
            return k, tq

        jobs = [('A', kv) for kv in range(c.A_KV)] + [('B', hb) for hb in range(c.B_H)]

        def job_kv(ji):
            kind, idx = jobs[ji]
            if kind == 'A':
                return load_kv(ji % 2, [idx], idx * 128, 128, idx)
            return load_kv(ji % 2, [c.A_KV + 2 * idx, c.A_KV + 2 * idx + 1], c.A_KV * 128 + idx * 256, 256, c.A_KV + idx)

        pendB = []
        tkv_next = job_kv(0)
        for ji, (kind, idx) in enumerate(jobs):
            slot = ji % 2
            tkv = tkv_next
            if ji + 1 < len(jobs):
                tkv_next = job_kv(ji + 1)
            PE.wait(tkv)
            if kind == 'A':
                for hq in range(4 * idx, 4 * idx + 4):
                    kq_, tq = load_q(hq)
                    PE.wait(tq)
                    for (s0, nq, cl) in qblocks:
                        chunks = [(KT[slot][:, 0, ci * 128:(ci + 1) * 128], [VT[slot][:, ci, 0:128]], None) for ci in cl]
                        obk, sbk, tok = attn_unit(ac, QT.t[kq_][:, s0:s0 + nq], nq, chunks, 1)
                        epiA(obk, sbk, tok, nq, attT[hq, :, s0:s0 + nq])
                        run_pending(pendB)
                    QT.free[kq_] = (PE, PE.n)
            else:
                hb = idx
                qs = [load_q(c.A_H + 2 * hb + m) for m in range(2)]
                for (s0, nq, cl) in qblocks:
                    for m in range(2):
                        PE.wait(qs[m][1])
                        chunks = [(KT[slot][:, m, ci * 128:(ci + 1) * 128], [VT[slot][:, ci, 0:128], VT[slot][:, ci, 128:256]], None) for ci in cl]
                        obk, sbk, tok = attn_unit(ac, QT.t[qs[m][0]][:, s0:s0 + nq], nq, chunks, 2)
                        DVE.wait(tok, bst["recfree"])
                        tr = DVE.sig(nc.vector.reciprocal(out=recb[:, :nq], in_=banks[sbk][:, :nq]))
                        DVE.wait(tr, bst["omapfree"])
                        for cc_ in range(2):
                            to = DVE.sig(nc.vector.tensor_tensor(out=omap[:, m, cc_, :nq], in0=banks[obk[cc_]][:, :nq], in1=recb[:, :nq], op=ALU.mult))
                            P.bank_free[obk[cc_]] = to
                        bst["recfree"] = to
                        run_pending(pendB)
                        if m == 0:
                            P.bank_free[sbk] = tr
                            continue
                        DVE.wait(to, bst["dfree"])
                        for cc_ in range(2):
                            td = DVE.sig(nc.vector.scalar_tensor_tensor(out=dbuf[:, cc_, :nq], in0=omap[:, 1, cc_, :nq], scalar=lamb[:, 0:1],
                                                                        in1=omap[:, 0, cc_, :nq], op0=ALU.mult, op1=ALU.add))
                        bst["omapfree"] = td
                        ACT.wait(td, bst["sqfree"])
                        ts = ACT.sig(nc.scalar.activation(out=sqb[:, :, :nq], in_=dbuf[:, :, :nq], func=AF.Square))
                        bst["dfree"] = ts

                        def stage1(sbk=sbk, nq=nq, s0=s0, hb=hb, ts=ts, tr=tr):
                            yield
                            PE.wait(ts, tr)
                            nc.tensor.matmul(banks[sbk][:, :nq], ones[:, :], sqb[:, 0, :nq], start=True, stop=False)
                            tp = PE.sig(nc.tensor.matmul(banks[sbk][:, :nq], ones[:, :], sqb[:, 1, :nq], start=False, stop=True))
                            bst["sqfree"] = tp
                            ACT.wait(tp, bst["rsfree"])
                            ta = ACT.sig(nc.scalar.activation(out=rsB[:, :nq], in_=banks[sbk][:, :nq], func=AF.Sqrt, bias=epsb[:, 0:1], scale=1.0 / 256))
                            P.bank_free[sbk] = ta
                            DVE.wait(ta)
                            tr2 = DVE.sig(nc.vector.reciprocal(out=rsB[:, :nq], in_=rsB[:, :nq]))
                            for cc_ in range(2):
                                ko = obB.nxt()
                                DVE.wait(tr2, (obB.ds[ko], obB.ds[ko].n))
                                to2 = DVE.sig(nc.vector.scalar_tensor_tensor(out=obB.t[ko][:, :nq], in0=dbuf[:, cc_, :nq], scalar=Gq[:, 2 + cc_:3 + cc_],
                                                                             in1=rsB[:, :nq], op0=ALU.mult, op1=ALU.mult))
                                P.dma(SP, attT[c.A_H + 2 * hb + cc_, :, s0:s0 + nq], obB.t[ko][:, :nq], obB.ds[ko], waits=[to2])
                            bst["rsfree"] = to2
                            bst["dfree"] = [ts, to2]
                        pendB.append(stage1())
                        run_pending(pendB)
                for m in range(2):
                    QT.free[qs[m][0]] = (PE, PE.n)
            kvfree[slot] = (PE, PE.n)
        run_pending(pendB, final=True)
        P.barrier()

    if STOP_AFTER == 3:
        return finish()
    for sbi in range(2):
        wout(0, w_out0, sbi, c.SB[sbi])
    for sbi in range(2):
        mlp(0, sbi, c.SB[sbi])

    if STOP_AFTER == 4:
        return finish()
    OWN0 = 256
    BEL0 = 256 + T
    CTX0 = 512 + T
    for sbi in range(2):
        with ExitStack() as st:
            res = P.sb(st, "res", [128, KC, c.RW], BF16)
            alloc_w(st)
            norm_phase(1, 0, sbi, res)

            def q1_dst(m, s0, n, kind):
                return Q1[m, :, s0:s0 + n]

            def k1_dst(m, s0, n, kind):
                e0 = OWN0 + s0 if kind == 0 else CTX0 + s0 - T
                return K1[m - KC, :, e0:e0 + n]

            def v1_dst(tt, vcol0, wn):
                e0 = OWN0 + tt if tt < T else CTX0 + tt - T
                return [(V1[e0:e0 + 128, vcol0:vcol0 + wn], 0, wn)]
            eq = make_copy_epi(st, q1_dst)
            ek = make_copy_epi(st, k1_dst)
            ev = make_v_epi(st, v1_dst)

            def chunk_fn1(m):
                if m < KC:
                    return ('r', eq, True)
                if m < 2 * KC:
                    return ('r', ek)
                return ('v', (m - 2 * KC) * 128, ev)
            gemm(res, w_in1.ap(), 3 * D, sbi, chunk_fn1)

    if STOP_AFTER == 5:
        return finish()
    K1f = K1.ap().rearrange("h p n -> (h p) n")
    for p_ in range(NKP):
        rs_ = slice(p_ * HP * 128, (p_ + 1) * HP * 128)
        P.dma(SP, Kb[p_][:, 0:256], K1f[rs_, OWN0:OWN0 + 256], d_misc)
        P.dma(SP, Kb[p_][:, 256:512], K1f[rs_, T:T + 256], d_misc)
    for i_ in range(4):
        r0_ = (OWN0 if i_ < 2 else T) + (i_ % 2) * 128
        for j_ in range(NVP):
            P.dma(SP, Vb[i_][j_][:, :], V1[r0_:r0_ + 128, j_ * VPW:(j_ + 1) * VPW], d_misc)
    POOL.wait((d_misc, d_misc.n))
    for p_ in range(NKP):
        allgather(Kb[p_], Kbg[p_])
    for i_ in range(4):
        for j_ in range(NVP):
            allgather(Vb[i_][j_], Vbg[i_][j_])
    P.barrier()
    with ExitStack() as st:
        X = P.sb(st, "hx", [128, 4, KC * 256], BF16)
        Y = [P.sb(st, "hy%d" % i, [128, KC * 256], BF16) for i in range(2)]
        yds = [P.ds("hy", st) for _ in range(2)]
        xfree = None
        it = 0
        for (isv, side) in ((0, 0), (0, 1), (1, 0), (1, 1)):
            c0_ = 256 if side == 0 else 0
            tl_ = None
            for r in range(4):
                if isv == 0:
                    for p_ in range(NKP):
                        src = Kbg[p_][r * HP * 128:(r + 1) * HP * 128, c0_:c0_ + 256].rearrange("(h p) n -> p h n", p=128)
                        dstx = X[:, r, p_ * HP * 256:(p_ + 1) * HP * 256].rearrange("p (h n) -> p h n", n=256)
                        tl_ = P.dma(SP, dstx, src, d_misc, waits=[xfree])
                else:
                    for cl_ in range(2):
                        for j_ in range(NVP):
                            src = Vbg[(c0_ // 128) + cl_][j_][r * 128:(r + 1) * 128, :]
                            tl_ = P.dma(SP, X[:, r, cl_ * D + j_ * VPW:cl_ * D + (j_ + 1) * VPW], src, d_misc, waits=[xfree])
            W_ = KC * 256 if isv == 0 else 2 * D
            y = Y[it % 2]
            DVE.wait(tl_, (d_misc, d_misc.n), (yds[it % 2], yds[it % 2].n))
            ty = DVE.sig(nc.vector.tensor_scalar(out=y[:, :W_], in0=X[:, 0, :W_], scalar1=selb[:, 4 * side:4 * side + 1], scalar2=None, op0=ALU.mult))
            for r in range(1, 4):
                DVE.wait(ty)
                ty = DVE.sig(nc.vector.scalar_tensor_tensor(out=y[:, :W_], in0=X[:, r, :W_], scalar=selb[:, 4 * side + r:4 * side + r + 1],
                                                            in1=y[:, :W_], op0=ALU.mult, op1=ALU.add))
            xfree = ty
            e0 = 0 if side == 0 else BEL0
            if isv == 0:
                P.dma(SP, K1[:, :, e0:e0 + 256].rearrange("h p n -> p h n"), y[:, :W_].rearrange("p (h n) -> p h n", n=256), yds[it % 2], waits=[ty])
            else:
                P.dma(SP, V1[e0:e0 + 256, :].rearrange("(c p) w -> p c w", p=128), y[:, :W_].rearrange("p (c w) -> p c w", w=D), yds[it % 2], waits=[ty])
            it += 1
        P.barrier()

    if STOP_AFTER == 6:
        return finish()
    HG = 2
    NT1 = c.EXT // 128
    with ExitStack() as st:
        ac = AttnCtx(st, with_tab=True)
        epiA = make_attn_epi_A(st)
        KT1 = [P.sb(st, "K1T%d" % i, [128, HG, c.EXT], BF16) for i in range(2)]
        VT1 = [P.sb(st, "V1T%d" % i, [128, NT1, HG * 128], BF16) for i in range(2)]
        kvds = [P.ds("kv1", st) for _ in range(2)]
        kvfree = [None, None]
        QT = Ring(st, "Q1T", [128, T], BF16, 2, dsem=True)
        TB = Ring(st, "TB", [128, 3, 8, 512], F32, 2, dsem=True)

        def load_tb(h):
            k = TB.nxt()
            for ts__ in range(3):
                P.dma(SP, TB.t[k][:, ts__, :, :], tabs[ts__, h].rearrange("k p n -> p k n"), TB.ds[k], waits=[TB.free[k]])
            return k, (TB.ds[k], TB.ds[k].n)
        tb_next = load_tb(0)

        def load_kv1(hg):
            slot = hg % 2
            P.dma(SP, KT1[slot][:, :, :], K1[hg * HG:(hg + 1) * HG].rearrange("h p n -> p h n"), kvds[slot], waits=[kvfree[slot]])
            P.dma(SP, VT1[slot][:, :, :], V1[:, hg * HG * 128:(hg + 1) * HG * 128].rearrange("(c p) w -> p c w", p=128), kvds[slot])
            return (kvds[slot], kvds[slot].n)
        nhg = KC // HG
        tnext = load_kv1(0)
        for hg in range(nhg):
            slot = hg % 2
            tkv = tnext
            if hg + 1 < nhg:
                tnext = load_kv1(hg + 1)
            PE.wait(tkv)
            for hh in range(HG):
                h = hg * HG + hh
                ktb, ac.tabtok = tb_next
                if h + 1 < KC:
                    tb_next = load_tb(h + 1)
                kq_ = QT.nxt()
                tq = P.dma(SP, QT.t[kq_][:, :], Q1[h], QT.ds[kq_], waits=[QT.free[kq_]])
                PE.wait(tq)
                for g in range(c.G):
                    ts_ = 0 if g == 0 else (2 if g == c.G - 1 else 1)
                    chunks = []
                    for k in range(8):
                        e0 = g * 512 + k * 128
                        chunks.append((KT1[slot][:, hh, e0:e0 + 128], [VT1[slot][:, e0 // 128, hh * 128:(hh + 1) * 128]], TB.t[ktb][:, ts_, k, :]))
                    for j in range(2):
                        e0 = CTX0 + j * 128
                        chunks.append((KT1[slot][:, hh, e0:e0 + 128], [VT1[slot][:, e0 // 128, hh * 128:(hh + 1) * 128]], None))
                    obk, sbk, tok = attn_unit(ac, QT.t[kq_][:, g * 512:(g + 1) * 512], 512, chunks, 1)
                    epiA(obk, sbk, tok, 512, attT[h, :, g * 512:(g + 1) * 512])
                QT.free[kq_] = (PE, PE.n)
                TB.free[ktb] = (DVE, DVE.n)
            kvfree[slot] = (PE, PE.n)
        P.barrier()

    for sbi in range(2):
        lg = lat_groups(sbi)
        if lg:
            wout(1, w_out1, sbi, lg)
    for sbi in range(2):
        lg = lat_groups(sbi)
        if lg:
            mlp(1, sbi, lg)
    for sbi in range(2):
        lg = lat_groups(sbi)
        if lg:
            norm_phase(1, 0, sbi, None, final=True, groups=lg)
    P.barrier()
    P.es.close()
    return P


_CACHE = {}


def _rope_tables(cfg, q):
    T, TT = cfg.T, cfg.TT
    t = np.arange(T, dtype=np.int32) + q * T
    row = (t // GRID_W).astype(np.float32)
    col = (t % GRID_W).astype(np.float32)
    nf = HD // 4
    freqs = (np.float32(10000.0) ** (-np.arange(nf, dtype=np.float32) / np.float32(nf))).astype(np.float32)
    cosT = np.ones((128, TT), np.float32)
    sinT = np.zeros((128, TT), np.float32)
    for axis, pos in enumerate((row, col)):
        ang = (pos[None, :] * freqs[:, None]).astype(np.float32)
        cs, sn = np.cos(ang).astype(np.float32), np.sin(ang).astype(np.float32)
        for pair in range(2):
            d0 = axis * 64 + pair * 32
            cosT[d0:d0 + 32, :T] = cs
            sinT[d0:d0 + 32, :T] = -sn if pair == 0 else sn
    return cosT, sinT


def _na_tables(cfg, rel_bias):
    ROWS = cfg.ROWS
    out = []
    for qr0 in (0, 8, ROWS - 8):
        w = np.arange(16)
        krow = qr0 - 4 + w
        j = np.arange(8)
        qrow = qr0 + j
        r0q = np.clip(qrow - 4, 0, ROWS - 8)
        vrow = (krow[:, None] >= 0) & (krow[:, None] < ROWS) & (krow[:, None] >= r0q[None, :]) & (krow[:, None] < r0q[None, :] + 8)
        kcol = np.arange(GRID_W)
        qcol = np.arange(GRID_W)
        c0 = np.clip(qcol - 8, 0, GRID_W - 16)
        vcol = (kcol[:, None] >= c0[None, :]) & (kcol[:, None] < c0[None, :] + 16)
        dr = np.clip(krow[:, None] - qrow[None, :] + 7, 0, 14)
        dc = np.clip(kcol[:, None] - qcol[None, :] + 15, 0, 30)
        tab = rel_bias[:, dr[:, None, :, None], dc[None, :, None, :]]
        valid = vrow[:, None, :, None] & vcol[None, :, None, :]
        tab = np.where(valid[None], tab, np.float32(NEG)).astype(np.float32)
        out.append(tab.reshape(rel_bias.shape[0], 8, 128, 512))
    return out


def kernel(x, c, ctx, c_ctx, ada_w, ada_b, norm1_g, norm2_g, w_in_even, w_out_even,
           a_q_norm, a_k_norm, b_lambda_q1, b_lambda_k1, b_lambda_q2, b_lambda_k2, b_subln_g,
           w_in_odd, w_out_odd, na_rel_bias, mlp_w1, mlp_w2, final_g, _cfg=None):
    f = lambda a: np.ascontiguousarray(np.asarray(a, dtype=np.float32))
    x, c, ctx, c_ctx = f(x), f(c), f(ctx), f(c_ctx)
    B, S, D = x.shape
    cfg = _cfg or Cfg(D, S)
    KC, T, TT, MJ = cfg.KC, cfg.T, cfg.TT, cfg.MJ
    key = (D, S)
    if key not in _CACHE:
        _CACHE[key] = build(cfg)
    P = _CACHE[key]
    ada_w, ada_b = f(ada_w), f(ada_b)
    tabs3 = _na_tables(cfg, f(na_rel_bias)[0])
    perm = np.zeros((128, 128), np.float32)
    perm[np.arange(128) ^ 32, np.arange(128)] = 1.0
    smalls = np.stack([f(a_q_norm)[0], f(a_k_norm)[0], f(b_lambda_q1)[0], f(b_lambda_k1)[0], f(b_lambda_q2)[0], f(b_lambda_k2)[0],
                       f(b_subln_g)[0][:128], f(b_subln_g)[0][128:]], axis=1)
    shared = {
        "gn1": f(f(norm1_g).reshape(2, KC, 128).transpose(2, 0, 1)),
        "gn2": f(f(norm2_g).reshape(2, KC, 128).transpose(2, 0, 1)),
        "gfin": f(f(final_g).reshape(KC, 128).T),
        "w_in0": f(w_in_even)[0], "w_out0": f(w_out_even)[0], "w_in1": f(w_in_odd)[0], "w_out1": f(w_out_odd)[0],
        "w1": f(mlp_w1), "w2": f(mlp_w2), "smalls": f(smalls), "perm": perm,
    }
    in_maps = []
    for r in range(8):
        b, q = r // 4, r % 4
        xt = np.concatenate([x[b, q * T:(q + 1) * T], ctx[b]], axis=0)
        cosT, sinT = _rope_tables(cfg, q)
        sel = np.zeros((128, 8), np.float32)
        sel[:, (q - 1) % 4] = 1.0
        sel[:, 4 + (q + 1) % 4] = 1.0
        m = dict(shared)
        m.update({
            "xT": f(xt.T.reshape(KC, 128, TT)),
            "cvT": f(np.stack([c[b], c_ctx]).reshape(2, KC, 128).transpose(2, 1, 0)),
            "adaw": f(ada_w[:, :, q * MJ * 128:(q + 1) * MJ * 128]),
            "adab": f(ada_b[:, q * MJ * 128:(q + 1) * MJ * 128].reshape(2, MJ, 128).transpose(2, 0, 1)),
            "cosT": cosT, "sinT": sinT,
            "tabs": f(np.stack([tabs3[0] if q == 0 else tabs3[1], tabs3[1], tabs3[2] if q == 3 else tabs3[1]])),
            "sel": sel,
        })
        in_maps.append(m)
    res = run_bass_kernel_spmd(P.nc, in_maps, core_ids=list(range(8)))
    out = np.empty((B, S, D), np.float32)
    for r in range(8):
        b, q = r // 4, r % 4
        out[b, q * T:(q + 1) * T] = res.results[r]["outT"].reshape(D, T).T
    kernel.last = res
    return out
```

```python
import math
from contextlib import ExitStack
import numpy as np
import concourse.bass as bass
import concourse.mybir as mybir
from concourse.bass_utils import run_bass_kernel_spmd

F32 = mybir.dt.float32
BF16 = mybir.dt.bfloat16
AF = mybir.ActivationFunctionType
ALU = mybir.AluOpType
EPS = 1e-6
GRID_W = 64
CTX = 256
HD = 128
NEG = -30000.0
STOP_AFTER = 99


class Cfg:
    def __init__(s, D=4096, S=8192):
        s.D, s.S = D, S
        s.KC = D // 128
        s.T = S // 4
        s.TT = s.T + CTX
        s.R = s.T // GRID_W
        s.G = s.R // 8
        s.ROWS = S // GRID_W
        s.A_H = D // 256
        s.A_KV = s.A_H // 4
        s.B_H = D // 512
        s.H = 4 * D
        s.NM_K = s.A_KV + 2 * s.B_H
        s.NM_Q = s.A_H + 2 * s.B_H
        s.VW = s.A_KV * 128 + s.B_H * 256
        s.NKC = (4 * s.T + CTX) // 128
        s.EXT = (8 + s.R) * GRID_W + CTX
        s.MJ = 6 * s.KC // 4
        if s.T == 2048:
            s.SB = [[(0, 384, 0), (384, 384, 0), (768, 384, 0)], [(1152, 512, 0), (1664, 384, 0), (2048, 256, 1)]]
        elif s.T == 1024:
            s.SB = [[(0, 512, 0), (512, 512, 0)], [(1024, 256, 1)]]
        else:
            raise ValueError
        s.RW = max(sum(n for _, n, _ in sb) for sb in s.SB)


class DS:
    def __init__(s, sem):
        s.sem, s.n, s.waited = sem, 0, 0


class Eng:
    def __init__(s, P, name, eng):
        s.P, s.name, s.eng = P, name, eng
        s.sem = P.new_sem("e_" + name)
        s.n = 0
        s.seen = {}

    def wait(s, *toks):
        for t in toks:
            if t is None:
                continue
            if isinstance(t, list):
                s.wait(*t)
                continue
            src, v = t
            if v <= 0 or s.seen.get(id(src), 0) >= v:
                continue
            s.eng.wait_ge(src.sem, v)
            s.seen[id(src)] = v
            if isinstance(src, DS):
                src.waited = max(src.waited, v)

    def sig(s, ins):
        s.n += 1
        ins.then_inc(s.sem, 1)
        return (s, s.n)


class Prog:
    def __init__(s, cfg):
        s.cfg = cfg
        s.nc = bass.Bass("TRN2", target_bir_lowering=False)
        s.es = ExitStack()
        s.nsem = 0
        s.dsems = []
        s.pool_sync = True

    def new_sem(s, name):
        s.nsem += 1
        return s.es.enter_context(s.nc.semaphore(name + "_%d" % s.nsem))

    def ds(s, name="d", st=None):
        if not hasattr(s, "dfree"):
            s.dfree = []
        if s.dfree:
            d = s.dfree.pop()
        else:
            d = DS(s.new_sem(name))
            s.dsems.append(d)
        if st is not None:
            st.callback(lambda d=d: s.dfree.append(d))
        return d

    def start(s):
        nc = s.nc
        s.block = s.es.enter_context(nc.Block())
        s.PE = Eng(s, "pe", nc.tensor)
        s.ACT = Eng(s, "act", nc.scalar)
        s.DVE = Eng(s, "dve", nc.vector)
        s.POOL = Eng(s, "pool", nc.gpsimd)
        s.SP = Eng(s, "sp", nc.sync)
        s.engs = [s.PE, s.ACT, s.DVE, s.POOL, s.SP]
        s.banks = [s.es.enter_context(nc.psum_tensor("bank%d" % i, [128, 512], F32)) for i in range(8)]
        s.bank_free = [None] * 8

    def dma(s, q, out, in_, ds, waits=()):
        q.wait(*waits)
        if ds.waited > 0:
            q.wait((ds, ds.waited))
        ins = q.eng.dma_start(out=out, in_=in_)
        ds.n += 16
        ins.then_inc(ds.sem, 16)
        return (ds, ds.n)

    def barrier(s, full=False):
        toks = [(e, e.n) for e in s.engs] + [(d, d.n) for d in s.dsems]
        for e in s.engs:
            if e is s.POOL and not (full or s.pool_sync):
                continue
            e.wait(*toks)
        s.bank_free = [None] * 8

    def sb(s, st, name, shape, dt):
        s.nsb = getattr(s, "nsb", 0) + 1
        return st.enter_context(s.nc.sbuf_tensor("%s_u%d" % (name, s.nsb), shape, dt))


def build(cfg, debug=False):
    P = Prog(cfg)
    nc = P.nc
    c = cfg
    D, KC, T, TT, H = c.D, c.KC, c.T, c.TT, c.H

    def din(name, shape, dt=F32):
        return nc.dram_tensor(name, list(shape), dt, kind="ExternalInput")

    def dscr(name, shape, dt, out=False):
        return nc.dram_tensor(name, list(shape), dt, kind=("ExternalOutput" if (out and debug) else "Internal"))

    xT = din("xT", [KC, 128, TT])
    cvT = din("cvT", [128, KC, 2])
    adaw = din("adaw", [2, D, c.MJ * 128])
    adab = din("adab", [128, 2, c.MJ])
    gn1 = din("gn1", [128, 2, KC])
    gn2 = din("gn2", [128, 2, KC])
    gfin = din("gfin", [128, KC])
    w_in0 = din("w_in0", [D, 9 * D // 4])
    w_out0 = din("w_out0", [D, D])
    w_in1 = din("w_in1", [D, 3 * D])
    w_out1 = din("w_out1", [D, D])
    w1 = din("w1", [2, D, H])
    w2 = din("w2", [2, H, D])
    smalls = din("smalls", [128, 8])
    cosT = din("cosT", [128, TT])
    sinT = din("sinT", [128, TT])
    permI = din("perm", [128, 128])
    tabs = din("tabs", [3, KC, 8, 128, 512])
    sel = din("sel", [128, 8])
    outT = nc.dram_tensor("outT", [KC, 128, T], F32, kind="ExternalOutput")

    hT = dscr("hT", [KC, 128, TT], F32, out=True)
    mod_src = dscr("mod_src", [128, 2 * c.MJ * 2], F32)
    mod_g = dscr("mod_g", [4 * 128, 2 * c.MJ * 2], F32)
    Q0 = dscr("Q0", [c.NM_Q, 128, TT], BF16)
    NKG = c.A_KV + c.B_H
    KGR = [128] * c.A_KV + [256] * c.B_H
    K0m = [dscr("K0m%d" % i, [KGR[i], T], BF16) for i in range(NKG)]
    K0mg = [dscr("K0mg%d" % i, [4 * KGR[i], T], BF16) for i in range(NKG)]
    K0c = dscr("K0c", [c.NM_K, 128, CTX], BF16)
    NVH = c.A_KV + c.B_H
    DV = [128] * c.A_KV + [256] * c.B_H
    V0h = [dscr("V0h%d" % i, [T, DV[i]], BF16) for i in range(NVH)]
    V0hg = [dscr("V0hg%d" % i, [4 * T, DV[i]], BF16) for i in range(NVH)]
    V0c = dscr("V0c", [CTX, c.VW], BF16)
    attT = dscr("attT", [KC, 128, TT], BF16)
    actT = dscr("actT", [4 * KC, 128, TT], BF16)
    Q1 = dscr("Q1", [KC, 128, T], BF16)
    K1 = dscr("K1", [KC, 128, c.EXT], BF16)
    V1 = dscr("V1", [c.EXT, D], BF16)
    HP = min(8, KC)
    NKP = KC // HP
    Kb = [dscr("Kb%d" % i, [HP * 128, 512], BF16) for i in range(NKP)]
    Kbg = [dscr("Kbg%d" % i, [4 * HP * 128, 512], BF16) for i in range(NKP)]
    VPW = min(D, 4096)
    NVP = D // VPW
    Vb = [[dscr("Vb%d_%d" % (i, j), [128, VPW], BF16) for j in range(NVP)] for i in range(4)]
    Vbg = [[dscr("Vbg%d_%d" % (i, j), [4 * 128, VPW], BF16) for j in range(NVP)] for i in range(4)]

    P.start()
    PE, ACT, DVE, POOL, SP = P.PE, P.ACT, P.DVE, P.POOL, P.SP
    banks = P.banks
    G4 = [[0, 1, 2, 3], [4, 5, 6, 7]]

    gs = P.es
    ones = P.sb(gs, "ones", [128, 128], BF16)
    perm = P.sb(gs, "permb", [128, 128], BF16)
    epsb = P.sb(gs, "epsb", [128, 1], F32)
    smb = P.sb(gs, "smb", [128, 8], F32)
    selb = P.sb(gs, "selb", [128, 8], F32)
    modf = P.sb(gs, "modf", [128, 2, 6 * KC, 2], F32)
    Am = P.sb(gs, "Am", [128, 2, 2, 2, KC], F32)
    g1b = P.sb(gs, "g1b", [128, 2, KC], F32)
    g2b = P.sb(gs, "g2b", [128, 2, KC], F32)
    gfb = P.sb(gs, "gfb", [128, KC], F32)
    Gq = P.sb(gs, "Gq", [128, 4], F32)
    lamb = P.sb(gs, "lamb", [128, 2], F32)
    cosb = P.sb(gs, "cosb", [128, TT], BF16)
    sinb = P.sb(gs, "sinb", [128, TT], BF16)
    NW = 4
    wring = [None] * NW
    wds = [P.ds("w") for _ in range(NW)]
    wfree = [None] * NW
    wcnt = [0]

    def alloc_w(st):
        P.barrier(full=True)
        wring[:] = [P.sb(st, "wr%d_%d" % (i, P.nsem), [128, KC, 256], BF16) for i in range(NW)]
        for i in range(NW):
            wfree[i] = None
        P.pool_sync = False

        def _restore():
            P.pool_sync = True
            P.barrier(full=True)
        st.callback(_restore)
    d_misc = P.ds("misc")
    d_pm = P.ds("pmisc")
    d_cc = P.ds("cc")

    def modv(i, part, v):
        return modf[:, i, part * KC:(part + 1) * KC, v]

    with ExitStack() as st:
        cvf = P.sb(st, "cvf", [128, KC, 2], F32)
        cvb = P.sb(st, "cvb", [128, KC, 2], BF16)
        adabb = P.sb(st, "adabb", [128, 2, c.MJ], F32)
        modl = P.sb(st, "modl", [128, 2, c.MJ, 2], F32)
        lt = P.sb(st, "lt", [128, 4], F32)
        alloc_w(st)
        t_in = [P.dma(SP, cvf[:, :, :], cvT[:, :, :], d_misc),
                P.dma(SP, adabb[:, :, :], adab[:, :, :], d_misc),
                P.dma(SP, smb[:, :], smalls[:, :], d_misc),
                P.dma(SP, selb[:, :], sel[:, :], d_misc),
                P.dma(SP, g1b[:, :, :], gn1[:, :, :], d_misc),
                P.dma(SP, g2b[:, :, :], gn2[:, :, :], d_misc),
                P.dma(SP, gfb[:, :], gfin[:, :], d_misc),
                P.dma(POOL, perm[:, :], permI[:, :], d_pm),
                P.dma(POOL, cosb[:, :], cosT[:, :], d_pm),
                P.dma(POOL, sinb[:, :], sinT[:, :], d_pm)]
        t_ones = DVE.sig(nc.vector.memset(ones[:, :], 1.0))
        t_eps = DVE.sig(nc.vector.memset(epsb[:, :], EPS))
        ACT.wait(t_in[6], t_in[-1])
        t_cv = ACT.sig(nc.scalar.activation(out=cvb[:, :, :], in_=cvf[:, :, :], func=AF.Silu))
        psm = banks[0]
        for i in range(2):
            for ct in range(c.MJ // 2):
                slot = wcnt[0] % NW
                wcnt[0] += 1
                src = adaw[i].rearrange("(kc p) n -> p kc n", p=128)[:, :, ct * 256:(ct + 1) * 256]
                tw = P.dma(POOL, wring[slot][:, :, :], src, wds[slot], waits=[wfree[slot]])
                PE.wait(tw, t_cv)
                for mc in range(2):
                    j = ct * 2 + mc
                    col = (i * c.MJ + j) * 2
                    for kc in range(KC):
                        mm = nc.tensor.matmul(psm[:, col:col + 2], wring[slot][:, kc, mc * 128:(mc + 1) * 128],
                                              cvb[:, kc, :], start=(kc == 0), stop=(kc == KC - 1))
                wfree[slot] = PE.sig(mm)
        DVE.wait((PE, PE.n), t_in[6], t_in[-1])
        for i in range(2):
            for v in range(2):
                tm = DVE.sig(nc.vector.tensor_tensor(
                    out=modl[:, i, :, v], in0=psm[:, i * c.MJ * 2:(i + 1) * c.MJ * 2].rearrange("p (j v) -> p j v", v=2)[:, :, v],
                    in1=adabb[:, i, :], op=ALU.add))
        t1 = P.dma(SP, mod_src[:, :], modl[:, :, :, :].rearrange("p i j v -> p (i j v)"), d_misc, waits=[tm])
        POOL.wait(t1)
        cc = nc.gpsimd.collective_compute("AllGather", ALU.bypass, replica_groups=G4,
                                          ins=[mod_src.ap().opt()], outs=[mod_g.ap().opt()])
        d_cc.n += 1
        cc.then_inc(d_cc.sem, 1)
        tcc = (d_cc, d_cc.n)
        tl = None
        for r in range(4):
            for i in range(2):
                tl = P.dma(SP, modf[:, i, r * c.MJ:(r + 1) * c.MJ, :],
                           mod_g[r * 128:(r + 1) * 128, :].rearrange("p (i j v) -> p i j v", i=2, v=2)[:, i, :, :],
                           d_misc, waits=[tcc])
        DVE.wait(tl)
        for i in range(2):
            for sub, (part, gb) in enumerate(((1, g1b), (4, g2b))):
                for v in range(2):
                    ta = DVE.sig(nc.vector.scalar_tensor_tensor(out=Am[:, i, sub, v, :], in0=modv(i, part, v), scalar=1.0,
                                                                in1=gb[:, i, :], op0=ALU.add, op1=ALU.mult))
        DVE.sig(nc.vector.tensor_copy(out=Gq[:, 0:2], in_=smb[:, 0:2]))
        DVE.sig(nc.vector.tensor_scalar(out=Gq[:, 2:4], in0=smb[:, 6:8], scalar1=0.8, scalar2=None, op0=ALU.mult))
        onesf = P.sb(st, "onesf", [128, 128], F32)
        tof = DVE.sig(nc.vector.memset(onesf[:, :], 1.0))
        DVE.sig(nc.vector.tensor_tensor(out=lt[:, 0:1], in0=smb[:, 2:3], in1=smb[:, 3:4], op=ALU.mult))
        tl2 = DVE.sig(nc.vector.tensor_tensor(out=lt[:, 1:2], in0=smb[:, 4:5], in1=smb[:, 5:6], op=ALU.mult))
        PE.wait(tl2)
        tp = PE.sig(nc.tensor.matmul(banks[1][:, 0:2], onesf[:, :], lt[:, 0:2], start=True, stop=True))
        ACT.wait(tp)
        te = ACT.sig(nc.scalar.activation(out=lt[:, 2:4], in_=banks[1][:, 0:2], func=AF.Exp))
        DVE.wait(te)
        DVE.sig(nc.vector.tensor_tensor(out=lamb[:, 0:1], in0=lt[:, 3:4], in1=lt[:, 2:3], op=ALU.subtract))
        DVE.wait((DVE, DVE.n))
        DVE.sig(nc.vector.tensor_scalar(out=lamb[:, 0:1], in0=lamb[:, 0:1], scalar1=-0.2, scalar2=None, op0=ALU.add))
        for kc in range(KC):
            P.dma(SP, hT[kc], xT[kc], d_misc)
        P.barrier()

    def w_tile(Wap, c0):
        slot = wcnt[0] % NW
        wcnt[0] += 1
        src = Wap.rearrange("(kc p) n -> p kc n", p=128)[:, :, c0:c0 + 256]
        tok = P.dma(POOL, wring[slot][:, :, :], src, wds[slot], waits=[wfree[slot]])
        return slot, tok

    bank_rr = [0]

    def next_bank(lo, n):
        b = lo + (bank_rr[0] % n)
        bank_rr[0] += 1
        return b

    def run_pending(pend, final=False):
        keep = []
        for g in pend:
            try:
                next(g)
                keep.append(g)
            except StopIteration:
                pass
        pend[:] = keep
        if final:
            while pend:
                run_pending(pend)

    def norm_phase(layer, sub, sbi, res, final=False, groups=None):
        groups = c.SB[sbi] if groups is None else groups
        if not groups:
            return
        t0 = c.SB[sbi][0][0]
        with ExitStack() as st:
            stg = [P.sb(st, "nst%d" % i, [128, 512], F32) for i in range(3)]
            sgd = [P.ds("nst", st) for _ in range(3)]
            sgf = [None] * 3
            sq = [P.sb(st, "nsq%d" % i, [128, 512], BF16) for i in range(2)]
            sqf = [None] * 2
            rstd = P.sb(st, "rstd", [128, c.RW], F32)
            tmp = [P.sb(st, "ntmp%d" % i, [128, 512], F32) for i in range(2)]
            tmpf = [None] * 2
            od = [P.ds("no", st) for _ in range(2)]
            u = 0
            for kc in range(KC):
                for gi, (s0, n, kind) in enumerate(groups):
                    sl = u % 3
                    td = P.dma(SP, stg[sl][:, :n], hT[kc, :, s0:s0 + n], sgd[sl], waits=[sgf[sl]])
                    ACT.wait(td, sqf[u % 2])
                    ta = ACT.sig(nc.scalar.activation(out=sq[u % 2][:, :n], in_=stg[sl][:, :n], func=AF.Square))
                    sgf[sl] = ta
                    PE.wait(ta)
                    mm = nc.tensor.matmul(banks[gi][:, :n], ones[:, :], sq[u % 2][:, :n], start=(kc == 0), stop=(kc == KC - 1))
                    sqf[u % 2] = PE.sig(mm)
                    u += 1
            for gi, (s0, n, kind) in enumerate(groups):
                ACT.wait((PE, PE.n))
                ta = ACT.sig(nc.scalar.activation(out=rstd[:, s0 - t0:s0 - t0 + n], in_=banks[gi][:, :n], func=AF.Sqrt,
                                                  bias=epsb[:, 0:1], scale=1.0 / D))
                DVE.wait(ta)
                tr = DVE.sig(nc.vector.reciprocal(out=rstd[:, s0 - t0:s0 - t0 + n], in_=rstd[:, s0 - t0:s0 - t0 + n]))
            DVE.wait(tr)
            for kc in range(KC):
                for gi, (s0, n, kind) in enumerate(groups):
                    sl = u % 3
                    td = P.dma(SP, stg[sl][:, :n], hT[kc, :, s0:s0 + n], sgd[sl], waits=[sgf[sl]])
                    DVE.wait(td, tmpf[u % 2])
                    tv = DVE.sig(nc.vector.tensor_tensor(out=tmp[u % 2][:, :n], in0=stg[sl][:, :n],
                                                         in1=rstd[:, s0 - t0:s0 - t0 + n], op=ALU.mult))
                    sgf[sl] = tv
                    ACT.wait(tv)
                    if final:
                        ta = ACT.sig(nc.scalar.activation(out=tmp[u % 2][:, :n], in_=tmp[u % 2][:, :n], func=AF.Identity,
                                                          scale=gfb[:, kc:kc + 1]))
                        P.dma(SP, outT[kc, :, s0:s0 + n], tmp[u % 2][:, :n], od[u % 2], waits=[ta])
                        tmpf[u % 2] = (od[u % 2], od[u % 2].n)
                    else:
                        ta = ACT.sig(nc.scalar.activation(out=res[:, kc, s0 - t0:s0 - t0 + n], in_=tmp[u % 2][:, :n],
                                                          func=AF.Identity, scale=Am[:, layer, sub, kind, kc:kc + 1],
                                                          bias=modf[:, layer, (0 if sub == 0 else 3) * KC + kc, kind:kind + 1]))
                        tmpf[u % 2] = ta
                    u += 1
            P.barrier()

    def load_res(res, src, sbi, kq=0, groups=None):
        groups = c.SB[sbi] if groups is None else groups
        t0, t1 = groups[0][0], groups[-1][0] + groups[-1][1]
        tk = None
        for kc in range(KC):
            tk = P.dma(SP, res[:, kc, 0:t1 - t0], src[kq * KC + kc, :, t0:t1], d_misc)
        PE.wait((d_misc, d_misc.n))

    def gemm(res, Wap, ncols, sbi, chunk_fn, groups=None, nb=4):
        groups = c.SB[sbi] if groups is None else groups
        t0 = c.SB[sbi][0][0]
        pend = []
        nt = ncols // 256
        tiles = []
        for ct in range(nt):
            while len(tiles) < min(nt, ct + NW):
                tiles.append(w_tile(Wap, len(tiles) * 256))
            slot, tw = tiles[ct]
            PE.wait(tw)
            kinds = [chunk_fn(ct * 2 + mc) for mc in range(2)]
            last = None
            if kinds[0] is not None and kinds[0][0] == 'v' and kinds[1] is not None and kinds[1][0] == 'v':
                vlist = [(0, 256, kinds[0])]
            else:
                vlist = [(mc * 128, 128, kinds[mc]) for mc in range(2) if kinds[mc] is not None and kinds[mc][0] == 'v']
            for (wc0, wn, kd) in vlist:
                tb0, tb1 = groups[0][0], groups[-1][0] + groups[-1][1]
                for tt in range(tb0, tb1, 128):
                    b = next_bank(0, nb)
                    PE.wait(P.bank_free[b])
                    for kc in range(KC):
                        mm = nc.tensor.matmul(banks[b][:, :wn], res[:, kc, tt - t0:tt - t0 + 128],
                                              wring[slot][:, kc, wc0:wc0 + wn], start=(kc == 0), stop=(kc == KC - 1))
                    last = PE.sig(mm)
                    pend.append(kd[2](kd[1], wn, tt, b, last))
                    run_pending(pend)
            for mc in range(2):
                kd = kinds[mc]
                if kd is None or kd[0] != 'r':
                    continue
                for gi, grp in enumerate(groups):
                    s0, n, kind = grp
                    if len(kd) > 2 and kd[2] and kind == 1:
                        continue
                    b = next_bank(0, nb)
                    PE.wait(P.bank_free[b])
                    for kc in range(KC):
                        mm = nc.tensor.matmul(banks[b][:, :n], wring[slot][:, kc, mc * 128:(mc + 1) * 128],
                                              res[:, kc, s0 - t0:s0 - t0 + n], start=(kc == 0), stop=(kc == KC - 1))
                    last = PE.sig(mm)
                    pend.append(kd[1](ct * 2 + mc, gi, grp, b, last))
                    run_pending(pend)
            wfree[slot] = last if last is not None else (PE, PE.n)
        run_pending(pend, final=True)
        P.barrier()

    class Ring:
        def __init__(s, st, name, shape, dt, n, dsem=False):
            s.t = [P.sb(st, "%s%d" % (name, i), shape, dt) for i in range(n)]
            s.free = [None] * n
            s.ds = [P.ds(name, st) for _ in range(n)] if dsem else None
            s.i = 0
            s.n = n

        def nxt(s):
            k = s.i % s.n
            s.i += 1
            return k

    def make_qk_epi(st):
        sq = Ring(st, "esq", [128, 512], BF16, 2)
        rs = Ring(st, "ers", [128, 512], F32, 2)
        qn = Ring(st, "eqn", [128, 512], BF16, 3)
        t1 = Ring(st, "et1", [128, 512], F32, 2)
        t2 = Ring(st, "et2", [128, 512], F32, 2)
        ob = Ring(st, "eob", [128, 512], BF16, 2, dsem=True)

        def epi(m, gi, grp, b, tok, normed, gcol, dstap):
            s0, n, kind = grp
            kq = qn.nxt()
            if normed:
                k1 = sq.nxt()
                ACT.wait(tok, sq.free[k1])
                ta = ACT.sig(nc.scalar.activation(out=sq.t[k1][:, :n], in_=banks[b][:, :n], func=AF.Square))
                yield
                bs = next_bank(4, 2)
                PE.wait(ta, P.bank_free[bs])
                tp = PE.sig(nc.tensor.matmul(banks[bs][:, :n], ones[:, :], sq.t[k1][:, :n], start=True, stop=True))
                sq.free[k1] = tp
                k2 = rs.nxt()
                ACT.wait(tp, rs.free[k2])
                ta2 = ACT.sig(nc.scalar.activation(out=rs.t[k2][:, :n], in_=banks[bs][:, :n], func=AF.Sqrt,
                                                   bias=epsb[:, 0:1], scale=1.0 / 128))
                P.bank_free[bs] = ta2
                DVE.wait(ta2)
                tr = DVE.sig(nc.vector.reciprocal(out=rs.t[k2][:, :n], in_=rs.t[k2][:, :n]))
                DVE.wait(tr, qn.free[kq])
                tq = DVE.sig(nc.vector.scalar_tensor_tensor(out=qn.t[kq][:, :n], in0=banks[b][:, :n], scalar=Gq[:, gcol:gcol + 1],
                                                            in1=rs.t[k2][:, :n], op0=ALU.mult, op1=ALU.mult))
                rs.free[k2] = tq
            else:
                ACT.wait(tok, qn.free[kq])
                tq = ACT.sig(nc.scalar.activation(out=qn.t[kq][:, :n], in_=banks[b][:, :n], func=AF.Copy))
            P.bank_free[b] = tq
            yield
            bw = next_bank(6, 2)
            PE.wait(tq, P.bank_free[bw])
            tp2 = PE.sig(nc.tensor.matmul(banks[bw][:, :n], perm[:, :], qn.t[kq][:, :n], start=True, stop=True))
            ka, kb, ko = t1.nxt(), t2.nxt(), ob.nxt()
            DVE.wait(tq, t1.free[ka])
            ta_ = DVE.sig(nc.vector.tensor_tensor(out=t1.t[ka][:, :n], in0=qn.t[kq][:, :n], in1=cosb[:, s0:s0 + n], op=ALU.mult))
            DVE.wait(tp2, t2.free[kb])
            tb_ = DVE.sig(nc.vector.tensor_tensor(out=t2.t[kb][:, :n], in0=banks[bw][:, :n], in1=sinb[:, s0:s0 + n], op=ALU.mult))
            P.bank_free[bw] = tb_
            qn.free[kq] = [tp2, ta_]
            DVE.wait(tb_, (ob.ds[ko], ob.ds[ko].n))
            to = DVE.sig(nc.vector.tensor_tensor(out=ob.t[ko][:, :n], in0=t1.t[ka][:, :n], in1=t2.t[kb][:, :n], op=ALU.add))
            t1.free[ka] = to
            t2.free[kb] = to
            P.dma(SP, dstap, ob.t[ko][:, :n], ob.ds[ko], waits=[to])
        return epi

    def make_copy_epi(st, dst_fn):
        ob = Ring(st, "cob", [128, 512], BF16, 3, dsem=True)

        def epi(m, gi, grp, b, tok):
            s0, n, kind = grp
            ko = ob.nxt()
            eng = ACT if (ob.i % 2 == 0) else DVE
            eng.wait(tok, (ob.ds[ko], ob.ds[ko].n))
            if eng is ACT:
                tq = ACT.sig(nc.scalar.activation(out=ob.t[ko][:, :n], in_=banks[b][:, :n], func=AF.Copy))
            else:
                tq = DVE.sig(nc.vector.tensor_copy(out=ob.t[ko][:, :n], in_=banks[b][:, :n]))
            P.bank_free[b] = tq
            P.dma(SP, dst_fn(m, s0, n, kind), ob.t[ko][:, :n], ob.ds[ko], waits=[tq])
            return
            yield
        return epi

    def make_v_epi(st, dst_fn):
        ob = Ring(st, "vob", [128, 256], BF16, 3, dsem=True)

        def epi(vcol0, wn, tt, b, tok):
            ko = ob.nxt()
            eng = ACT if (ob.i % 2 == 0) else DVE
            eng.wait(tok, (ob.ds[ko], ob.ds[ko].n))
            if eng is ACT:
                tq = ACT.sig(nc.scalar.activation(out=ob.t[ko][:, :wn], in_=banks[b][:, :wn], func=AF.Copy))
            else:
                tq = DVE.sig(nc.vector.tensor_copy(out=ob.t[ko][:, :wn], in_=banks[b][:, :wn]))
            P.bank_free[b] = tq
            for (dst, a_, b_) in dst_fn(tt, vcol0, wn):
                P.dma(SP, dst, ob.t[ko][:, a_:b_], ob.ds[ko], waits=[tq])
            return
            yield
        return epi

    def make_resid_epi(st, layer, gate_part, kq_off=0):
        hin = Ring(st, "hin", [128, 512], F32, 3, dsem=True)
        hout = Ring(st, "hout", [128, 512], F32, 3, dsem=True)

        def epi(m, gi, grp, b, tok):
            s0, n, kind = grp
            ki, ko = hin.nxt(), hout.nxt()
            td = P.dma(SP, hin.t[ki][:, :n], hT[m, :, s0:s0 + n], hin.ds[ki], waits=[hin.free[ki]])
            DVE.wait(tok, td, (hout.ds[ko], hout.ds[ko].n))
            to = DVE.sig(nc.vector.scalar_tensor_tensor(out=hout.t[ko][:, :n], in0=banks[b][:, :n],
                                                        scalar=modf[:, layer, gate_part * KC + m, kind:kind + 1],
                                                        in1=hin.t[ki][:, :n], op0=ALU.mult, op1=ALU.add))
            hin.free[ki] = to
            P.bank_free[b] = to
            P.dma(SP, hT[m, :, s0:s0 + n], hout.t[ko][:, :n], hout.ds[ko], waits=[to])
            return
            yield
        return epi

    def make_relu2_epi(st):
        rb = Ring(st, "rb", [128, 512], F32, 3)
        ob = Ring(st, "rob", [128, 512], BF16, 3, dsem=True)

        def epi(m, gi, grp, b, tok):
            s0, n, kind = grp
            kr, ko = rb.nxt(), ob.nxt()
            ACT.wait(tok, rb.free[kr])
            ta = ACT.sig(nc.scalar.activation(out=rb.t[kr][:, :n], in_=banks[b][:, :n], func=AF.Relu))
            P.bank_free[b] = ta
            DVE.wait(ta, (ob.ds[ko], ob.ds[ko].n))
            to = DVE.sig(nc.vector.tensor_tensor(out=ob.t[ko][:, :n], in0=rb.t[kr][:, :n], in1=rb.t[kr][:, :n], op=ALU.mult))
            rb.free[kr] = to
            P.dma(SP, actT[m, :, s0:s0 + n], ob.t[ko][:, :n], ob.ds[ko], waits=[to])
            return
            yield
        return epi

    SCALE = 1.0 / math.sqrt(HD)

    class AttnCtx:
        def __init__(s, st, with_tab=False):
            s.pb = Ring(st, "apb", [128, 512], BF16, 5)
            s.si = 0
            s.ui = 0
            s.tabtok = None
            if with_tab:
                s.sbuf = Ring(st, "asb", [128, 512], F32, 3)

    def attn_unit(ac, qT, nq, chunks, nv, hooks=None):
        if nv == 1:
            Sb, LA = [0, 1, 2, 3], 3
            base = 4 + 2 * (ac.ui % 2)
            ob, sb_ = [base], base + 1
        else:
            Sb, LA = [0, 1], 1
            base = 2 + 3 * (ac.ui % 2)
            ob, sb_ = [base, base + 1], base + 2
        ac.ui += 1
        nch = len(chunks)

        def qk(i):
            b = Sb[ac.si % len(Sb)]
            ac.si += 1
            PE.wait(P.bank_free[b])
            mm = nc.tensor.matmul(banks[b][:, :nq], chunks[i][0], qT, start=True, stop=True)
            return b, PE.sig(mm)
        for bb in ob + [sb_]:
            PE.wait(P.bank_free[bb])
        qks = {}
        for j in range(min(LA, nch)):
            qks[j] = qk(j)
        last = None
        for i in range(nch):
            if i + LA < nch:
                qks[i + LA] = qk(i + LA)
            b, tqk = qks.pop(i)
            kp = ac.pb.nxt()
            tabsrc = chunks[i][2]
            if tabsrc is not None:
                ks = ac.sbuf.nxt()
                DVE.wait(tqk, ac.tabtok, ac.sbuf.free[ks])
                tv = DVE.sig(nc.vector.scalar_tensor_tensor(out=ac.sbuf.t[ks][:, :nq], in0=banks[b][:, :nq], scalar=SCALE,
                                                            in1=tabsrc, op0=ALU.mult, op1=ALU.add))
                P.bank_free[b] = tv
                ACT.wait(tv, ac.pb.free[kp])
                te = ACT.sig(nc.scalar.activation(out=ac.pb.t[kp][:, :nq], in_=ac.sbuf.t[ks][:, :nq], func=AF.Exp))
                ac.sbuf.free[ks] = te
            else:
                ACT.wait(tqk, ac.pb.free[kp])
                te = ACT.sig(nc.scalar.activation(out=ac.pb.t[kp][:, :nq], in_=banks[b][:, :nq], func=AF.Exp, scale=SCALE))
                P.bank_free[b] = te
            PE.wait(te)
            for vi in range(nv):
                nc.tensor.matmul(banks[ob[vi]][:, :nq], chunks[i][1][vi], ac.pb.t[kp][:, :nq], start=(i == 0), stop=(i == nch - 1))
            mm = nc.tensor.matmul(banks[sb_][:, :nq], ones[:, :], ac.pb.t[kp][:, :nq], start=(i == 0), stop=(i == nch - 1))
            last = PE.sig(mm)
            ac.pb.free[kp] = last
            if hooks and i in hooks:
                hooks[i]()
        return ob, sb_, last

    def make_attn_epi_A(st):
        rec = Ring(st, "arec", [128, 512], F32, 2)
        ob = Ring(st, "aob", [128, 512], BF16, 3, dsem=True)

        def parts(obanks, sbank, tok, nq, dst):
            stt = {}

            def p1():
                stt["kr"], stt["ko"] = rec.nxt(), ob.nxt()
                kr = stt["kr"]
                DVE.wait(tok, rec.free[kr])
                stt["tr"] = DVE.sig(nc.vector.reciprocal(out=rec.t[kr][:, :nq], in_=banks[sbank][:, :nq]))
                P.bank_free[sbank] = stt["tr"]

            def p2():
                kr, ko, tr = stt["kr"], stt["ko"], stt["tr"]
                DVE.wait(tr, (ob.ds[ko], ob.ds[ko].n))
                to = DVE.sig(nc.vector.tensor_tensor(out=ob.t[ko][:, :nq], in0=banks[obanks[0]][:, :nq], in1=rec.t[kr][:, :nq], op=ALU.mult))
                rec.free[kr] = to
                P.bank_free[obanks[0]] = to
                P.dma(SP, dst, ob.t[ko][:, :nq], ob.ds[ko], waits=[to])
            return p1, p2

        def epi(obanks, sbank, tok, nq, dst):
            p1, p2 = parts(obanks, sbank, tok, nq, dst)
            p1()
            p2()
        epi.parts = parts
        return epi

    def finish():
        P.barrier()
        P.es.close()
        return P
    if STOP_AFTER == 0:
        return finish()
    n_aq, n_ak, n_av, n_bq, n_bk = c.A_H, c.A_KV, c.A_KV, 2 * c.B_H, 2 * c.B_H
    o_ak = n_aq
    o_av = o_ak + n_ak
    o_bq = o_av + n_av
    o_bk = o_bq + n_bq
    o_bv = o_bk + n_bk
    def k0_dst(mk, grp):
        s0, n, kind = grp
        if kind == 0:
            if mk < c.A_KV:
                return K0m[mk][:, s0:s0 + n]
            hb_, m_ = (mk - c.A_KV) // 2, (mk - c.A_KV) % 2
            return K0m[c.A_KV + hb_][m_ * 128:(m_ + 1) * 128, s0:s0 + n]
        return K0c[mk, :, s0 - T:s0 - T + n]

    def v0_dst(tt, vcol0, wn):
        if tt >= T:
            return [(V0c[tt - T:tt - T + 128, vcol0:vcol0 + wn], 0, wn)]
        outl = []
        for a_ in range(0, wn, 128):
            vc = vcol0 + a_
            if vc < c.A_KV * 128:
                j, off = vc // 128, 0
            else:
                j, off = c.A_KV + (vc - c.A_KV * 128) // 256, (vc - c.A_KV * 128) % 256
            outl.append((V0h[j][tt:tt + 128, off:off + 128], a_, a_ + 128))
        return outl

    def lat_groups(sbi):
        return [g for g in c.SB[sbi] if g[2] == 0]

    def mlp(layer, sbi, groups):
        with ExitStack() as st:
            res = P.sb(st, "res", [128, KC, c.RW], BF16)
            alloc_w(st)
            norm_phase(layer, 1, sbi, res, groups=groups)
            with ExitStack() as st2:
                e = make_relu2_epi(st2)
                gemm(res, w1[layer], H, sbi, lambda m: ('r', e), groups=groups)
            for kq in range(4):
                load_res(res, actT, sbi, kq, groups=groups)
                with ExitStack() as st2:
                    e = make_resid_epi(st2, layer, 5)
                    gemm(res, w2[layer, kq * D:(kq + 1) * D, :], D, sbi, lambda m: ('r', e), groups=groups)

    def wout(layer, W, sbi, groups):
        with ExitStack() as st:
            res = P.sb(st, "res", [128, KC, c.RW], BF16)
            alloc_w(st)
            load_res(res, attT, sbi, 0, groups=groups)
            e = make_resid_epi(st, layer, 2)
            gemm(res, W.ap(), D, sbi, lambda m: ('r', e), groups=groups)

    for sbi in range(2):
        with ExitStack() as st:
            res = P.sb(st, "res", [128, KC, c.RW], BF16)
            alloc_w(st)
            norm_phase(0, 0, sbi, res)
            qk = make_qk_epi(st)
            vepi = make_v_epi(st, v0_dst)

            def chunk_fn(m):
                if m < o_ak:
                    return ('r', lambda m_, gi, grp, b, tok, mq=m: qk(m_, gi, grp, b, tok, True, 0, Q0[mq, :, grp[0]:grp[0] + grp[1]]))
                if m < o_av:
                    return ('r', lambda m_, gi, grp, b, tok, mk=m - o_ak: qk(m_, gi, grp, b, tok, True, 1, k0_dst(mk, grp)))
                if m < o_bq:
                    return ('v', (m - o_av) * 128, vepi)
                if m < o_bk:
                    return ('r', lambda m_, gi, grp, b, tok, mq=c.A_H + m - o_bq: qk(m_, gi, grp, b, tok, False, 0, Q0[mq, :, grp[0]:grp[0] + grp[1]]))
                if m < o_bv:
                    return ('r', lambda m_, gi, grp, b, tok, mk=c.A_KV + m - o_bk: qk(m_, gi, grp, b, tok, False, 0, k0_dst(mk, grp)))
                return ('v', c.A_KV * 128 + (m - o_bv) * 128, vepi)
            gemm(res, w_in0.ap(), 9 * D // 4, sbi, chunk_fn)

    def allgather(src, dst, ds=None):
        ds = d_cc if ds is None else ds
        ins = nc.gpsimd.collective_compute("AllGather", ALU.bypass, replica_groups=G4, ins=[src.ap().opt()], outs=[dst.ap().opt()])
        ds.n += 1
        ins.then_inc(ds.sem, 1)
        return (ds, ds.n)

    if STOP_AFTER == 1:
        return finish()
    P.barrier(full=True)
    jobcc = []
    for i in range(NKG):
        allgather(K0m[i], K0mg[i])
        allgather(V0h[i], V0hg[i])
        jobcc.append(None)
    P.barrier(full=True)
    if STOP_AFTER == 2:
        return finish()

    NKC = c.NKC
    with ExitStack() as st:
        ac = AttnCtx(st)
        epiA = make_attn_epi_A(st)
        KT = [P.sb(st, "KT%d" % i, [128, 2, NKC * 128], BF16) for i in range(2)]
        VT = [P.sb(st, "VT%d" % i, [128, NKC, 256], BF16) for i in range(2)]
        kvds = [P.ds("kv", st) for _ in range(2)]
        kvfree = [None, None]
        QT = Ring(st, "QT", [128, TT], BF16, 4, dsem=True)
        omap = P.sb(st, "omap", [128, 2, 2, 512], F32)
        dbuf = P.sb(st, "dbuf", [128, 2, 512], F32)
        sqb = P.sb(st, "sqb", [128, 2, 512], BF16)
        recb = P.sb(st, "recb", [128, 512], F32)
        rsB = P.sb(st, "rsB", [128, 512], F32)
        obB = Ring(st, "obB", [128, 512], BF16, 3, dsem=True)
        bst = {"dfree": None, "sqfree": None, "recfree": None, "omapfree": None, "rsfree": None}
        qblocks = [(s, 512, list(range(NKC))) for s in range(0, T, 512)] + [(T, CTX, [NKC - 2, NKC - 1])]
        TC = T // 128

        def load_kv(slot, kmaps, vcol0, dv, vh):
            w = [kvfree[slot], jobcc[vh]]
            nm_ = len(kmaps)
            for j, mk in enumerate(kmaps):
                P.dma(SP, KT[slot][:, j, 0:4 * T].rearrange("p (r t) -> p r t", r=4),
                      K0mg[vh].ap().rearrange("(r m p) t -> p m r t", r=4, m=nm_)[:, j], kvds[slot], waits=w)
                P.dma(SP, KT[slot][:, j, 4 * T:4 * T + CTX], K0c[mk], kvds[slot])
            P.dma(SP, VT[slot][:, 0:4 * TC, 0:dv], V0hg[vh].ap().rearrange("(c p) w -> p c w", p=128), kvds[slot])
            P.dma(SP, VT[slot][:, 4 * TC:4 * TC + 2, 0:dv], V0c[:, vcol0:vcol0 + dv].rearrange("(c p) w -> p c w", p=128), kvds[slot])
            return (kvds[slot], kvds[slot].n)

        def load_q(mq):
            k = QT.nxt()
            tq = P.dma(SP, QT.t[k][:, :], Q0[mq], QT.ds[k], waits=[QT.free[k]])
            return k, tq

        jobs = [('A', kv) for kv in range(c.A_KV)] + [('B', hb) for hb in range(c.B_H)]

        def job_kv(ji):
            kind, idx = jobs[ji]
            if kind == 'A':
                return load_kv(ji % 2, [idx], idx * 128, 128, idx)
            return load_kv(ji % 2, [c.A_KV + 2 * idx, c.A_KV + 2 * idx + 1], c.A_KV * 128 + idx * 256, 256, c.A_KV + idx)

        pendB = []
        tkv_next = job_kv(0)
        for ji, (kind, idx) in enumerate(jobs):
            slot = ji % 2
            tkv = tkv_next
            if ji + 1 < len(jobs):
                tkv_next = job_kv(ji + 1)
            PE.wait(tkv)
            if kind == 'A':
                for hq in range(4 * idx, 4 * idx + 4):
                    kq_, tq = load_q(hq)
                    PE.wait(tq)
                    for (s0, nq, cl) in qblocks:
                        chunks = [(KT[slot][:, 0, ci * 128:(ci + 1) * 128], [VT[slot][:, ci, 0:128]], None) for ci in cl]
                        obk, sbk, tok = attn_unit(ac, QT.t[kq_][:, s0:s0 + nq], nq, chunks, 1)
                        epiA(obk, sbk, tok, nq, attT[hq, :, s0:s0 + nq])
                        run_pending(pendB)
                    QT.free[kq_] = (PE, PE.n)
            else:
                hb = idx
                qs = [load_q(c.A_H + 2 * hb + m) for m in range(2)]
                for (s0, nq, cl) in qblocks:
                    for m in range(2):
                        PE.wait(qs[m][1])
                        chunks = [(KT[slot][:, m, ci * 128:(ci + 1) * 128], [VT[slot][:, ci, 0:128], VT[slot][:, ci, 128:256]], None) for ci in cl]
                        obk, sbk, tok = attn_unit(ac, QT.t[qs[m][0]][:, s0:s0 + nq], nq, chunks, 2)
                        DVE.wait(tok, bst["recfree"])
                        tr = DVE.sig(nc.vector.reciprocal(out=recb[:, :nq], in_=banks[sbk][:, :nq]))
                        DVE.wait(tr, bst["omapfree"])
                        for cc_ in range(2):
                            to = DVE.sig(nc.vector.tensor_tensor(out=omap[:, m, cc_, :nq], in0=banks[obk[cc_]][:, :nq], in1=recb[:, :nq], op=ALU.mult))
                            P.bank_free[obk[cc_]] = to
                        bst["recfree"] = to
                        run_pending(pendB)
                        if m == 0:
                            P.bank_free[sbk] = tr
                            continue
                        DVE.wait(to, bst["dfree"])
                        for cc_ in range(2):
                            td = DVE.sig(nc.vector.scalar_tensor_tensor(out=dbuf[:, cc_, :nq], in0=omap[:, 1, cc_, :nq], scalar=lamb[:, 0:1],
                                                                        in1=omap[:, 0, cc_, :nq], op0=ALU.mult, op1=ALU.add))
                        bst["omapfree"] = td
                        ACT.wait(td, bst["sqfree"])
                        ts = ACT.sig(nc.scalar.activation(out=sqb[:, :, :nq], in_=dbuf[:, :, :nq], func=AF.Square))
                        bst["dfree"] = ts

                        def stage1(sbk=sbk, nq=nq, s0=s0, hb=hb, ts=ts, tr=tr):
                            yield
                            PE.wait(ts, tr)
                            nc.tensor.matmul(banks[sbk][:, :nq], ones[:, :], sqb[:, 0, :nq], start=True, stop=False)
                            tp = PE.sig(nc.tensor.matmul(banks[sbk][:, :nq], ones[:, :], sqb[:, 1, :nq], start=False, stop=True))
                            bst["sqfree"] = tp
                            ACT.wait(tp, bst["rsfree"])
                            ta = ACT.sig(nc.scalar.activation(out=rsB[:, :nq], in_=banks[sbk][:, :nq], func=AF.Sqrt, bias=epsb[:, 0:1], scale=1.0 / 256))
                            P.bank_free[sbk] = ta
                            DVE.wait(ta)
                            tr2 = DVE.sig(nc.vector.reciprocal(out=rsB[:, :nq], in_=rsB[:, :nq]))
                            for cc_ in range(2):
                                ko = obB.nxt()
                                DVE.wait(tr2, (obB.ds[ko], obB.ds[ko].n))
                                to2 = DVE.sig(nc.vector.scalar_tensor_tensor(out=obB.t[ko][:, :nq], in0=dbuf[:, cc_, :nq], scalar=Gq[:, 2 + cc_:3 + cc_],
                                                                             in1=rsB[:, :nq], op0=ALU.mult, op1=ALU.mult))
                                P.dma(SP, attT[c.A_H + 2 * hb + cc_, :, s0:s0 + nq], obB.t[ko][:, :nq], obB.ds[ko], waits=[to2])
                            bst["rsfree"] = to2
                            bst["dfree"] = [ts, to2]
                        pendB.append(stage1())
                        run_pending(pendB)
                for m in range(2):
                    QT.free[qs[m][0]] = (PE, PE.n)
            kvfree[slot] = (PE, PE.n)
        run_pending(pendB, final=True)
        P.barrier()

    if STOP_AFTER == 3:
        return finish()
    for sbi in range(2):
        wout(0, w_out0, sbi, c.SB[sbi])
    for sbi in range(2):
        mlp(0, sbi, c.SB[sbi])

    if STOP_AFTER == 4:
        return finish()
    OWN0 = 256
    BEL0 = 256 + T
    CTX0 = 512 + T
    for sbi in range(2):
        with ExitStack() as st:
            res = P.sb(st, "res", [128, KC, c.RW], BF16)
            alloc_w(st)
            norm_phase(1, 0, sbi, res)

            def q1_dst(m, s0, n, kind):
                return Q1[m, :, s0:s0 + n]

            def k1_dst(m, s0, n, kind):
                e0 = OWN0 + s0 if kind == 0 else CTX0 + s0 - T
                return K1[m - KC, :, e0:e0 + n]

            def v1_dst(tt, vcol0, wn):
                e0 = OWN0 + tt if tt < T else CTX0 + tt - T
                return [(V1[e0:e0 + 128, vcol0:vcol0 + wn], 0, wn)]
            eq = make_copy_epi(st, q1_dst)
            ek = make_copy_epi(st, k1_dst)
            ev = make_v_epi(st, v1_dst)

            def chunk_fn1(m):
                if m < KC:
                    return ('r', eq, True)
                if m < 2 * KC:
                    return ('r', ek)
                return ('v', (m - 2 * KC) * 128, ev)
            gemm(res, w_in1.ap(), 3 * D, sbi, chunk_fn1)

    if STOP_AFTER == 5:
        return finish()
    K1f = K1.ap().rearrange("h p n -> (h p) n")
    for p_ in range(NKP):
        rs_ = slice(p_ * HP * 128, (p_ + 1) * HP * 128)
        P.dma(SP, Kb[p_][:, 0:256], K1f[rs_, OWN0:OWN0 + 256], d_misc)
        P.dma(SP, Kb[p_][:, 256:512], K1f[rs_, T:T + 256], d_misc)
    for i_ in range(4):
        r0_ = (OWN0 if i_ < 2 else T) + (i_ % 2) * 128
        for j_ in range(NVP):
            P.dma(SP, Vb[i_][j_][:, :], V1[r0_:r0_ + 128, j_ * VPW:(j_ + 1) * VPW], d_misc)
    POOL.wait((d_misc, d_misc.n))
    for p_ in range(NKP):
        allgather(Kb[p_], Kbg[p_])
    for i_ in range(4):
        for j_ in range(NVP):
            allgather(Vb[i_][j_], Vbg[i_][j_])
    P.barrier()
    with ExitStack() as st:
        X = P.sb(st, "hx", [128, 4, KC * 256], BF16)
        Y = [P.sb(st, "hy%d" % i, [128, KC * 256], BF16) for i in range(2)]
        yds = [P.ds("hy", st) for _ in range(2)]
        xfree = None
        it = 0
        for (isv, side) in ((0, 0), (0, 1), (1, 0), (1, 1)):
            c0_ = 256 if side == 0 else 0
            tl_ = None
            for r in range(4):
                if isv == 0:
                    for p_ in range(NKP):
                        src = Kbg[p_][r * HP * 128:(r + 1) * HP * 128, c0_:c0_ + 256].rearrange("(h p) n -> p h n", p=128)
                        dstx = X[:, r, p_ * HP * 256:(p_ + 1) * HP * 256].rearrange("p (h n) -> p h n", n=256)
                        tl_ = P.dma(SP, dstx, src, d_misc, waits=[xfree])
                else:
                    for cl_ in range(2):
                        for j_ in range(NVP):
                            src = Vbg[(c0_ // 128) + cl_][j_][r * 128:(r + 1) * 128, :]
                            tl_ = P.dma(SP, X[:, r, cl_ * D + j_ * VPW:cl_ * D + (j_ + 1) * VPW], src, d_misc, waits=[xfree])
            W_ = KC * 256 if isv == 0 else 2 * D
            y = Y[it % 2]
            DVE.wait(tl_, (d_misc, d_misc.n), (yds[it % 2], yds[it % 2].n))
            ty = DVE.sig(nc.vector.tensor_scalar(out=y[:, :W_], in0=X[:, 0, :W_], scalar1=selb[:, 4 * side:4 * side + 1], scalar2=None, op0=ALU.mult))
            for r in range(1, 4):
                DVE.wait(ty)
                ty = DVE.sig(nc.vector.scalar_tensor_tensor(out=y[:, :W_], in0=X[:, r, :W_], scalar=selb[:, 4 * side + r:4 * side + r + 1],
                                                            in1=y[:, :W_], op0=ALU.mult, op1=ALU.add))
            xfree = ty
            e0 = 0 if side == 0 else BEL0
            if isv == 0:
                P.dma(SP, K1[:, :, e0:e0 + 256].rearrange("h p n -> p h n"), y[:, :W_].rearrange("p (h n) -> p h n", n=256), yds[it % 2], waits=[ty])
            else:
                P.dma(SP, V1[e0:e0 + 256, :].rearrange("(c p) w -> p c w", p=128), y[:, :W_].rearrange("p (c w) -> p c w", w=D), yds[it % 2], waits=[ty])
            it += 1
        P.barrier()

    if STOP_AFTER == 6:
        return finish()
    HG = 2
    NT1 = c.EXT // 128
    with ExitStack() as st:
        ac = AttnCtx(st, with_tab=True)
        epiA = make_attn_epi_A(st)
        KT1 = [P.sb(st, "K1T%d" % i, [128, HG, c.EXT], BF16) for i in range(2)]
        VT1 = [P.sb(st, "V1T%d" % i, [128, NT1, HG * 128], BF16) for i in range(2)]
        kvds = [P.ds("kv1", st) for _ in range(2)]
        kvfree = [None, None]
        QT = Ring(st, "Q1T", [128, T], BF16, 2, dsem=True)
        TB = Ring(st, "TB", [128, 3, 8, 512], F32, 2, dsem=True)

        def load_tb(h):
            k = TB.nxt()
            for ts__ in range(3):
                P.dma(SP, TB.t[k][:, ts__, :, :], tabs[ts__, h].rearrange("k p n -> p k n"), TB.ds[k], waits=[TB.free[k]])
            return k, (TB.ds[k], TB.ds[k].n)
        tb_next = load_tb(0)

        def load_q1(h):
            k = QT.nxt()
            return k, P.dma(SP, QT.t[k][:, :], Q1[h], QT.ds[k], waits=[QT.free[k]])
        pend_epi = [None]

        def load_kv1(hg):
            slot = hg % 2
            P.dma(SP, KT1[slot][:, :, :], K1[hg * HG:(hg + 1) * HG].rearrange("h p n -> p h n"), kvds[slot], waits=[kvfree[slot]])
            P.dma(SP, VT1[slot][:, :, :], V1[:, hg * HG * 128:(hg + 1) * HG * 128].rearrange("(c p) w -> p c w", p=128), kvds[slot])
            return (kvds[slot], kvds[slot].n)
        nhg = KC // HG
        tnext = load_kv1(0)
        for hg in range(nhg):
            slot = hg % 2
            tkv = tnext
            if hg + 1 < nhg:
                tnext = load_kv1(hg + 1)
            PE.wait(tkv)
            for hh in range(HG):
                h = hg * HG + hh
                ktb, ac.tabtok = tb_next
                if h + 1 < KC:
                    tb_next = load_tb(h + 1)
                if h == 0:
                    q_next = load_q1(0)
                kq_, tq = q_next
                if h + 1 < KC:
                    q_next = load_q1(h + 1)
                PE.wait(tq)
                for g in range(c.G):
                    ts_ = 0 if g == 0 else (2 if g == c.G - 1 else 1)
                    chunks = []
                    for k in range(8):
                        e0 = g * 512 + k * 128
                        chunks.append((KT1[slot][:, hh, e0:e0 + 128], [VT1[slot][:, e0 // 128, hh * 128:(hh + 1) * 128]], TB.t[ktb][:, ts_, k, :]))
                    for j in range(2):
                        e0 = CTX0 + j * 128
                        chunks.append((KT1[slot][:, hh, e0:e0 + 128], [VT1[slot][:, e0 // 128, hh * 128:(hh + 1) * 128]], None))
                    hooks = None
                    if pend_epi[0] is not None:
                        hooks = {2: pend_epi[0][0], 6: pend_epi[0][1]}
                    obk, sbk, tok = attn_unit(ac, QT.t[kq_][:, g * 512:(g + 1) * 512], 512, chunks, 1, hooks=hooks)
                    pend_epi[0] = epiA.parts(obk, sbk, tok, 512, attT[h, :, g * 512:(g + 1) * 512])
                QT.free[kq_] = (PE, PE.n)
                TB.free[ktb] = (DVE, DVE.n)
            kvfree[slot] = (PE, PE.n)
        pend_epi[0][0]()
        pend_epi[0][1]()
        P.barrier()

    for sbi in range(2):
        lg = lat_groups(sbi)
        if lg:
            wout(1, w_out1, sbi, lg)
    for sbi in range(2):
        lg = lat_groups(sbi)
        if lg:
            mlp(1, sbi, lg)
    for sbi in range(2):
        lg = lat_groups(sbi)
        if lg:
            norm_phase(1, 0, sbi, None, final=True, groups=lg)
    P.barrier()
    P.es.close()
    return P


_CACHE = {}


def _rope_tables(cfg, q):
    T, TT = cfg.T, cfg.TT
    t = np.arange(T, dtype=np.int32) + q * T
    row = (t // GRID_W).astype(np.float32)
    col = (t % GRID_W).astype(np.float32)
    nf = HD // 4
    freqs = (np.float32(10000.0) ** (-np.arange(nf, dtype=np.float32) / np.float32(nf))).astype(np.float32)
    cosT = np.ones((128, TT), np.float32)
    sinT = np.zeros((128, TT), np.float32)
    for axis, pos in enumerate((row, col)):
        ang = (pos[None, :] * freqs[:, None]).astype(np.float32)
        cs, sn = np.cos(ang).astype(np.float32), np.sin(ang).astype(np.float32)
        for pair in range(2):
            d0 = axis * 64 + pair * 32
            cosT[d0:d0 + 32, :T] = cs
            sinT[d0:d0 + 32, :T] = -sn if pair == 0 else sn
    return cosT, sinT


def _na_tables(cfg, rel_bias):
    ROWS = cfg.ROWS
    out = []
    for qr0 in (0, 8, ROWS - 8):
        w = np.arange(16)
        krow = qr0 - 4 + w
        j = np.arange(8)
        qrow = qr0 + j
        r0q = np.clip(qrow - 4, 0, ROWS - 8)
        vrow = (krow[:, None] >= 0) & (krow[:, None] < ROWS) & (krow[:, None] >= r0q[None, :]) & (krow[:, None] < r0q[None, :] + 8)
        kcol = np.arange(GRID_W)
        qcol = np.arange(GRID_W)
        c0 = np.clip(qcol - 8, 0, GRID_W - 16)
        vcol = (kcol[:, None] >= c0[None, :]) & (kcol[:, None] < c0[None, :] + 16)
        dr = np.clip(krow[:, None] - qrow[None, :] + 7, 0, 14)
        dc = np.clip(kcol[:, None] - qcol[None, :] + 15, 0, 30)
        tab = rel_bias[:, dr[:, None, :, None], dc[None, :, None, :]]
        valid = vrow[:, None, :, None] & vcol[None, :, None, :]
        tab = np.where(valid[None], tab, np.float32(NEG)).astype(np.float32)
        out.append(tab.reshape(rel_bias.shape[0], 8, 128, 512))
    return out


def kernel(x, c, ctx, c_ctx, ada_w, ada_b, norm1_g, norm2_g, w_in_even, w_out_even,
           a_q_norm, a_k_norm, b_lambda_q1, b_lambda_k1, b_lambda_q2, b_lambda_k2, b_subln_g,
           w_in_odd, w_out_odd, na_rel_bias, mlp_w1, mlp_w2, final_g, _cfg=None):
    f = lambda a: np.ascontiguousarray(np.asarray(a, dtype=np.float32))
    x, c, ctx, c_ctx = f(x), f(c), f(ctx), f(c_ctx)
    B, S, D = x.shape
    cfg = _cfg or Cfg(D, S)
    KC, T, TT, MJ = cfg.KC, cfg.T, cfg.TT, cfg.MJ
    key = (D, S)
    if key not in _CACHE:
        _CACHE[key] = build(cfg)
    P = _CACHE[key]
    ada_w, ada_b = f(ada_w), f(ada_b)
    tabs3 = _na_tables(cfg, f(na_rel_bias)[0])
    perm = np.zeros((128, 128), np.float32)
    perm[np.arange(128) ^ 32, np.arange(128)] = 1.0
    smalls = np.stack([f(a_q_norm)[0], f(a_k_norm)[0], f(b_lambda_q1)[0], f(b_lambda_k1)[0], f(b_lambda_q2)[0], f(b_lambda_k2)[0],
                       f(b_subln_g)[0][:128], f(b_subln_g)[0][128:]], axis=1)
    shared = {
        "gn1": f(f(norm1_g).reshape(2, KC, 128).transpose(2, 0, 1)),
        "gn2": f(f(norm2_g).reshape(2, KC, 128).transpose(2, 0, 1)),
        "gfin": f(f(final_g).reshape(KC, 128).T),
        "w_in0": f(w_in_even)[0], "w_out0": f(w_out_even)[0], "w_in1": f(w_in_odd)[0], "w_out1": f(w_out_odd)[0],
        "w1": f(mlp_w1), "w2": f(mlp_w2), "smalls": f(smalls), "perm": perm,
    }
    in_maps = []
    for r in range(8):
        b, q = r // 4, r % 4
        xt = np.concatenate([x[b, q * T:(q + 1) * T], ctx[b]], axis=0)
        cosT, sinT = _rope_tables(cfg, q)
        sel = np.zeros((128, 8), np.float32)
        sel[:, (q - 1) % 4] = 1.0
        sel[:, 4 + (q + 1) % 4] = 1.0
        m = dict(shared)
        m.update({
            "xT": f(xt.T.reshape(KC, 128, TT)),
            "cvT": f(np.stack([c[b], c_ctx]).reshape(2, KC, 128).transpose(2, 1, 0)),
            "adaw": f(ada_w[:, :, q * MJ * 128:(q + 1) * MJ * 128]),
            "adab": f(ada_b[:, q * MJ * 128:(q + 1) * MJ * 128].reshape(2, MJ, 128).transpose(2, 0, 1)),
            "cosT": cosT, "sinT": sinT,
            "tabs": f(np.stack([tabs3[0] if q == 0 else tabs3[1], tabs3[1], tabs3[2] if q == 3 else tabs3[1]])),
            "sel": sel,
        })
        in_maps.append(m)
    res = run_bass_kernel_spmd(P.nc, in_maps, core_ids=list(range(8)))
    out = np.empty((B, S, D), np.float32)
    for r in range(8):
        b, q = r // 4, r % 4
        out[b, q * T:(q + 1) * T] = res.results[r]["outT"].reshape(D, T).T
    kernel.last = res
    return out
```

```python
import math
from contextlib import ExitStack
import numpy as np
import concourse.bass as bass
import concourse.mybir as mybir
from concourse.bass_utils import run_bass_kernel_spmd

F32 = mybir.dt.float32
BF16 = mybir.dt.bfloat16
AF = mybir.ActivationFunctionType
ALU = mybir.AluOpType
EPS = 1e-6
GRID_W = 64
CTX = 256
HD = 128
NEG = -30000.0
STOP_AFTER = 99


class Cfg:
    def __init__(s, D=4096, S=8192):
        s.D, s.S = D, S
        s.KC = D // 128
        s.T = S // 4
        s.TT = s.T + CTX
        s.R = s.T // GRID_W
        s.G = s.R // 8
        s.ROWS = S // GRID_W
        s.A_H = D // 256
        s.A_KV = s.A_H // 4
        s.B_H = D // 512
        s.H = 4 * D
        s.NM_K = s.A_KV + 2 * s.B_H
        s.NM_Q = s.A_H + 2 * s.B_H
        s.VW = s.A_KV * 128 + s.B_H * 256
        s.NKC = (4 * s.T + CTX) // 128
        s.EXT = (8 + s.R) * GRID_W + CTX
        s.MJ = 6 * s.KC // 4
        if s.T == 2048:
            s.SB = [[(0, 384, 0), (384, 384, 0), (768, 384, 0)], [(1152, 512, 0), (1664, 384, 0), (2048, 256, 1)]]
        elif s.T == 1024:
            s.SB = [[(0, 512, 0), (512, 512, 0)], [(1024, 256, 1)]]
        else:
            raise ValueError
        s.RW = max(sum(n for _, n, _ in sb) for sb in s.SB)


class DS:
    def __init__(s, sem):
        s.sem, s.n, s.waited = sem, 0, 0


class Eng:
    def __init__(s, P, name, eng):
        s.P, s.name, s.eng = P, name, eng
        s.sem = P.new_sem("e_" + name)
        s.n = 0
        s.seen = {}

    def wait(s, *toks):
        for t in toks:
            if t is None:
                continue
            if isinstance(t, list):
                s.wait(*t)
                continue
            src, v = t
            if v <= 0 or s.seen.get(id(src), 0) >= v:
                continue
            s.eng.wait_ge(src.sem, v)
            s.seen[id(src)] = v
            if isinstance(src, DS):
                src.waited = max(src.waited, v)

    def sig(s, ins):
        s.n += 1
        ins.then_inc(s.sem, 1)
        return (s, s.n)


class Prog:
    def __init__(s, cfg):
        s.cfg = cfg
        s.nc = bass.Bass("TRN2", target_bir_lowering=False)
        s.es = ExitStack()
        s.nsem = 0
        s.dsems = []
        s.pool_sync = True

    def new_sem(s, name):
        s.nsem += 1
        return s.es.enter_context(s.nc.semaphore(name + "_%d" % s.nsem))

    def ds(s, name="d", st=None):
        if not hasattr(s, "dfree"):
            s.dfree = []
        if s.dfree:
            d = s.dfree.pop()
        else:
            d = DS(s.new_sem(name))
            s.dsems.append(d)
        if st is not None:
            st.callback(lambda d=d: s.dfree.append(d))
        return d

    def start(s):
        nc = s.nc
        s.block = s.es.enter_context(nc.Block())
        s.PE = Eng(s, "pe", nc.tensor)
        s.ACT = Eng(s, "act", nc.scalar)
        s.DVE = Eng(s, "dve", nc.vector)
        s.POOL = Eng(s, "pool", nc.gpsimd)
        s.SP = Eng(s, "sp", nc.sync)
        s.engs = [s.PE, s.ACT, s.DVE, s.POOL, s.SP]
        s.banks = [s.es.enter_context(nc.psum_tensor("bank%d" % i, [128, 512], F32)) for i in range(8)]
        s.bank_free = [None] * 8

    def dma(s, q, out, in_, ds, waits=()):
        q.wait(*waits)
        if ds.waited > 0:
            q.wait((ds, ds.waited))
        ins = q.eng.dma_start(out=out, in_=in_)
        ds.n += 16
        ins.then_inc(ds.sem, 16)
        return (ds, ds.n)

    def barrier(s, full=False):
        toks = [(e, e.n) for e in s.engs] + [(d, d.n) for d in s.dsems]
        for e in s.engs:
            if e is s.POOL and not (full or s.pool_sync):
                continue
            e.wait(*toks)
        s.bank_free = [None] * 8

    def sb(s, st, name, shape, dt):
        s.nsb = getattr(s, "nsb", 0) + 1
        return st.enter_context(s.nc.sbuf_tensor("%s_u%d" % (name, s.nsb), shape, dt))


def build(cfg, debug=False):
    P = Prog(cfg)
    nc = P.nc
    c = cfg
    D, KC, T, TT, H = c.D, c.KC, c.T, c.TT, c.H

    def din(name, shape, dt=F32):
        return nc.dram_tensor(name, list(shape), dt, kind="ExternalInput")

    def dscr(name, shape, dt, out=False):
        return nc.dram_tensor(name, list(shape), dt, kind=("ExternalOutput" if (out and debug) else "Internal"))

    xT = din("xT", [KC, 128, TT])
    cvT = din("cvT", [128, KC, 2])
    adaw = din("adaw", [2, D, c.MJ * 128])
    adab = din("adab", [128, 2, c.MJ])
    gn1 = din("gn1", [128, 2, KC])
    gn2 = din("gn2", [128, 2, KC])
    gfin = din("gfin", [128, KC])
    w_in0 = din("w_in0", [D, 9 * D // 4])
    w_out0 = din("w_out0", [D, D])
    w_in1 = din("w_in1", [D, 3 * D])
    w_out1 = din("w_out1", [D, D])
    w1 = din("w1", [2, D, H])
    w2 = din("w2", [2, H, D])
    smalls = din("smalls", [128, 8])
    cosT = din("cosT", [128, TT])
    sinT = din("sinT", [128, TT])
    permI = din("perm", [128, 128])
    tabs = din("tabs", [3, KC, 8, 128, 512])
    sel = din("sel", [128, 8])
    outT = nc.dram_tensor("outT", [KC, 128, T], F32, kind="ExternalOutput")

    hT = dscr("hT", [KC, 128, TT], F32, out=True)
    mod_src = dscr("mod_src", [128, 2 * c.MJ * 2], F32)
    mod_g = dscr("mod_g", [4 * 128, 2 * c.MJ * 2], F32)
    Q0 = dscr("Q0", [c.NM_Q, 128, TT], BF16)
    NKG = c.A_KV + c.B_H
    KGR = [128] * c.A_KV + [256] * c.B_H
    K0m = [dscr("K0m%d" % i, [KGR[i], T], BF16) for i in range(NKG)]
    K0mg = [dscr("K0mg%d" % i, [4 * KGR[i], T], BF16) for i in range(NKG)]
    K0c = dscr("K0c", [c.NM_K, 128, CTX], BF16)
    NVH = c.A_KV + c.B_H
    DV = [128] * c.A_KV + [256] * c.B_H
    V0h = [dscr("V0h%d" % i, [T, DV[i]], BF16) for i in range(NVH)]
    V0hg = [dscr("V0hg%d" % i, [4 * T, DV[i]], BF16) for i in range(NVH)]
    V0c = dscr("V0c", [CTX, c.VW], BF16)
    attT = dscr("attT", [KC, 128, TT], BF16)
    actT = dscr("actT", [4 * KC, 128, TT], BF16)
    Q1 = dscr("Q1", [KC, 128, T], BF16)
    K1 = dscr("K1", [KC, 128, c.EXT], BF16)
    V1 = dscr("V1", [c.EXT, D], BF16)
    HP = min(8, KC)
    NKP = KC // HP
    Kb = [dscr("Kb%d" % i, [HP * 128, 512], BF16) for i in range(NKP)]
    Kbg = [dscr("Kbg%d" % i, [4 * HP * 128, 512], BF16) for i in range(NKP)]
    VPW = min(D, 4096)
    NVP = D // VPW
    Vb = [[dscr("Vb%d_%d" % (i, j), [128, VPW], BF16) for j in range(NVP)] for i in range(4)]
    Vbg = [[dscr("Vbg%d_%d" % (i, j), [4 * 128, VPW], BF16) for j in range(NVP)] for i in range(4)]

    P.start()
    PE, ACT, DVE, POOL, SP = P.PE, P.ACT, P.DVE, P.POOL, P.SP
    banks = P.banks
    G4 = [[0, 1, 2, 3], [4, 5, 6, 7]]

    gs = P.es
    ones = P.sb(gs, "ones", [128, 128], BF16)
    perm = P.sb(gs, "permb", [128, 128], BF16)
    epsb = P.sb(gs, "epsb", [128, 1], F32)
    smb = P.sb(gs, "smb", [128, 8], F32)
    selb = P.sb(gs, "selb", [128, 8], F32)
    modf = P.sb(gs, "modf", [128, 2, 6 * KC, 2], F32)
    Am = P.sb(gs, "Am", [128, 2, 2, 2, KC], F32)
    g1b = P.sb(gs, "g1b", [128, 2, KC], F32)
    g2b = P.sb(gs, "g2b", [128, 2, KC], F32)
    gfb = P.sb(gs, "gfb", [128, KC], F32)
    Gq = P.sb(gs, "Gq", [128, 4], F32)
    lamb = P.sb(gs, "lamb", [128, 2], F32)
    cosb = P.sb(gs, "cosb", [128, TT], BF16)
    sinb = P.sb(gs, "sinb", [128, TT], BF16)
    NW = 4
    wring = [None] * NW
    wds = [P.ds("w") for _ in range(NW)]
    wfree = [None] * NW
    wcnt = [0]

    def alloc_w(st):
        P.barrier(full=True)
        wring[:] = [P.sb(st, "wr%d_%d" % (i, P.nsem), [128, KC, 256], BF16) for i in range(NW)]
        for i in range(NW):
            wfree[i] = None
        P.pool_sync = False

        def _restore():
            P.pool_sync = True
            P.barrier(full=True)
        st.callback(_restore)
    d_misc = P.ds("misc")
    d_pm = P.ds("pmisc")
    d_cc = P.ds("cc")

    def modv(i, part, v):
        return modf[:, i, part * KC:(part + 1) * KC, v]

    with ExitStack() as st:
        cvf = P.sb(st, "cvf", [128, KC, 2], F32)
        cvb = P.sb(st, "cvb", [128, KC, 2], BF16)
        adabb = P.sb(st, "adabb", [128, 2, c.MJ], F32)
        modl = P.sb(st, "modl", [128, 2, c.MJ, 2], F32)
        lt = P.sb(st, "lt", [128, 4], F32)
        alloc_w(st)
        t_in = [P.dma(SP, cvf[:, :, :], cvT[:, :, :], d_misc),
                P.dma(SP, adabb[:, :, :], adab[:, :, :], d_misc),
                P.dma(SP, smb[:, :], smalls[:, :], d_misc),
                P.dma(SP, selb[:, :], sel[:, :], d_misc),
                P.dma(SP, g1b[:, :, :], gn1[:, :, :], d_misc),
                P.dma(SP, g2b[:, :, :], gn2[:, :, :], d_misc),
                P.dma(SP, gfb[:, :], gfin[:, :], d_misc),
                P.dma(POOL, perm[:, :], permI[:, :], d_pm),
                P.dma(POOL, cosb[:, :], cosT[:, :], d_pm),
                P.dma(POOL, sinb[:, :], sinT[:, :], d_pm)]
        t_ones = DVE.sig(nc.vector.memset(ones[:, :], 1.0))
        t_eps = DVE.sig(nc.vector.memset(epsb[:, :], EPS))
        ACT.wait(t_in[6], t_in[-1])
        t_cv = ACT.sig(nc.scalar.activation(out=cvb[:, :, :], in_=cvf[:, :, :], func=AF.Silu))
        psm = banks[0]
        for i in range(2):
            for ct in range(c.MJ // 2):
                slot = wcnt[0] % NW
                wcnt[0] += 1
                src = adaw[i].rearrange("(kc p) n -> p kc n", p=128)[:, :, ct * 256:(ct + 1) * 256]
                tw = P.dma(POOL, wring[slot][:, :, :], src, wds[slot], waits=[wfree[slot]])
                PE.wait(tw, t_cv)
                for mc in range(2):
                    j = ct * 2 + mc
                    col = (i * c.MJ + j) * 2
                    for kc in range(KC):
                        mm = nc.tensor.matmul(psm[:, col:col + 2], wring[slot][:, kc, mc * 128:(mc + 1) * 128],
                                              cvb[:, kc, :], start=(kc == 0), stop=(kc == KC - 1))
                wfree[slot] = PE.sig(mm)
        DVE.wait((PE, PE.n), t_in[6], t_in[-1])
        for i in range(2):
            for v in range(2):
                tm = DVE.sig(nc.vector.tensor_tensor(
                    out=modl[:, i, :, v], in0=psm[:, i * c.MJ * 2:(i + 1) * c.MJ * 2].rearrange("p (j v) -> p j v", v=2)[:, :, v],
                    in1=adabb[:, i, :], op=ALU.add))
        t1 = P.dma(SP, mod_src[:, :], modl[:, :, :, :].rearrange("p i j v -> p (i j v)"), d_misc, waits=[tm])
        POOL.wait(t1)
        cc = nc.gpsimd.collective_compute("AllGather", ALU.bypass, replica_groups=G4,
                                          ins=[mod_src.ap().opt()], outs=[mod_g.ap().opt()])
        d_cc.n += 1
        cc.then_inc(d_cc.sem, 1)
        tcc = (d_cc, d_cc.n)
        tl = None
        for r in range(4):
            for i in range(2):
                tl = P.dma(SP, modf[:, i, r * c.MJ:(r + 1) * c.MJ, :],
                           mod_g[r * 128:(r + 1) * 128, :].rearrange("p (i j v) -> p i j v", i=2, v=2)[:, i, :, :],
                           d_misc, waits=[tcc])
        DVE.wait(tl)
        for i in range(2):
            for sub, (part, gb) in enumerate(((1, g1b), (4, g2b))):
                for v in range(2):
                    ta = DVE.sig(nc.vector.scalar_tensor_tensor(out=Am[:, i, sub, v, :], in0=modv(i, part, v), scalar=1.0,
                                                                in1=gb[:, i, :], op0=ALU.add, op1=ALU.mult))
        DVE.sig(nc.vector.tensor_copy(out=Gq[:, 0:2], in_=smb[:, 0:2]))
        DVE.sig(nc.vector.tensor_scalar(out=Gq[:, 2:4], in0=smb[:, 6:8], scalar1=0.8, scalar2=None, op0=ALU.mult))
        onesf = P.sb(st, "onesf", [128, 128], F32)
        tof = DVE.sig(nc.vector.memset(onesf[:, :], 1.0))
        DVE.sig(nc.vector.tensor_tensor(out=lt[:, 0:1], in0=smb[:, 2:3], in1=smb[:, 3:4], op=ALU.mult))
        tl2 = DVE.sig(nc.vector.tensor_tensor(out=lt[:, 1:2], in0=smb[:, 4:5], in1=smb[:, 5:6], op=ALU.mult))
        PE.wait(tl2)
        tp = PE.sig(nc.tensor.matmul(banks[1][:, 0:2], onesf[:, :], lt[:, 0:2], start=True, stop=True))
        ACT.wait(tp)
        te = ACT.sig(nc.scalar.activation(out=lt[:, 2:4], in_=banks[1][:, 0:2], func=AF.Exp))
        DVE.wait(te)
        DVE.sig(nc.vector.tensor_tensor(out=lamb[:, 0:1], in0=lt[:, 3:4], in1=lt[:, 2:3], op=ALU.subtract))
        DVE.wait((DVE, DVE.n))
        DVE.sig(nc.vector.tensor_scalar(out=lamb[:, 0:1], in0=lamb[:, 0:1], scalar1=-0.2, scalar2=None, op0=ALU.add))
        for kc in range(KC):
            P.dma(SP, hT[kc], xT[kc], d_misc)
        P.barrier()

    def w_tile(Wap, c0):
        slot = wcnt[0] % NW
        wcnt[0] += 1
        src = Wap.rearrange("(kc p) n -> p kc n", p=128)[:, :, c0:c0 + 256]
        tok = P.dma(POOL, wring[slot][:, :, :], src, wds[slot], waits=[wfree[slot]])
        return slot, tok

    bank_rr = [0]

    def next_bank(lo, n):
        b = lo + (bank_rr[0] % n)
        bank_rr[0] += 1
        return b

    def run_pending(pend, final=False):
        keep = []
        for g in pend:
            try:
                next(g)
                keep.append(g)
            except StopIteration:
                pass
        pend[:] = keep
        if final:
            while pend:
                run_pending(pend)

    def norm_phase(layer, sub, sbi, res, final=False, groups=None):
        groups = c.SB[sbi] if groups is None else groups
        if not groups:
            return
        t0 = c.SB[sbi][0][0]
        with ExitStack() as st:
            NS_, NQ_, NT_ = 6, 3, 4
            stg = [P.sb(st, "nst%d" % i, [128, 512], F32) for i in range(NS_)]
            sgd = [P.ds("nst", st) for _ in range(NS_)]
            sgf = [None] * NS_
            sq = [P.sb(st, "nsq%d" % i, [128, 512], BF16) for i in range(NQ_)]
            sqf = [None] * NQ_
            rstd = P.sb(st, "rstd", [128, c.RW], F32)
            tmp = [P.sb(st, "ntmp%d" % i, [128, 512], F32) for i in range(NT_)]
            tmpf = [None] * NT_
            od = [P.ds("no", st) for _ in range(NT_)]
            u = 0
            for kc in range(KC):
                for gi, (s0, n, kind) in enumerate(groups):
                    sl = u % NS_
                    td = P.dma(SP, stg[sl][:, :n], hT[kc, :, s0:s0 + n], sgd[sl], waits=[sgf[sl]])
                    ACT.wait(td, sqf[u % NQ_])
                    ta = ACT.sig(nc.scalar.activation(out=sq[u % NQ_][:, :n], in_=stg[sl][:, :n], func=AF.Square))
                    sgf[sl] = ta
                    PE.wait(ta)
                    mm = nc.tensor.matmul(banks[gi][:, :n], ones[:, :], sq[u % NQ_][:, :n], start=(kc == 0), stop=(kc == KC - 1))
                    sqf[u % NQ_] = PE.sig(mm)
                    u += 1
            for gi, (s0, n, kind) in enumerate(groups):
                ACT.wait((PE, PE.n))
                ta = ACT.sig(nc.scalar.activation(out=rstd[:, s0 - t0:s0 - t0 + n], in_=banks[gi][:, :n], func=AF.Sqrt,
                                                  bias=epsb[:, 0:1], scale=1.0 / D))
                DVE.wait(ta)
                tr = DVE.sig(nc.vector.reciprocal(out=rstd[:, s0 - t0:s0 - t0 + n], in_=rstd[:, s0 - t0:s0 - t0 + n]))
            DVE.wait(tr)
            for kc in range(KC):
                for gi, (s0, n, kind) in enumerate(groups):
                    sl = u % NS_
                    td = P.dma(SP, stg[sl][:, :n], hT[kc, :, s0:s0 + n], sgd[sl], waits=[sgf[sl]])
                    DVE.wait(td, tmpf[u % NT_])
                    tv = DVE.sig(nc.vector.tensor_tensor(out=tmp[u % NT_][:, :n], in0=stg[sl][:, :n],
                                                         in1=rstd[:, s0 - t0:s0 - t0 + n], op=ALU.mult))
                    sgf[sl] = tv
                    ACT.wait(tv)
                    if final:
                        ta = ACT.sig(nc.scalar.activation(out=tmp[u % NT_][:, :n], in_=tmp[u % NT_][:, :n], func=AF.Identity,
                                                          scale=gfb[:, kc:kc + 1]))
                        P.dma(SP, outT[kc, :, s0:s0 + n], tmp[u % NT_][:, :n], od[u % NT_], waits=[ta])
                        tmpf[u % NT_] = (od[u % NT_], od[u % NT_].n)
                    else:
                        ta = ACT.sig(nc.scalar.activation(out=res[:, kc, s0 - t0:s0 - t0 + n], in_=tmp[u % NT_][:, :n],
                                                          func=AF.Identity, scale=Am[:, layer, sub, kind, kc:kc + 1],
                                                          bias=modf[:, layer, (0 if sub == 0 else 3) * KC + kc, kind:kind + 1]))
                        tmpf[u % NT_] = ta
                    u += 1
            P.barrier()

    def load_res(res, src, sbi, kq=0, groups=None):
        groups = c.SB[sbi] if groups is None else groups
        t0, t1 = groups[0][0], groups[-1][0] + groups[-1][1]
        tk = None
        for kc in range(KC):
            tk = P.dma(SP, res[:, kc, 0:t1 - t0], src[kq * KC + kc, :, t0:t1], d_misc)
        PE.wait((d_misc, d_misc.n))

    def gemm(res, Wap, ncols, sbi, chunk_fn, groups=None, nb=4):
        groups = c.SB[sbi] if groups is None else groups
        t0 = c.SB[sbi][0][0]
        pend = []
        nt = ncols // 256
        tiles = []
        for ct in range(nt):
            while len(tiles) < min(nt, ct + NW):
                tiles.append(w_tile(Wap, len(tiles) * 256))
            slot, tw = tiles[ct]
            PE.wait(tw)
            kinds = [chunk_fn(ct * 2 + mc) for mc in range(2)]
            last = None
            if kinds[0] is not None and kinds[0][0] == 'v' and kinds[1] is not None and kinds[1][0] == 'v':
                vlist = [(0, 256, kinds[0])]
            else:
                vlist = [(mc * 128, 128, kinds[mc]) for mc in range(2) if kinds[mc] is not None and kinds[mc][0] == 'v']
            for (wc0, wn, kd) in vlist:
                tb0, tb1 = groups[0][0], groups[-1][0] + groups[-1][1]
                for tt in range(tb0, tb1, 128):
                    b = next_bank(0, nb)
                    PE.wait(P.bank_free[b])
                    for kc in range(KC):
                        mm = nc.tensor.matmul(banks[b][:, :wn], res[:, kc, tt - t0:tt - t0 + 128],
                                              wring[slot][:, kc, wc0:wc0 + wn], start=(kc == 0), stop=(kc == KC - 1))
                    last = PE.sig(mm)
                    pend.append(kd[2](kd[1], wn, tt, b, last))
                    run_pending(pend)
            for mc in range(2):
                kd = kinds[mc]
                if kd is None or kd[0] != 'r':
                    continue
                for gi, grp in enumerate(groups):
                    s0, n, kind = grp
                    if len(kd) > 2 and kd[2] and kind == 1:
                        continue
                    b = next_bank(0, nb)
                    PE.wait(P.bank_free[b])
                    for kc in range(KC):
                        mm = nc.tensor.matmul(banks[b][:, :n], wring[slot][:, kc, mc * 128:(mc + 1) * 128],
                                              res[:, kc, s0 - t0:s0 - t0 + n], start=(kc == 0), stop=(kc == KC - 1))
                    last = PE.sig(mm)
                    pend.append(kd[1](ct * 2 + mc, gi, grp, b, last))
                    run_pending(pend)
            wfree[slot] = last if last is not None else (PE, PE.n)
        run_pending(pend, final=True)
        P.barrier()

    class Ring:
        def __init__(s, st, name, shape, dt, n, dsem=False):
            s.t = [P.sb(st, "%s%d" % (name, i), shape, dt) for i in range(n)]
            s.free = [None] * n
            s.ds = [P.ds(name, st) for _ in range(n)] if dsem else None
            s.i = 0
            s.n = n

        def nxt(s):
            k = s.i % s.n
            s.i += 1
            return k

    def make_qk_epi(st):
        sq = Ring(st, "esq", [128, 512], BF16, 2)
        rs = Ring(st, "ers", [128, 512], F32, 2)
        qn = Ring(st, "eqn", [128, 512], BF16, 3)
        t1 = Ring(st, "et1", [128, 512], F32, 2)
        t2 = Ring(st, "et2", [128, 512], F32, 2)
        ob = Ring(st, "eob", [128, 512], BF16, 2, dsem=True)

        def epi(m, gi, grp, b, tok, normed, gcol, dstap):
            s0, n, kind = grp
            kq = qn.nxt()
            if normed:
                k1 = sq.nxt()
                ACT.wait(tok, sq.free[k1])
                ta = ACT.sig(nc.scalar.activation(out=sq.t[k1][:, :n], in_=banks[b][:, :n], func=AF.Square))
                yield
                bs = next_bank(4, 2)
                PE.wait(ta, P.bank_free[bs])
                tp = PE.sig(nc.tensor.matmul(banks[bs][:, :n], ones[:, :], sq.t[k1][:, :n], start=True, stop=True))
                sq.free[k1] = tp
                k2 = rs.nxt()
                ACT.wait(tp, rs.free[k2])
                ta2 = ACT.sig(nc.scalar.activation(out=rs.t[k2][:, :n], in_=banks[bs][:, :n], func=AF.Sqrt,
                                                   bias=epsb[:, 0:1], scale=1.0 / 128))
                P.bank_free[bs] = ta2
                DVE.wait(ta2)
                tr = DVE.sig(nc.vector.reciprocal(out=rs.t[k2][:, :n], in_=rs.t[k2][:, :n]))
                DVE.wait(tr, qn.free[kq])
                tq = DVE.sig(nc.vector.scalar_tensor_tensor(out=qn.t[kq][:, :n], in0=banks[b][:, :n], scalar=Gq[:, gcol:gcol + 1],
                                                            in1=rs.t[k2][:, :n], op0=ALU.mult, op1=ALU.mult))
                rs.free[k2] = tq
            else:
                ACT.wait(tok, qn.free[kq])
                tq = ACT.sig(nc.scalar.activation(out=qn.t[kq][:, :n], in_=banks[b][:, :n], func=AF.Copy))
            P.bank_free[b] = tq
            yield
            bw = next_bank(6, 2)
            PE.wait(tq, P.bank_free[bw])
            tp2 = PE.sig(nc.tensor.matmul(banks[bw][:, :n], perm[:, :], qn.t[kq][:, :n], start=True, stop=True))
            ka, kb, ko = t1.nxt(), t2.nxt(), ob.nxt()
            DVE.wait(tq, t1.free[ka])
            ta_ = DVE.sig(nc.vector.tensor_tensor(out=t1.t[ka][:, :n], in0=qn.t[kq][:, :n], in1=cosb[:, s0:s0 + n], op=ALU.mult))
            DVE.wait(tp2, t2.free[kb])
            tb_ = DVE.sig(nc.vector.tensor_tensor(out=t2.t[kb][:, :n], in0=banks[bw][:, :n], in1=sinb[:, s0:s0 + n], op=ALU.mult))
            P.bank_free[bw] = tb_
            qn.free[kq] = [tp2, ta_]
            DVE.wait(tb_, (ob.ds[ko], ob.ds[ko].n))
            to = DVE.sig(nc.vector.tensor_tensor(out=ob.t[ko][:, :n], in0=t1.t[ka][:, :n], in1=t2.t[kb][:, :n], op=ALU.add))
            t1.free[ka] = to
            t2.free[kb] = to
            P.dma(SP, dstap, ob.t[ko][:, :n], ob.ds[ko], waits=[to])
        return epi

    def make_copy_epi(st, dst_fn):
        ob = Ring(st, "cob", [128, 512], BF16, 3, dsem=True)

        def epi(m, gi, grp, b, tok):
            s0, n, kind = grp
            ko = ob.nxt()
            eng = ACT if (ob.i % 2 == 0) else DVE
            eng.wait(tok, (ob.ds[ko], ob.ds[ko].n))
            if eng is ACT:
                tq = ACT.sig(nc.scalar.activation(out=ob.t[ko][:, :n], in_=banks[b][:, :n], func=AF.Copy))
            else:
                tq = DVE.sig(nc.vector.tensor_copy(out=ob.t[ko][:, :n], in_=banks[b][:, :n]))
            P.bank_free[b] = tq
            P.dma(SP, dst_fn(m, s0, n, kind), ob.t[ko][:, :n], ob.ds[ko], waits=[tq])
            return
            yield
        return epi

    def make_v_epi(st, dst_fn):
        ob = Ring(st, "vob", [128, 256], BF16, 3, dsem=True)

        def epi(vcol0, wn, tt, b, tok):
            ko = ob.nxt()
            eng = ACT if (ob.i % 2 == 0) else DVE
            eng.wait(tok, (ob.ds[ko], ob.ds[ko].n))
            if eng is ACT:
                tq = ACT.sig(nc.scalar.activation(out=ob.t[ko][:, :wn], in_=banks[b][:, :wn], func=AF.Copy))
            else:
                tq = DVE.sig(nc.vector.tensor_copy(out=ob.t[ko][:, :wn], in_=banks[b][:, :wn]))
            P.bank_free[b] = tq
            for (dst, a_, b_) in dst_fn(tt, vcol0, wn):
                P.dma(SP, dst, ob.t[ko][:, a_:b_], ob.ds[ko], waits=[tq])
            return
            yield
        return epi

    def make_resid_epi(st, layer, gate_part, kq_off=0):
        hin = Ring(st, "hin", [128, 512], F32, 3, dsem=True)
        hout = Ring(st, "hout", [128, 512], F32, 3, dsem=True)

        def epi(m, gi, grp, b, tok):
            s0, n, kind = grp
            ki, ko = hin.nxt(), hout.nxt()
            td = P.dma(SP, hin.t[ki][:, :n], hT[m, :, s0:s0 + n], hin.ds[ki], waits=[hin.free[ki]])
            DVE.wait(tok, td, (hout.ds[ko], hout.ds[ko].n))
            to = DVE.sig(nc.vector.scalar_tensor_tensor(out=hout.t[ko][:, :n], in0=banks[b][:, :n],
                                                        scalar=modf[:, layer, gate_part * KC + m, kind:kind + 1],
                                                        in1=hin.t[ki][:, :n], op0=ALU.mult, op1=ALU.add))
            hin.free[ki] = to
            P.bank_free[b] = to
            P.dma(SP, hT[m, :, s0:s0 + n], hout.t[ko][:, :n], hout.ds[ko], waits=[to])
            return
            yield
        return epi

    def make_relu2_epi(st):
        rb = Ring(st, "rb", [128, 512], F32, 3)
        ob = Ring(st, "rob", [128, 512], BF16, 3, dsem=True)

        def epi(m, gi, grp, b, tok):
            s0, n, kind = grp
            kr, ko = rb.nxt(), ob.nxt()
            ACT.wait(tok, rb.free[kr])
            ta = ACT.sig(nc.scalar.activation(out=rb.t[kr][:, :n], in_=banks[b][:, :n], func=AF.Relu))
            P.bank_free[b] = ta
            DVE.wait(ta, (ob.ds[ko], ob.ds[ko].n))
            to = DVE.sig(nc.vector.tensor_tensor(out=ob.t[ko][:, :n], in0=rb.t[kr][:, :n], in1=rb.t[kr][:, :n], op=ALU.mult))
            rb.free[kr] = to
            P.dma(SP, actT[m, :, s0:s0 + n], ob.t[ko][:, :n], ob.ds[ko], waits=[to])
            return
            yield
        return epi

    SCALE = 1.0 / math.sqrt(HD)

    class AttnCtx:
        def __init__(s, st, with_tab=False):
            s.pb = Ring(st, "apb", [128, 512], BF16, 5)
            s.si = 0
            s.ui = 0
            s.tabtok = None
            if with_tab:
                s.sbuf = Ring(st, "asb", [128, 512], F32, 3)

    def attn_unit(ac, qT, nq, chunks, nv, hooks=None):
        if nv == 1:
            Sb, LA = [0, 1, 2, 3], 3
            base = 4 + 2 * (ac.ui % 2)
            ob, sb_ = [base], base + 1
        else:
            Sb, LA = [0, 1], 1
            base = 2 + 3 * (ac.ui % 2)
            ob, sb_ = [base, base + 1], base + 2
        ac.ui += 1
        nch = len(chunks)

        def qk(i):
            b = Sb[ac.si % len(Sb)]
            ac.si += 1
            PE.wait(P.bank_free[b])
            mm = nc.tensor.matmul(banks[b][:, :nq], chunks[i][0], qT, start=True, stop=True)
            return b, PE.sig(mm)
        for bb in ob + [sb_]:
            PE.wait(P.bank_free[bb])
        qks = {}
        for j in range(min(LA, nch)):
            qks[j] = qk(j)
        last = None
        for i in range(nch):
            if i + LA < nch:
                qks[i + LA] = qk(i + LA)
            b, tqk = qks.pop(i)
            kp = ac.pb.nxt()
            tabsrc = chunks[i][2]
            if tabsrc is not None:
                ks = ac.sbuf.nxt()
                DVE.wait(tqk, ac.tabtok, ac.sbuf.free[ks])
                tv = DVE.sig(nc.vector.scalar_tensor_tensor(out=ac.sbuf.t[ks][:, :nq], in0=banks[b][:, :nq], scalar=SCALE,
                                                            in1=tabsrc, op0=ALU.mult, op1=ALU.add))
                P.bank_free[b] = tv
                ACT.wait(tv, ac.pb.free[kp])
                te = ACT.sig(nc.scalar.activation(out=ac.pb.t[kp][:, :nq], in_=ac.sbuf.t[ks][:, :nq], func=AF.Exp))
                ac.sbuf.free[ks] = te
            else:
                ACT.wait(tqk, ac.pb.free[kp])
                te = ACT.sig(nc.scalar.activation(out=ac.pb.t[kp][:, :nq], in_=banks[b][:, :nq], func=AF.Exp, scale=SCALE))
                P.bank_free[b] = te
            PE.wait(te)
            for vi in range(nv):
                nc.tensor.matmul(banks[ob[vi]][:, :nq], chunks[i][1][vi], ac.pb.t[kp][:, :nq], start=(i == 0), stop=(i == nch - 1))
            mm = nc.tensor.matmul(banks[sb_][:, :nq], ones[:, :], ac.pb.t[kp][:, :nq], start=(i == 0), stop=(i == nch - 1))
            last = PE.sig(mm)
            ac.pb.free[kp] = last
            if hooks and i in hooks:
                hooks[i]()
        return ob, sb_, last

    def make_attn_epi_A(st):
        rec = Ring(st, "arec", [128, 512], F32, 2)
        ob = Ring(st, "aob", [128, 512], BF16, 3, dsem=True)

        def parts(obanks, sbank, tok, nq, dst):
            stt = {}

            def p1():
                stt["kr"], stt["ko"] = rec.nxt(), ob.nxt()
                kr = stt["kr"]
                DVE.wait(tok, rec.free[kr])
                stt["tr"] = DVE.sig(nc.vector.reciprocal(out=rec.t[kr][:, :nq], in_=banks[sbank][:, :nq]))
                P.bank_free[sbank] = stt["tr"]

            def p2():
                kr, ko, tr = stt["kr"], stt["ko"], stt["tr"]
                DVE.wait(tr, (ob.ds[ko], ob.ds[ko].n))
                to = DVE.sig(nc.vector.tensor_tensor(out=ob.t[ko][:, :nq], in0=banks[obanks[0]][:, :nq], in1=rec.t[kr][:, :nq], op=ALU.mult))
                rec.free[kr] = to
                P.bank_free[obanks[0]] = to
                P.dma(SP, dst, ob.t[ko][:, :nq], ob.ds[ko], waits=[to])
            return p1, p2

        def epi(obanks, sbank, tok, nq, dst):
            p1, p2 = parts(obanks, sbank, tok, nq, dst)
            p1()
            p2()
        epi.parts = parts
        return epi

    def finish():
        P.barrier()
        P.es.close()
        return P
    if STOP_AFTER == 0:
        return finish()
    n_aq, n_ak, n_av, n_bq, n_bk = c.A_H, c.A_KV, c.A_KV, 2 * c.B_H, 2 * c.B_H
    o_ak = n_aq
    o_av = o_ak + n_ak
    o_bq = o_av + n_av
    o_bk = o_bq + n_bq
    o_bv = o_bk + n_bk
    def k0_dst(mk, grp):
        s0, n, kind = grp
        if kind == 0:
            if mk < c.A_KV:
                return K0m[mk][:, s0:s0 + n]
            hb_, m_ = (mk - c.A_KV) // 2, (mk - c.A_KV) % 2
            return K0m[c.A_KV + hb_][m_ * 128:(m_ + 1) * 128, s0:s0 + n]
        return K0c[mk, :, s0 - T:s0 - T + n]

    def v0_dst(tt, vcol0, wn):
        if tt >= T:
            return [(V0c[tt - T:tt - T + 128, vcol0:vcol0 + wn], 0, wn)]
        outl = []
        for a_ in range(0, wn, 128):
            vc = vcol0 + a_
            if vc < c.A_KV * 128:
                j, off = vc // 128, 0
            else:
                j, off = c.A_KV + (vc - c.A_KV * 128) // 256, (vc - c.A_KV * 128) % 256
            outl.append((V0h[j][tt:tt + 128, off:off + 128], a_, a_ + 128))
        return outl

    def lat_groups(sbi):
        return [g for g in c.SB[sbi] if g[2] == 0]

    def mlp(layer, sbi, groups):
        with ExitStack() as st:
            res = P.sb(st, "res", [128, KC, c.RW], BF16)
            alloc_w(st)
            norm_phase(layer, 1, sbi, res, groups=groups)
            with ExitStack() as st2:
                e = make_relu2_epi(st2)
                gemm(res, w1[layer], H, sbi, lambda m: ('r', e), groups=groups)
            for kq in range(4):
                load_res(res, actT, sbi, kq, groups=groups)
                with ExitStack() as st2:
                    e = make_resid_epi(st2, layer, 5)
                    gemm(res, w2[layer, kq * D:(kq + 1) * D, :], D, sbi, lambda m: ('r', e), groups=groups)

    def wout(layer, W, sbi, groups):
        with ExitStack() as st:
            res = P.sb(st, "res", [128, KC, c.RW], BF16)
            alloc_w(st)
            load_res(res, attT, sbi, 0, groups=groups)
            e = make_resid_epi(st, layer, 2)
            gemm(res, W.ap(), D, sbi, lambda m: ('r', e), groups=groups)

    for sbi in range(2):
        with ExitStack() as st:
            res = P.sb(st, "res", [128, KC, c.RW], BF16)
            alloc_w(st)
            norm_phase(0, 0, sbi, res)
            qk = make_qk_epi(st)
            vepi = make_v_epi(st, v0_dst)

            def chunk_fn(m):
                if m < o_ak:
                    return ('r', lambda m_, gi, grp, b, tok, mq=m: qk(m_, gi, grp, b, tok, True, 0, Q0[mq, :, grp[0]:grp[0] + grp[1]]))
                if m < o_av:
                    return ('r', lambda m_, gi, grp, b, tok, mk=m - o_ak: qk(m_, gi, grp, b, tok, True, 1, k0_dst(mk, grp)))
                if m < o_bq:
                    return ('v', (m - o_av) * 128, vepi)
                if m < o_bk:
                    return ('r', lambda m_, gi, grp, b, tok, mq=c.A_H + m - o_bq: qk(m_, gi, grp, b, tok, False, 0, Q0[mq, :, grp[0]:grp[0] + grp[1]]))
                if m < o_bv:
                    return ('r', lambda m_, gi, grp, b, tok, mk=c.A_KV + m - o_bk: qk(m_, gi, grp, b, tok, False, 0, k0_dst(mk, grp)))
                return ('v', c.A_KV * 128 + (m - o_bv) * 128, vepi)
            gemm(res, w_in0.ap(), 9 * D // 4, sbi, chunk_fn)

    def allgather(src, dst, ds=None):
        ds = d_cc if ds is None else ds
        ins = nc.gpsimd.collective_compute("AllGather", ALU.bypass, replica_groups=G4, ins=[src.ap().opt()], outs=[dst.ap().opt()])
        ds.n += 1
        ins.then_inc(ds.sem, 1)
        return (ds, ds.n)

    if STOP_AFTER == 1:
        return finish()
    P.barrier(full=True)
    jobcc = []
    for i in range(NKG):
        allgather(K0m[i], K0mg[i])
        allgather(V0h[i], V0hg[i])
        jobcc.append(None)
    P.barrier(full=True)
    if STOP_AFTER == 2:
        return finish()

    NKC = c.NKC
    with ExitStack() as st:
        ac = AttnCtx(st)
        epiA = make_attn_epi_A(st)
        KT = [P.sb(st, "KT%d" % i, [128, 2, NKC * 128], BF16) for i in range(2)]
        VT = [P.sb(st, "VT%d" % i, [128, NKC, 256], BF16) for i in range(2)]
        kvds = [P.ds("kv", st) for _ in range(2)]
        kvfree = [None, None]
        QT = Ring(st, "QT", [128, TT], BF16, 4, dsem=True)
        omap = P.sb(st, "omap", [128, 2, 2, 512], F32)
        dbuf = P.sb(st, "dbuf", [128, 2, 512], F32)
        sqb = P.sb(st, "sqb", [128, 2, 512], BF16)
        recb = P.sb(st, "recb", [128, 512], F32)
        rsB = P.sb(st, "rsB", [128, 512], F32)
        obB = Ring(st, "obB", [128, 512], BF16, 3, dsem=True)
        bst = {"dfree": None, "sqfree": None, "recfree": None, "omapfree": None, "rsfree": None}
        qblocks = [(s, 512, list(range(NKC))) for s in range(0, T, 512)] + [(T, CTX, [NKC - 2, NKC - 1])]
        TC = T // 128

        def load_kv(slot, kmaps, vcol0, dv, vh):
            w = [kvfree[slot], jobcc[vh]]
            nm_ = len(kmaps)
            for j, mk in enumerate(kmaps):
                P.dma(SP, KT[slot][:, j, 0:4 * T].rearrange("p (r t) -> p r t", r=4),
                      K0mg[vh].ap().rearrange("(r m p) t -> p m r t", r=4, m=nm_)[:, j], kvds[slot], waits=w)
                P.dma(SP, KT[slot][:, j, 4 * T:4 * T + CTX], K0c[mk], kvds[slot])
            P.dma(SP, VT[slot][:, 0:4 * TC, 0:dv], V0hg[vh].ap().rearrange("(c p) w -> p c w", p=128), kvds[slot])
            P.dma(SP, VT[slot][:, 4 * TC:4 * TC + 2, 0:dv], V0c[:, vcol0:vcol0 + dv].rearrange("(c p) w -> p c w", p=128), kvds[slot])
            return (kvds[slot], kvds[slot].n)

        def load_q(mq):
            k = QT.nxt()
            tq = P.dma(SP, QT.t[k][:, :], Q0[mq], QT.ds[k], waits=[QT.free[k]])
            return k, tq

        jobs = [('A', kv) for kv in range(c.A_KV)] + [('B', hb) for hb in range(c.B_H)]

        def job_kv(ji):
            kind, idx = jobs[ji]
            if kind == 'A':
                return load_kv(ji % 2, [idx], idx * 128, 128, idx)
            return load_kv(ji % 2, [c.A_KV + 2 * idx, c.A_KV + 2 * idx + 1], c.A_KV * 128 + idx * 256, 256, c.A_KV + idx)

        pendB = []
        tkv_next = job_kv(0)
        for ji, (kind, idx) in enumerate(jobs):
            slot = ji % 2
            tkv = tkv_next
            if ji + 1 < len(jobs):
                tkv_next = job_kv(ji + 1)
            PE.wait(tkv)
            if kind == 'A':
                for hq in range(4 * idx, 4 * idx + 4):
                    kq_, tq = load_q(hq)
                    PE.wait(tq)
                    for (s0, nq, cl) in qblocks:
                        chunks = [(KT[slot][:, 0, ci * 128:(ci + 1) * 128], [VT[slot][:, ci, 0:128]], None) for ci in cl]
                        obk, sbk, tok = attn_unit(ac, QT.t[kq_][:, s0:s0 + nq], nq, chunks, 1)
                        epiA(obk, sbk, tok, nq, attT[hq, :, s0:s0 + nq])
                        run_pending(pendB)
                    QT.free[kq_] = (PE, PE.n)
            else:
                hb = idx
                qs = [load_q(c.A_H + 2 * hb + m) for m in range(2)]
                for (s0, nq, cl) in qblocks:
                    for m in range(2):
                        PE.wait(qs[m][1])
                        chunks = [(KT[slot][:, m, ci * 128:(ci + 1) * 128], [VT[slot][:, ci, 0:128], VT[slot][:, ci, 128:256]], None) for ci in cl]
                        obk, sbk, tok = attn_unit(ac, QT.t[qs[m][0]][:, s0:s0 + nq], nq, chunks, 2)
                        DVE.wait(tok, bst["recfree"])
                        tr = DVE.sig(nc.vector.reciprocal(out=recb[:, :nq], in_=banks[sbk][:, :nq]))
                        DVE.wait(tr, bst["omapfree"])
                        for cc_ in range(2):
                            to = DVE.sig(nc.vector.tensor_tensor(out=omap[:, m, cc_, :nq], in0=banks[obk[cc_]][:, :nq], in1=recb[:, :nq], op=ALU.mult))
                            P.bank_free[obk[cc_]] = to
                        bst["recfree"] = to
                        run_pending(pendB)
                        if m == 0:
                            P.bank_free[sbk] = tr
                            continue
                        DVE.wait(to, bst["dfree"])
                        for cc_ in range(2):
                            td = DVE.sig(nc.vector.scalar_tensor_tensor(out=dbuf[:, cc_, :nq], in0=omap[:, 1, cc_, :nq], scalar=lamb[:, 0:1],
                                                                        in1=omap[:, 0, cc_, :nq], op0=ALU.mult, op1=ALU.add))
                        bst["omapfree"] = td
                        ACT.wait(td, bst["sqfree"])
                        ts = ACT.sig(nc.scalar.activation(out=sqb[:, :, :nq], in_=dbuf[:, :, :nq], func=AF.Square))
                        bst["dfree"] = ts

                        def stage1(sbk=sbk, nq=nq, s0=s0, hb=hb, ts=ts, tr=tr):
                            yield
                            PE.wait(ts, tr)
                            nc.tensor.matmul(banks[sbk][:, :nq], ones[:, :], sqb[:, 0, :nq], start=True, stop=False)
                            tp = PE.sig(nc.tensor.matmul(banks[sbk][:, :nq], ones[:, :], sqb[:, 1, :nq], start=False, stop=True))
                            bst["sqfree"] = tp
                            ACT.wait(tp, bst["rsfree"])
                            ta = ACT.sig(nc.scalar.activation(out=rsB[:, :nq], in_=banks[sbk][:, :nq], func=AF.Sqrt, bias=epsb[:, 0:1], scale=1.0 / 256))
                            P.bank_free[sbk] = ta
                            DVE.wait(ta)
                            tr2 = DVE.sig(nc.vector.reciprocal(out=rsB[:, :nq], in_=rsB[:, :nq]))
                            for cc_ in range(2):
                                ko = obB.nxt()
                                DVE.wait(tr2, (obB.ds[ko], obB.ds[ko].n))
                                to2 = DVE.sig(nc.vector.scalar_tensor_tensor(out=obB.t[ko][:, :nq], in0=dbuf[:, cc_, :nq], scalar=Gq[:, 2 + cc_:3 + cc_],
                                                                             in1=rsB[:, :nq], op0=ALU.mult, op1=ALU.mult))
                                P.dma(SP, attT[c.A_H + 2 * hb + cc_, :, s0:s0 + nq], obB.t[ko][:, :nq], obB.ds[ko], waits=[to2])
                            bst["rsfree"] = to2
                            bst["dfree"] = [ts, to2]
                        pendB.append(stage1())
                        run_pending(pendB)
                for m in range(2):
                    QT.free[qs[m][0]] = (PE, PE.n)
            kvfree[slot] = (PE, PE.n)
        run_pending(pendB, final=True)
        P.barrier()

    if STOP_AFTER == 3:
        return finish()
    for sbi in range(2):
        wout(0, w_out0, sbi, c.SB[sbi])
    for sbi in range(2):
        mlp(0, sbi, c.SB[sbi])

    if STOP_AFTER == 4:
        return finish()
    OWN0 = 256
    BEL0 = 256 + T
    CTX0 = 512 + T
    for sbi in range(2):
        with ExitStack() as st:
            res = P.sb(st, "res", [128, KC, c.RW], BF16)
            alloc_w(st)
            norm_phase(1, 0, sbi, res)

            def q1_dst(m, s0, n, kind):
                return Q1[m, :, s0:s0 + n]

            def k1_dst(m, s0, n, kind):
                e0 = OWN0 + s0 if kind == 0 else CTX0 + s0 - T
                return K1[m - KC, :, e0:e0 + n]

            def v1_dst(tt, vcol0, wn):
                e0 = OWN0 + tt if tt < T else CTX0 + tt - T
                return [(V1[e0:e0 + 128, vcol0:vcol0 + wn], 0, wn)]
            eq = make_copy_epi(st, q1_dst)
            ek = make_copy_epi(st, k1_dst)
            ev = make_v_epi(st, v1_dst)

            def chunk_fn1(m):
                if m < KC:
                    return ('r', eq, True)
                if m < 2 * KC:
                    return ('r', ek)
                return ('v', (m - 2 * KC) * 128, ev)
            gemm(res, w_in1.ap(), 3 * D, sbi, chunk_fn1)

    if STOP_AFTER == 5:
        return finish()
    K1f = K1.ap().rearrange("h p n -> (h p) n")
    for p_ in range(NKP):
        rs_ = slice(p_ * HP * 128, (p_ + 1) * HP * 128)
        P.dma(SP, Kb[p_][:, 0:256], K1f[rs_, OWN0:OWN0 + 256], d_misc)
        P.dma(SP, Kb[p_][:, 256:512], K1f[rs_, T:T + 256], d_misc)
    for i_ in range(4):
        r0_ = (OWN0 if i_ < 2 else T) + (i_ % 2) * 128
        for j_ in range(NVP):
            P.dma(SP, Vb[i_][j_][:, :], V1[r0_:r0_ + 128, j_ * VPW:(j_ + 1) * VPW], d_misc)
    POOL.wait((d_misc, d_misc.n))
    for p_ in range(NKP):
        allgather(Kb[p_], Kbg[p_])
    for i_ in range(4):
        for j_ in range(NVP):
            allgather(Vb[i_][j_], Vbg[i_][j_])
    P.barrier()
    with ExitStack() as st:
        X = P.sb(st, "hx", [128, 4, KC * 256], BF16)
        Y = [P.sb(st, "hy%d" % i, [128, KC * 256], BF16) for i in range(2)]
        yds = [P.ds("hy", st) for _ in range(2)]
        xfree = None
        it = 0
        for (isv, side) in ((0, 0), (0, 1), (1, 0), (1, 1)):
            c0_ = 256 if side == 0 else 0
            tl_ = None
            for r in range(4):
                if isv == 0:
                    for p_ in range(NKP):
                        src = Kbg[p_][r * HP * 128:(r + 1) * HP * 128, c0_:c0_ + 256].rearrange("(h p) n -> p h n", p=128)
                        dstx = X[:, r, p_ * HP * 256:(p_ + 1) * HP * 256].rearrange("p (h n) -> p h n", n=256)
                        tl_ = P.dma(SP, dstx, src, d_misc, waits=[xfree])
                else:
                    for cl_ in range(2):
                        for j_ in range(NVP):
                            src = Vbg[(c0_ // 128) + cl_][j_][r * 128:(r + 1) * 128, :]
                            tl_ = P.dma(SP, X[:, r, cl_ * D + j_ * VPW:cl_ * D + (j_ + 1) * VPW], src, d_misc, waits=[xfree])
            W_ = KC * 256 if isv == 0 else 2 * D
            y = Y[it % 2]
            DVE.wait(tl_, (d_misc, d_misc.n), (yds[it % 2], yds[it % 2].n))
            ty = DVE.sig(nc.vector.tensor_scalar(out=y[:, :W_], in0=X[:, 0, :W_], scalar1=selb[:, 4 * side:4 * side + 1], scalar2=None, op0=ALU.mult))
            for r in range(1, 4):
                DVE.wait(ty)
                ty = DVE.sig(nc.vector.scalar_tensor_tensor(out=y[:, :W_], in0=X[:, r, :W_], scalar=selb[:, 4 * side + r:4 * side + r + 1],
                                                            in1=y[:, :W_], op0=ALU.mult, op1=ALU.add))
            xfree = ty
            e0 = 0 if side == 0 else BEL0
            if isv == 0:
                P.dma(SP, K1[:, :, e0:e0 + 256].rearrange("h p n -> p h n"), y[:, :W_].rearrange("p (h n) -> p h n", n=256), yds[it % 2], waits=[ty])
            else:
                P.dma(SP, V1[e0:e0 + 256, :].rearrange("(c p) w -> p c w", p=128), y[:, :W_].rearrange("p (c w) -> p c w", w=D), yds[it % 2], waits=[ty])
            it += 1
        P.barrier()

    if STOP_AFTER == 6:
        return finish()
    HG = 2
    NT1 = c.EXT // 128
    with ExitStack() as st:
        ac = AttnCtx(st, with_tab=True)
        epiA = make_attn_epi_A(st)
        KT1 = [P.sb(st, "K1T%d" % i, [128, HG, c.EXT], BF16) for i in range(2)]
        VT1 = [P.sb(st, "V1T%d" % i, [128, NT1, HG * 128], BF16) for i in range(2)]
        kvds = [P.ds("kv1", st) for _ in range(2)]
        kvfree = [None, None]
        QT = Ring(st, "Q1T", [128, T], BF16, 2, dsem=True)
        TB = Ring(st, "TB", [128, 3, 8, 512], F32, 2, dsem=True)

        def load_tb(h):
            k = TB.nxt()
            for ts__ in range(3):
                P.dma(SP, TB.t[k][:, ts__, :, :], tabs[ts__, h].rearrange("k p n -> p k n"), TB.ds[k], waits=[TB.free[k]])
            return k, (TB.ds[k], TB.ds[k].n)
        tb_next = load_tb(0)

        def load_q1(h):
            k = QT.nxt()
            return k, P.dma(SP, QT.t[k][:, :], Q1[h], QT.ds[k], waits=[QT.free[k]])
        pend_epi = [None]

        def load_kv1(hg):
            slot = hg % 2
            P.dma(SP, KT1[slot][:, :, :], K1[hg * HG:(hg + 1) * HG].rearrange("h p n -> p h n"), kvds[slot], waits=[kvfree[slot]])
            P.dma(SP, VT1[slot][:, :, :], V1[:, hg * HG * 128:(hg + 1) * HG * 128].rearrange("(c p) w -> p c w", p=128), kvds[slot])
            return (kvds[slot], kvds[slot].n)
        nhg = KC // HG
        tnext = load_kv1(0)
        for hg in range(nhg):
            slot = hg % 2
            tkv = tnext
            if hg + 1 < nhg:
                tnext = load_kv1(hg + 1)
            PE.wait(tkv)
            for hh in range(HG):
                h = hg * HG + hh
                ktb, ac.tabtok = tb_next
                if h + 1 < KC:
                    tb_next = load_tb(h + 1)
                if h == 0:
                    q_next = load_q1(0)
                kq_, tq = q_next
                if h + 1 < KC:
                    q_next = load_q1(h + 1)
                PE.wait(tq)
                for g in range(c.G):
                    ts_ = 0 if g == 0 else (2 if g == c.G - 1 else 1)
                    chunks = []
                    for k in range(8):
                        e0 = g * 512 + k * 128
                        chunks.append((KT1[slot][:, hh, e0:e0 + 128], [VT1[slot][:, e0 // 128, hh * 128:(hh + 1) * 128]], TB.t[ktb][:, ts_, k, :]))
                    for j in range(2):
                        e0 = CTX0 + j * 128
                        chunks.append((KT1[slot][:, hh, e0:e0 + 128], [VT1[slot][:, e0 // 128, hh * 128:(hh + 1) * 128]], None))
                    hooks = None
                    if pend_epi[0] is not None:
                        hooks = {2: pend_epi[0][0], 6: pend_epi[0][1]}
                    obk, sbk, tok = attn_unit(ac, QT.t[kq_][:, g * 512:(g + 1) * 512], 512, chunks, 1, hooks=hooks)
                    pend_epi[0] = epiA.parts(obk, sbk, tok, 512, attT[h, :, g * 512:(g + 1) * 512])
                QT.free[kq_] = (PE, PE.n)
                TB.free[ktb] = (DVE, DVE.n)
            kvfree[slot] = (PE, PE.n)
        pend_epi[0][0]()
        pend_epi[0][1]()
        P.barrier()

    for sbi in range(2):
        lg = lat_groups(sbi)
        if lg:
            wout(1, w_out1, sbi, lg)
    for sbi in range(2):
        lg = lat_groups(sbi)
        if lg:
            mlp(1, sbi, lg)
    for sbi in range(2):
        lg = lat_groups(sbi)
        if lg:
            norm_phase(1, 0, sbi, None, final=True, groups=lg)
    P.barrier()
    P.es.close()
    return P


_CACHE = {}


def _rope_tables(cfg, q):
    T, TT = cfg.T, cfg.TT
    t = np.arange(T, dtype=np.int32) + q * T
    row = (t // GRID_W).astype(np.float32)
    col = (t % GRID_W).astype(np.float32)
    nf = HD // 4
    freqs = (np.float32(10000.0) ** (-np.arange(nf, dtype=np.float32) / np.float32(nf))).astype(np.float32)
    cosT = np.ones((128, TT), np.float32)
    sinT = np.zeros((128, TT), np.float32)
    for axis, pos in enumerate((row, col)):
        ang = (pos[None, :] * freqs[:, None]).astype(np.float32)
        cs, sn = np.cos(ang).astype(np.float32), np.sin(ang).astype(np.float32)
        for pair in range(2):
            d0 = axis * 64 + pair * 32
            cosT[d0:d0 + 32, :T] = cs
            sinT[d0:d0 + 32, :T] = -sn if pair == 0 else sn
    return cosT, sinT


def _na_tables(cfg, rel_bias):
    ROWS = cfg.ROWS
    out = []
    for qr0 in (0, 8, ROWS - 8):
        w = np.arange(16)
        krow = qr0 - 4 + w
        j = np.arange(8)
        qrow = qr0 + j
        r0q = np.clip(qrow - 4, 0, ROWS - 8)
        vrow = (krow[:, None] >= 0) & (krow[:, None] < ROWS) & (krow[:, None] >= r0q[None, :]) & (krow[:, None] < r0q[None, :] + 8)
        kcol = np.arange(GRID_W)
        qcol = np.arange(GRID_W)
        c0 = np.clip(qcol - 8, 0, GRID_W - 16)
        vcol = (kcol[:, None] >= c0[None, :]) & (kcol[:, None] < c0[None, :] + 16)
        dr = np.clip(krow[:, None] - qrow[None, :] + 7, 0, 14)
        dc = np.clip(kcol[:, None] - qcol[None, :] + 15, 0, 30)
        tab = rel_bias[:, dr[:, None, :, None], dc[None, :, None, :]]
        valid = vrow[:, None, :, None] & vcol[None, :, None, :]
        tab = np.where(valid[None], tab, np.float32(NEG)).astype(np.float32)
        out.append(tab.reshape(rel_bias.shape[0], 8, 128, 512))
    return out


def kernel(x, c, ctx, c_ctx, ada_w, ada_b, norm1_g, norm2_g, w_in_even, w_out_even,
           a_q_norm, a_k_norm, b_lambda_q1, b_lambda_k1, b_lambda_q2, b_lambda_k2, b_subln_g,
           w_in_odd, w_out_odd, na_rel_bias, mlp_w1, mlp_w2, final_g, _cfg=None):
    f = lambda a: np.ascontiguousarray(np.asarray(a, dtype=np.float32))
    x, c, ctx, c_ctx = f(x), f(c), f(ctx), f(c_ctx)
    B, S, D = x.shape
    cfg = _cfg or Cfg(D, S)
    KC, T, TT, MJ = cfg.KC, cfg.T, cfg.TT, cfg.MJ
    key = (D, S)
    if key not in _CACHE:
        _CACHE[key] = build(cfg)
    P = _CACHE[key]
    ada_w, ada_b = f(ada_w), f(ada_b)
    tabs3 = _na_tables(cfg, f(na_rel_bias)[0])
    perm = np.zeros((128, 128), np.float32)
    perm[np.arange(128) ^ 32, np.arange(128)] = 1.0
    smalls = np.stack([f(a_q_norm)[0], f(a_k_norm)[0], f(b_lambda_q1)[0], f(b_lambda_k1)[0], f(b_lambda_q2)[0], f(b_lambda_k2)[0],
                       f(b_subln_g)[0][:128], f(b_subln_g)[0][128:]], axis=1)
    shared = {
        "gn1": f(f(norm1_g).reshape(2, KC, 128).transpose(2, 0, 1)),
        "gn2": f(f(norm2_g).reshape(2, KC, 128).transpose(2, 0, 1)),
        "gfin": f(f(final_g).reshape(KC, 128).T),
        "w_in0": f(w_in_even)[0], "w_out0": f(w_out_even)[0], "w_in1": f(w_in_odd)[0], "w_out1": f(w_out_odd)[0],
        "w1": f(mlp_w1), "w2": f(mlp_w2), "smalls": f(smalls), "perm": perm,
    }
    in_maps = []
    for r in range(8):
        b, q = r // 4, r % 4
        xt = np.concatenate([x[b, q * T:(q + 1) * T], ctx[b]], axis=0)
        cosT, sinT = _rope_tables(cfg, q)
        sel = np.zeros((128, 8), np.float32)
        sel[:, (q - 1) % 4] = 1.0
        sel[:, 4 + (q + 1) % 4] = 1.0
        m = dict(shared)
        m.update({
            "xT": f(xt.T.reshape(KC, 128, TT)),
            "cvT": f(np.stack([c[b], c_ctx]).reshape(2, KC, 128).transpose(2, 1, 0)),
            "adaw": f(ada_w[:, :, q * MJ * 128:(q + 1) * MJ * 128]),
            "adab": f(ada_b[:, q * MJ * 128:(q + 1) * MJ * 128].reshape(2, MJ, 128).transpose(2, 0, 1)),
            "cosT": cosT, "sinT": sinT,
            "tabs": f(np.stack([tabs3[0] if q == 0 else tabs3[1], tabs3[1], tabs3[2] if q == 3 else tabs3[1]])),
            "sel": sel,
        })
        in_maps.append(m)
    res = run_bass_kernel_spmd(P.nc, in_maps, core_ids=list(range(8)))
    out = np.empty((B, S, D), np.float32)
    for r in range(8):
        b, q = r // 4, r % 4
        out[b, q * T:(q + 1) * T] = res.results[r]["outT"].reshape(D, T).T
    kernel.last = res
    return out
```
